# Optimizing a Trainium2 kernel written in Bass

```python
import math
import jax, jax.numpy as jnp
from jax import lax
import numpy as np

D_MODEL = 2048
BATCH = 4
SEQ = 4096
DEPTH = 2

N_MIXERS = 2
N_HGRN_LAYERS = (DEPTH + N_MIXERS - 1) // N_MIXERS
N_NSA_LAYERS = DEPTH // N_MIXERS
RMS_EPS = 1e-6

HGRN_HEAD_DIM = 128
HGRN_HEADS = D_MODEL // HGRN_HEAD_DIM
HGRN_WIDTH = HGRN_HEADS * HGRN_HEAD_DIM
HGRN_CHUNK = 64

NSA_HEAD_DIM = 128
NSA_HEADS = D_MODEL // NSA_HEAD_DIM
NSA_KV_GROUPS = 4
NSA_HPG = NSA_HEADS // NSA_KV_GROUPS
NSA_WIDTH = NSA_HEADS * NSA_HEAD_DIM
NSA_KV_WIDTH = NSA_KV_GROUPS * NSA_HEAD_DIM
N_BRANCH = 3
CMP_BLOCK = 32
CMP_STRIDE = 16
SEL_BLOCK = 64
N_SELECT = 16
WINDOW = 512
PHI_HIDDEN = 128
Q_BLOCK = 32
NSA_PROJ_SIZES = (NSA_WIDTH,) + (NSA_KV_WIDTH,) * 6 + (NSA_HEADS * N_BRANCH, NSA_WIDTH)
NSA_PROJ = sum(NSA_PROJ_SIZES)

NUM_BUCKETS = 32
MAX_DISTANCE = 128

NEG_INF = -1e30
FORCE_SCORE = 1e9

kernel_name = "hybrid_hgrn2_nsa_interleaved"


def rms_norm(x, g):
    xf = x.astype(jnp.float32)
    y = xf * lax.rsqrt(jnp.mean(xf * xf, axis=-1, keepdims=True) + RMS_EPS) * g.astype(jnp.float32)
    return y.astype(x.dtype)


def t5_bucket(dist):
    n = jnp.maximum(dist, 0)
    max_exact = NUM_BUCKETS // 2
    large = max_exact + (jnp.log(jnp.maximum(n, 1).astype(jnp.float32) / max_exact)
                         / math.log(MAX_DISTANCE / max_exact) * (NUM_BUCKETS - max_exact)).astype(jnp.int32)
    large = jnp.minimum(large, NUM_BUCKETS - 1)
    return jnp.where(n < max_exact, n, large)


def masked_softmax(logits, mask):
    p = jax.nn.softmax(jnp.where(mask, logits.astype(jnp.float32), NEG_INF), axis=-1)
    return jnp.where(mask, p, 0.0)


def chunk_gated_recurrence(q, k, v, log_f):
    B, S, H, dk = q.shape
    dv = v.shape[-1]
    C = HGRN_CHUNK
    n_chunks = S // C

    def to_chunks(a):
        return a.reshape(B, n_chunks, C, H, a.shape[-1]).transpose(1, 0, 3, 2, 4)

    causal = jnp.tril(jnp.ones((C, C), dtype=bool))[:, :, None]

    def step(state, inp):
        qc, kc, vc, gc = inp
        b = jnp.cumsum(gc, axis=2)
        o_inter = jnp.einsum('bhtk,bhkv->bhtv', qc * jnp.exp(b), state)
        diff = b[:, :, :, None, :] - b[:, :, None, :, :]
        decay = jnp.where(causal, jnp.exp(jnp.where(causal, diff, 0.0)), 0.0)
        scores = jnp.einsum('bhtk,bhtsk,bhsk->bhts', qc, decay, kc)
        o_intra = jnp.einsum('bhts,bhsv->bhtv', scores, vc)
        b_last = b[:, :, -1:, :]
        state = (jnp.exp(b_last[:, :, 0, :, None]) * state
                 + jnp.einsum('bhsk,bhsv->bhkv', kc * jnp.exp(b_last - b), vc))
        return state, o_inter + o_intra

    state0 = jnp.zeros((B, H, dk, dv), jnp.float32)
    _, o = lax.scan(step, state0, (to_chunks(q), to_chunks(k), to_chunks(v), to_chunks(log_f)))
    return o.transpose(1, 0, 3, 2, 4).reshape(B, S, H, dv)


def hgrn2_mixer(h, w_in, lb, head_gain, w_out):
    B, S, _ = h.shape
    proj = (h @ w_in).astype(jnp.float32).reshape(B, S, 4, HGRN_HEADS, HGRN_HEAD_DIM)
    q, f_pre, v, z = proj[:, :, 0], proj[:, :, 1], proj[:, :, 2], proj[:, :, 3]
    lb = lb.reshape(HGRN_HEADS, HGRN_HEAD_DIM)
    log_f = jnp.log(lb + (1.0 - lb) * jax.nn.sigmoid(f_pre))
    k = (1.0 - lb) * jax.nn.sigmoid(-f_pre)
    o = chunk_gated_recurrence(q * HGRN_HEAD_DIM ** -0.5, k, v, log_f)
    o = rms_norm(o, head_gain) * jax.nn.silu(z)
    return o.reshape(B, S, HGRN_WIDTH).astype(h.dtype) @ w_out


def compress_kv(kv, pe, w1, b1, w2):
    B, S, G, dk = kv.shape
    n_cmp = (S - CMP_BLOCK) // CMP_STRIDE + 1
    chunks = kv.reshape(B, S // CMP_STRIDE, CMP_STRIDE, G, dk)
    ratio = CMP_BLOCK // CMP_STRIDE
    blocks = jnp.concatenate([chunks[:, m:m + n_cmp] for m in range(ratio)], axis=2)
    blocks = blocks + pe[None, None, :, None, :].astype(blocks.dtype)
    flat = blocks.transpose(0, 3, 1, 2, 4).reshape(B, G, n_cmp, CMP_BLOCK * dk)
    return jax.nn.gelu(flat @ w1.astype(jnp.float32) + b1.astype(jnp.float32)) @ w2.astype(jnp.float32)


def nsa_mixer(h, w_in, pe_k, pe_v, phi_k_w1, phi_k_b1, phi_k_w2, phi_v_w1, phi_v_b1, phi_v_w2,
              rel_bias, w_out):
    B, S, _ = h.shape
    G, hpg, dk = NSA_KV_GROUPS, NSA_HPG, NSA_HEAD_DIM
    proj = (h @ w_in).astype(jnp.float32)
    split_at = [int(s) for s in np.cumsum(NSA_PROJ_SIZES)[:-1]]
    q, k_c, v_c, k_s, v_s, k_w, v_w, gate_logits, z = jnp.split(proj, split_at, axis=-1)
    q = q.reshape(B, S, G, hpg, dk) * dk ** -0.5

    def kv4(a):
        return a.reshape(B, S, G, dk)

    K_cmp = compress_kv(kv4(k_c), pe_k, phi_k_w1, phi_k_b1, phi_k_w2)
    V_cmp = compress_kv(kv4(v_c), pe_v, phi_v_w1, phi_v_b1, phi_v_w2)
    n_cmp = K_cmp.shape[2]
    n_sel = S // SEL_BLOCK
    n_top = min(N_SELECT, n_sel)

    def sel_blocks(a):
        return a.reshape(B, n_sel, SEL_BLOCK, G, dk).transpose(0, 3, 1, 2, 4)

    def win_pad(a):
        return jnp.pad(kv4(a).transpose(0, 2, 1, 3), ((0, 0), (0, 0), (WINDOW, 0), (0, 0)))

    K_sel, V_sel = sel_blocks(k_s), sel_blocks(v_s)
    K_win, V_win = win_pad(k_w), win_pad(v_w)

    tab = rel_bias.astype(jnp.float32).reshape(NUM_BUCKETS, G, hpg).transpose(1, 0, 2)
    cmp_end = CMP_STRIDE * jnp.arange(n_cmp) + CMP_BLOCK - 1
    ci = jnp.arange(n_cmp)[:, None]
    sj = jnp.arange(n_sel)[None, :]
    overlap = ((CMP_STRIDE * ci < SEL_BLOCK * (sj + 1))
               & (CMP_STRIDE * ci + CMP_BLOCK > SEL_BLOCK * sj)).astype(jnp.float32)
    bidx = jnp.arange(B)[:, None, None, None]
    gidx = jnp.arange(G)[None, :, None, None]
    n_qblk = S // Q_BLOCK
    q_blocks = q.reshape(B, n_qblk, Q_BLOCK, G, hpg, dk).transpose(1, 0, 3, 2, 4, 5)

    def bias_shared(dist):
        return tab[:, t5_bucket(dist)].transpose(0, 1, 3, 2)

    def block_fn(args):
        blk, qb = args
        start = blk * Q_BLOCK
        t = start + jnp.arange(Q_BLOCK)
        dist_c = t[:, None] - cmp_end[None, :]
        logit_c = jnp.einsum('bgqhd,bgkd->bgqhk', qb, K_cmp) + bias_shared(dist_c)
        p_c = masked_softmax(logit_c, (dist_c >= 0)[:, None, :])
        o_c = jnp.einsum('bgqhk,bgkd->bgqhd', p_c, V_cmp)
        importance = jnp.einsum('bgqhk,kn->bgqn', p_c, overlap)
        j = jnp.arange(n_sel)[None, :]
        cur = (t // SEL_BLOCK)[:, None]
        forced = (j == 0) | (j == cur) | (j == cur - 1)
        visible = j * SEL_BLOCK <= t[:, None]
        score = jnp.where(forced, FORCE_SCORE, jnp.where(visible, importance, NEG_INF))
        _, idx = lax.top_k(score, n_top)
        k_g = K_sel[bidx, gidx, idx].reshape(B, G, Q_BLOCK, n_top * SEL_BLOCK, dk)
        v_g = V_sel[bidx, gidx, idx].reshape(B, G, Q_BLOCK, n_top * SEL_BLOCK, dk)
        pos_s = (idx[..., None] * SEL_BLOCK + jnp.arange(SEL_BLOCK)).reshape(B, G, Q_BLOCK, n_top * SEL_BLOCK)
        dist_s = t[:, None] - pos_s
        bias_s = jnp.moveaxis(tab[gidx, t5_bucket(dist_s)], -1, -2)
        logit_s = jnp.einsum('bgqhd,bgqkd->bgqhk', qb, k_g) + bias_s
        p_s = masked_softmax(logit_s, (dist_s >= 0)[..., None, :])
        o_s = jnp.einsum('bgqhk,bgqkd->bgqhd', p_s, v_g)
        k_wb = lax.dynamic_slice_in_dim(K_win, start, WINDOW + Q_BLOCK, axis=2)
        v_wb = lax.dynamic_slice_in_dim(V_win, start, WINDOW + Q_BLOCK, axis=2)
        pos_w = start - WINDOW + jnp.arange(WINDOW + Q_BLOCK)
        dist_w = t[:, None] - pos_w[None, :]
        mask_w = (dist_w >= 0) & (dist_w < WINDOW) & (pos_w[None, :] >= 0)
        logit_w = jnp.einsum('bgqhd,bgkd->bgqhk', qb, k_wb) + bias_shared(dist_w)
        p_w = masked_softmax(logit_w, mask_w[:, None, :])
        o_w = jnp.einsum('bgqhk,bgkd->bgqhd', p_w, v_wb)
        return jnp.stack([o_c, o_s, o_w], axis=-2)

    out = lax.map(block_fn, (jnp.arange(n_qblk), q_blocks))
    out = out.transpose(1, 0, 3, 2, 4, 5, 6).reshape(B, S, NSA_HEADS, N_BRANCH, dk)
    gates = jax.nn.sigmoid(gate_logits.reshape(B, S, NSA_HEADS, N_BRANCH))
    o = jnp.einsum('bshc,bshcd->bshd', gates, out).reshape(B, S, NSA_WIDTH) * jax.nn.silu(z)
    return o.astype(h.dtype) @ w_out


def setup_inputs(seed: int = 0) -> dict:
    key = jax.random.key(seed)
    ks = jax.random.split(key, 20)
    f32 = jnp.float32

    def nrm(k, shape, scale):
        return jax.random.normal(k, shape, f32) * scale

    nb = N_NSA_LAYERS
    return {
        "x": nrm(ks[0], (BATCH, SEQ, D_MODEL), 1.0),
        "norm_gains": 1.0 + nrm(ks[1], (DEPTH, D_MODEL), 0.02),
        "final_gain": 1.0 + nrm(ks[2], (D_MODEL,), 0.02),
        "rel_bias": nrm(ks[3], (NUM_BUCKETS, NSA_HEADS), 0.5),
        "hgrn_lb": nrm(ks[4], (DEPTH + 1, HGRN_WIDTH), 0.5),
        "hgrn_w_in": nrm(ks[5], (N_HGRN_LAYERS, D_MODEL, 4 * HGRN_WIDTH), D_MODEL ** -0.5),
        "hgrn_head_gain": 1.0 + nrm(ks[6], (N_HGRN_LAYERS, HGRN_HEAD_DIM), 0.02),
        "hgrn_w_out": nrm(ks[7], (N_HGRN_LAYERS, HGRN_WIDTH, D_MODEL), HGRN_WIDTH ** -0.5),
        "nsa_w_in": nrm(ks[8], (nb, D_MODEL, NSA_PROJ), D_MODEL ** -0.5),
        "nsa_pe_k": nrm(ks[9], (nb, CMP_BLOCK, NSA_HEAD_DIM), 0.1),
        "nsa_pe_v": nrm(ks[10], (nb, CMP_BLOCK, NSA_HEAD_DIM), 0.1),
        "nsa_phi_k_w1": nrm(ks[11], (nb, CMP_BLOCK * NSA_HEAD_DIM, PHI_HIDDEN), (CMP_BLOCK * NSA_HEAD_DIM) ** -0.5),
        "nsa_phi_k_b1": nrm(ks[12], (nb, PHI_HIDDEN), 0.01),
        "nsa_phi_k_w2": nrm(ks[13], (nb, PHI_HIDDEN, NSA_HEAD_DIM), PHI_HIDDEN ** -0.5),
        "nsa_phi_v_w1": nrm(ks[14], (nb, CMP_BLOCK * NSA_HEAD_DIM, PHI_HIDDEN), (CMP_BLOCK * NSA_HEAD_DIM) ** -0.5),
        "nsa_phi_v_b1": nrm(ks[15], (nb, PHI_HIDDEN), 0.01),
        "nsa_phi_v_w2": nrm(ks[16], (nb, PHI_HIDDEN, NSA_HEAD_DIM), PHI_HIDDEN ** -0.5),
        "nsa_w_out": nrm(ks[17], (nb, NSA_WIDTH, D_MODEL), NSA_WIDTH ** -0.5),
    }


def reference(x, norm_gains, final_gain, rel_bias, hgrn_lb, hgrn_w_in, hgrn_head_gain, hgrn_w_out,
              nsa_w_in, nsa_pe_k, nsa_pe_v, nsa_phi_k_w1, nsa_phi_k_b1, nsa_phi_k_w2,
              nsa_phi_v_w1, nsa_phi_v_b1, nsa_phi_v_w2, nsa_w_out):
    lower_bounds = jnp.cumsum(jax.nn.softmax(hgrn_lb.astype(jnp.float32), axis=0), axis=0)
    for i in range(DEPTH):
        h = rms_norm(x, norm_gains[i])
        a = i // N_MIXERS
        if i % N_MIXERS == 0:
            y = hgrn2_mixer(h, hgrn_w_in[a], lower_bounds[i], hgrn_head_gain[a], hgrn_w_out[a])
        else:
            y = nsa_mixer(h, nsa_w_in[a], nsa_pe_k[a], nsa_pe_v[a],
                          nsa_phi_k_w1[a], nsa_phi_k_b1[a], nsa_phi_k_w2[a],
                          nsa_phi_v_w1[a], nsa_phi_v_b1[a], nsa_phi_v_w2[a],
                          rel_bias, nsa_w_out[a])
        x = x + y.astype(x.dtype)
    return rms_norm(x, final_gain)
```

```python
import contextlib
import numpy as np
import concourse.bass as bass
import concourse.mybir as mybir
from concourse.bass_utils import run_bass_kernel_spmd

F32 = mybir.dt.float32
BF16 = mybir.dt.bfloat16
AF = mybir.ActivationFunctionType
ALU = mybir.AluOpType
AX = mybir.AxisListType

S = 4096
D = 2048
NT = S // 128
NST = S // 512
EPS = 1e-6
NEG = -30000.0
HW = D // 2
NGC = 2
PAIRS = [[0, 1], [2, 3], [4, 5], [6, 7]]


class Buf:
    __slots__ = ("t", "w", "r", "name")

    def __init__(self, t, name=""):
        self.t = t
        self.w = None
        self.r = {}
        self.name = name

    def __getitem__(self, idx):
        return self.t[idx]


class KB:
    def __init__(self, nc, es, nd=8):
        self.nc = nc
        self.es = es
        self.eng = {"pe": nc.tensor, "act": nc.scalar, "dve": nc.vector, "pool": nc.gpsimd, "sp": nc.sync}
        self.sems = {}
        self.cnt = {}
        for e in ["pe", "act", "dve", "pool"]:
            self.sems[e] = es.enter_context(nc.semaphore("s_" + e))
            self.cnt[e] = 0
        self.waited = {e: {} for e in self.eng}
        self.nd = nd
        self.dq = {}
        for q in ["sp", "pool"]:
            lst = []
            for i in range(nd):
                key = "d_%s%d" % (q, i)
                self.sems[key] = es.enter_context(nc.semaphore(key))
                lst.append([key, 0])
            self.dq[q] = [lst, 0]
        self.n_inst = 0
        self.cq = []
        for i in range(4):
            key = "cc_%d" % i
            self.sems[key] = es.enter_context(nc.semaphore(key))
            self.cq.append([key, 0])
        self.cq_i = 0

    def coll(self, src_t, dst_t, reads, writes):
        slot = self.cq[self.cq_i]
        self.cq_i = (self.cq_i + 1) % len(self.cq)
        deps = self._deps(reads, writes)
        if slot[1] > 0 and deps.get(slot[0], 0) < slot[1]:
            deps[slot[0]] = slot[1]
        self._wait("pool", deps)
        inst = self.nc.gpsimd.collective_compute("AllGather", ALU.bypass, replica_groups=PAIRS,
                                                 ins=[src_t.ap().opt()], outs=[dst_t.ap().opt()])
        slot[1] += 1
        inst.then_inc(self.sems[slot[0]], 1)
        self._mark((slot[0], slot[1]), reads, writes)
        self.n_inst += 1

    def tile(self, name, shape, dtype):
        self.n_tiles = getattr(self, "n_tiles", 0) + 1
        t = self.es.enter_context(self.nc.sbuf_tensor("sb%d_%s" % (self.n_tiles, name), list(shape), dtype))
        return Buf(t, name)

    def psum(self, name, shape, dtype):
        self.n_tiles = getattr(self, "n_tiles", 0) + 1
        t = self.es.enter_context(self.nc.psum_tensor("ps%d_%s" % (self.n_tiles, name), list(shape), dtype))
        return Buf(t, name)

    def dram(self, name, shape, dtype, kind="Internal"):
        t = self.nc.dram_tensor(name, list(shape), dtype, kind=kind)
        return Buf(t.ap(), name)

    def _deps(self, reads, writes):
        deps = {}

        def add(ev):
            if ev is None:
                return
            k, v = ev
            if deps.get(k, 0) < v:
                deps[k] = v

        for b in reads:
            add(b.w)
        for b in writes:
            add(b.w)
            for k, v in b.r.items():
                add((k, v))
        return deps

    def _wait(self, e, deps):
        w = self.waited[e]
        eo = self.eng[e]
        for k, v in deps.items():
            if e == "pe" and k == "pe":
                continue
            if w.get(k, 0) < v:
                eo.wait_ge(self.sems[k], v)
                w[k] = v

    def _mark(self, ev, reads, writes):
        k, v = ev
        for b in reads:
            if b.r.get(k, 0) < v:
                b.r[k] = v
        for b in writes:
            b.w = ev
            b.r = {}

    def op(self, e, fn, reads, writes):
        self._wait(e, self._deps(reads, writes))
        inst = fn(self.eng[e])
        self.cnt[e] += 1
        inst.then_inc(self.sems[e], 1)
        self._mark((e, self.cnt[e]), reads, writes)
        self.n_inst += 1

    def dma(self, q, out, in_, reads, writes, **kw):
        lst, nxt = self.dq[q]
        slot = lst[nxt]
        self.dq[q][1] = (nxt + 1) % self.nd
        deps = self._deps(reads, writes)
        if slot[1] > 0:
            if deps.get(slot[0], 0) < slot[1]:
                deps[slot[0]] = slot[1]
        self._wait(q, deps)
        inst = self.eng[q].dma_start(out=out, in_=in_, **kw)
        slot[1] += 16
        inst.then_inc(self.sems[slot[0]], 16)
        self._mark((slot[0], slot[1]), reads, writes)
        self.n_inst += 1

    def barrier(self):
        deps = {}
        for e in ["pe", "act", "dve", "pool"]:
            if self.cnt[e] > 0:
                deps[e] = self.cnt[e]
        for q in self.dq:
            for key, val in self.dq[q][0]:
                if val > 0:
                    deps[key] = val
        for key, val in self.cq:
            if val > 0:
                deps[key] = val
        for e in self.eng:
            d = dict(deps)
            if e in d:
                del d[e]
            w = self.waited[e]
            for k, v in d.items():
                if w.get(k, 0) < v:
                    self.eng[e].wait_ge(self.sems[k], v)
                    w[k] = v

    def wait_all(self, e, bufs):
        deps = {}
        for b in bufs:
            if b.w is not None:
                k, v = b.w
                if deps.get(k, 0) < v:
                    deps[k] = v
        self._wait(e, deps)

    def mm(self, out_b, out_ap, l_b, l_ap, r_b, r_ap, start, stop, **kw):
        self.op("pe", lambda pe: pe.matmul(out_ap, l_ap, r_ap, start=start, stop=stop, **kw), [l_b, r_b], [out_b])

    def tr(self, out_b, out_ap, in_b, in_ap, ident):
        self.op("pe", lambda pe: pe.transpose(out_ap, in_ap, ident[:]), [in_b, ident], [out_b])

    def act(self, out_b, out_ap, in_b, in_ap, func, extra_reads=(), **kw):
        self.op("act", lambda a: a.activation(out=out_ap, in_=in_ap, func=func, **kw), [in_b] + list(extra_reads), [out_b])

    def tt(self, e, out_b, out_ap, a_b, a_ap, b_b, b_ap, op):
        self.op(e, lambda v: v.tensor_tensor(out=out_ap, in0=a_ap, in1=b_ap, op=op), [a_b, b_b], [out_b])

    def ts(self, e, out_b, out_ap, a_b, a_ap, s1, s2, op0, op1=None, extra_reads=()):
        if op1 is None:
            fn = lambda v: v.tensor_scalar(out=out_ap, in0=a_ap, scalar1=s1, scalar2=None, op0=op0)
        else:
            fn = lambda v: v.tensor_scalar(out=out_ap, in0=a_ap, scalar1=s1, scalar2=s2, op0=op0, op1=op1)
        self.op(e, fn, [a_b] + list(extra_reads), [out_b])

    def stt(self, out_b, out_ap, a_b, a_ap, scalar, b_b, b_ap, op0, op1, extra_reads=()):
        self.op("dve", lambda v: v.scalar_tensor_tensor(out=out_ap, in0=a_ap, scalar=scalar, in1=b_ap, op0=op0, op1=op1),
                [a_b, b_b] + list(extra_reads), [out_b])

    def copy(self, e, out_b, out_ap, in_b, in_ap):
        if e == "act":
            self.op("act", lambda a: a.copy(out=out_ap, in_=in_ap), [in_b], [out_b])
        else:
            self.op(e, lambda v: v.tensor_copy(out=out_ap, in_=in_ap), [in_b], [out_b])


class Ctx:
    pass


def setup_common(kb, c):
    c.xt = kb.tile("xt", [128, D], F32)
    c.hb = kb.tile("hb", [128, D], BF16)
    c.st1 = kb.tile("st1", [128, 8], F32)
    c.hT = kb.tile("hT", [128, 16, 512], BF16)
    c.ogT = kb.tile("ogT", [128, 16, 512], BF16)
    c.wb = [kb.tile("wb%d" % i, [128, 16, 512], BF16) for i in range(2)]
    c.wb_i = 0
    c.gb = kb.tile("gb", [128, D], F32)
    c.og = kb.tile("og", [128, 4, HW], BF16)
    c.ogf = kb.tile("ogf", [128, 4, D], BF16)
    c.xs = [kb.tile("xs%d" % i, [128, 512], F32) for i in range(2)]
    c.xo = [kb.tile("xo%d" % i, [128, 512], F32) for i in range(2)]
    c.xs_i = 0


def next_pacc(c):
    p = c.pacc[c.pacc_i]
    c.pacc_i = (c.pacc_i + 1) % len(c.pacc)
    return p


def next_ptr(c):
    p = c.ptr[c.ptr_i]
    c.ptr_i = (c.ptr_i + 1) % len(c.ptr)
    return p


def load_norm_transpose(kb, c, x_tiles, x_ap_fn, st, gain_b):
    for j in range(4):
        ti = st * 4 + j
        for (ap, col0, ncols, rb) in x_ap_fn(ti):
            kb.dma("sp", c.xt[:, col0:col0 + ncols], ap, [rb], [c.xt])
        kb.act(c.hb, c.hb[:, :], c.xt, c.xt[:, :], AF.Square, extra_reads=[], accum_out=c.st1[:, 0:1])
        kb.op("act", lambda a: a.activation(out=c.st1[:, 1:2], in_=c.st1[:, 0:1], func=AF.Sqrt, bias=c.epsb[:, 0:1], scale=1.0 / D),
              [c.hb, c.st1, c.epsb], [c.st1])
        kb.op("dve", lambda v: v.reciprocal(out=c.st1[:, 2:3], in_=c.st1[:, 1:2]), [c.st1], [c.st1])
        kb.stt(c.hb, c.hb[:, :], c.xt, c.xt[:, :], c.st1[:, 2:3], gain_b, gain_b[:, :], ALU.mult, ALU.mult, extra_reads=[c.st1])
        transpose_16(kb, c, c.hb, lambda kc: c.hb[:, kc * 128:(kc + 1) * 128], c.hT, j)


def transpose_16(kb, c, src_b, src_fn, dst_b, j):
    for half in range(2):
        p = next_ptr(c)
        for i in range(8):
            kc = half * 8 + i
            kb.tr(p, p[:, i * 128:(i + 1) * 128], src_b, src_fn(kc), c.ident)
        eng = "act" if half == 0 else "dve"
        kb.copy(eng, dst_b, dst_b[:, half * 8:(half + 1) * 8, j * 128:(j + 1) * 128],
                p, p[:, :].rearrange("p (a b) -> p a b", b=128))


WBLOCKS = {
    "hgrn_w_in": [(i * 512, 512) for i in range(8)],
    "hgrn_w_out": [(i * 512, 512) for i in range(2)],
    "nsa_w_in": [(0, 512), (512, 512), (1024, 512), (1536, 512), (2048, 512), (2560, 24), (2584, 512), (3096, 512)],
    "nsa_w_out": [(i * 512, 512) for i in range(2)],
}


def setup_wconv(kb, c, ins):
    c.WB = {}
    c.WBbuf = {}
    c.conv_q = []
    for wn, blks in WBLOCKS.items():
        c.WB[wn] = kb.dram("WB_" + wn, [len(blks), 128, 16, 512], BF16)
        c.WBbuf[wn] = [Buf(None, "%s_%d" % (wn, i)) for i in range(len(blks))]
        w3 = ins[wn][:, :].rearrange("(kc p) n -> p kc n", p=128)
        order = list(range(len(blks)))
        if wn == "hgrn_w_in":
            order = [sec * NGC + g for g in range(NGC) for sec in range(4)]
        for i in order:
            col0, ncols = blks[i]
            c.conv_q.append((wn, i, w3[:, :, col0:col0 + ncols], ncols))


def issue_conv(kb, c, n):
    for _ in range(n):
        if not c.conv_q:
            return
        wn, i, src, ncols = c.conv_q.pop(0)
        kb.dma("pool", c.WB[wn][i, :, :, 0:ncols], src, [], [c.WBbuf[wn][i]])


def load_w_block(kb, c, wn, i):
    ncols = WBLOCKS[wn][i][1]
    wb = c.wb[c.wb_i]
    c.wb_i = (c.wb_i + 1) % len(c.wb)
    kb.dma("sp", wb[:, :, 0:ncols], c.WB[wn][i, :, :, 0:ncols], [c.WBbuf[wn][i]], [wb])
    return wb


def proj_feature_major(kb, c, wb, cc, ncols_tok=512):
    p = next_pacc(c)
    for kc in range(16):
        kb.mm(p, p[:, 0:ncols_tok], wb, wb[:, kc, cc * 128:(cc + 1) * 128], c.hT, c.hT[:, kc, 0:ncols_tok], kc == 0, kc == 15)
    return p


def proj_token_major(kb, c, wb, j, ncols=512, aT=None):
    aT = aT or c.hT
    p = next_pacc(c)
    for kc in range(16):
        kb.mm(p, p[:, 0:ncols], aT, aT[:, kc, j * 128:(j + 1) * 128], wb, wb[:, kc, 0:ncols], kc == 0, kc == 15)
    return p


def load_gathered_og(kb, c, gath, st_buf, dstT):
    for j in range(4):
        for r in range(2):
            kb.dma("sp", c.ogf[:, j, r * HW:(r + 1) * HW], gath.ap()[r * 512 + j * 128:r * 512 + (j + 1) * 128, :], [st_buf], [c.ogf])
    for j in range(4):
        transpose_16(kb, c, c.ogf, lambda kc, j=j: c.ogf[:, j, kc * 128:(kc + 1) * 128], dstT, j)


def hgrn_out(kb, c, ins, scr, st):
    load_gathered_og(kb, c, scr["OG0g"][st], scr["OG0g_b"][st], c.ogT)
    for nb in range(2):
        wb = load_w_block(kb, c, "hgrn_w_out", nb)
        for j in range(4):
            ti = st * 4 + j
            p = proj_token_major(kb, c, wb, j, aT=c.ogT)
            xs = c.xs[c.xs_i]
            xo = c.xo[c.xs_i]
            c.xs_i ^= 1
            kb.dma("sp", xs[:, :], ins["xh"][ti * 128:(ti + 1) * 128, nb * 512:(nb + 1) * 512], [], [xs])
            kb.tt("dve", xo, xo[:, :], p, p[:, :], xs, xs[:, :], ALU.add)
            kb.dma("pool", scr["X1own"][ti * 128:(ti + 1) * 128, nb * 512:(nb + 1) * 512], xo[:, :], [xo], [])
            kb.dma("pool", scr["X1src"][st].ap()[j * 128:(j + 1) * 128, nb * 512:(nb + 1) * 512], xo[:, :], [xo], [scr["X1src_b"][st]])
    kb.coll(scr["X1src"][st], scr["X1g"][st], [scr["X1src_b"][st]], [scr["X1g_b"][st]])


def setup_hgrn(kb, c, ins):
    c.lbT = kb.tile("lbT", [128, 3, 4 * NGC], F32)
    c.lbw = kb.tile("lbw", [128, 6, 4 * NGC], F32)
    c.G4 = kb.tile("G4", [128, 512], F32)
    c.M4 = kb.tile("M4", [128, 4, 128], F32)
    c.rmask = kb.tile("rmask", [128, 512], F32)
    c.qs = kb.tile("qs", [128, 4, 512], F32)
    c.tmp = [kb.tile("htmp%d" % i, [128, 512], F32) for i in range(9)]
    c.qin = kb.tile("qin", [128, 4, 512], BF16)
    c.qmid = kb.tile("qmid", [128, 4, 512], BF16)
    c.kmid = kb.tile("kmid", [128, 4, 512], BF16)
    c.kend = kb.tile("kend", [128, 4, 512], BF16)
    c.kendT = kb.tile("kendT", [128, 4, 512], BF16)
    c.Vg = kb.tile("Vg", [128, 4, 512], BF16)
    c.gz = kb.tile("gz", [128, 4, 512], F32)
    c.dec = kb.tile("dec", [128, 4, 4], F32)
    c.Sst = kb.tile("Sst", [128, 4 * NGC, 128], F32)
    c.SbfA = [kb.tile("SbfA%d" % i, [128, 4, 128], BF16) for i in range(4)]
    c.PT4 = [kb.tile("PT%d" % i, [128, 4, 128], BF16) for i in range(4)]
    c.dS = [kb.tile("dS%d" % i, [128, 512], F32) for i in range(4)]
    c.sq2 = [kb.tile("sq2_%d" % i, [128, 512], F32) for i in range(2)]
    c.st22 = [kb.tile("st22_%d" % i, [128, 12], F32) for i in range(2)]
    c.stmp = kb.tile("stmp", [128, 512], F32)

    kb.dma("sp", c.lbT[:, :, :], ins["lbT"][:, :, :], [ins["lbT"]], [c.lbT])
    kb.dma("sp", c.G4[:, :], ins["G4"][:, :], [ins["G4"]], [c.G4])
    kb.dma("sp", c.M4[:, :, :], ins["M4"][:, :, :], [ins["M4"]], [c.M4])
    kb.dma("sp", c.rmask[:, :], ins["rmask"][:, :], [ins["rmask"]], [c.rmask])
    w = c.lbw
    kb.tt("dve", w, w[:, 3, :], c.lbT, c.lbT[:, 0, :], c.lbT, c.lbT[:, 1, :], ALU.max)
    kb.tt("dve", w, w[:, 3, :], w, w[:, 3, :], c.lbT, c.lbT[:, 2, :], ALU.max)
    for r in range(3):
        kb.tt("dve", c.lbT, c.lbT[:, r, :], c.lbT, c.lbT[:, r, :], w, w[:, 3, :], ALU.subtract)
    kb.act(c.lbT, c.lbT[:, :, :], c.lbT, c.lbT[:, :, :], AF.Exp)
    kb.tt("dve", w, w[:, 4, :], c.lbT, c.lbT[:, 0, :], c.lbT, c.lbT[:, 1, :], ALU.add)
    kb.tt("dve", w, w[:, 4, :], w, w[:, 4, :], c.lbT, c.lbT[:, 2, :], ALU.add)
    kb.op("dve", lambda v: v.reciprocal(out=w[:, 5, :], in_=w[:, 4, :]), [w], [w])
    kb.tt("dve", w, w[:, 0, :], c.lbT, c.lbT[:, 0, :], w, w[:, 5, :], ALU.mult)
    kb.ts("dve", w, w[:, 1, :], w, w[:, 0, :], -1.0, 1.0, ALU.mult, ALU.add)
    kb.ts("dve", w, w[:, 2, :], w, w[:, 1, :], -1.0, None, ALU.mult)
    kb.op("dve", lambda v: v.memset(c.Sst[:, :, :], 0.0), [], [c.Sst])
    for i in range(4):
        kb.op("dve", lambda v, i=i: v.memset(c.PT4[i][:, :, :], 0.0), [], [c.PT4[i]])


def hgrn_layer(kb, c, ins, scr, x_tiles, x_ap_fn, nst=NST):
    lbw = c.lbw
    T = c.tmp
    for st in range(nst):
        issue_conv(kb, c, 4)
        load_norm_transpose(kb, c, x_tiles, x_ap_fn, st, c.gb)
        for g in range(NGC):
            wb = load_w_block(kb, c, "hgrn_w_in", 0 * NGC + g)
            for cc in range(4):
                p = proj_feature_major(kb, c, wb, cc)
                kb.op("act", lambda a, p=p, cc=cc: a.mul(out=c.qs[:, cc, :], in_=p[:, :], mul=128.0 ** -0.5), [p], [c.qs])
            wb = load_w_block(kb, c, "hgrn_w_in", 1 * NGC + g)
            for cc in range(4):
                hd = 4 * g + cc
                p = proj_feature_major(kb, c, wb, cc)
                sig, lf, bb, nbb, kk, e1, e2, e3, e4 = T
                kb.act(sig, sig[:, :], p, p[:, :], AF.Sigmoid)
                kb.act(lf, lf[:, :], sig, sig[:, :], AF.Ln, extra_reads=[lbw], bias=lbw[:, 0, hd:hd + 1], scale=lbw[:, 1, hd:hd + 1])
                kb.op("dve", lambda v: v.tensor_tensor_scan(out=bb[:, :], data0=c.rmask[:, :], data1=lf[:, :], initial=0.0,
                                                            op0=ALU.mult, op1=ALU.add), [c.rmask, lf], [bb])
                kb.ts("dve", nbb, nbb[:, :], bb, bb[:, :], -1.0, None, ALU.mult)
                kb.ts("pool", kk, kk[:, :], sig, sig[:, :], lbw[:, 2, hd:hd + 1], lbw[:, 1, hd:hd + 1], ALU.mult, ALU.add, extra_reads=[lbw])
                kb.act(e1, e1[:, :], bb, bb[:, :], AF.Exp)
                for j in range(4):
                    sl = slice(j * 128, (j + 1) * 128)
                    mid = j * 128 + 63
                    last = j * 128 + 127
                    kb.act(e2, e2[:, sl], bb, bb[:, sl], AF.Exp, extra_reads=[nbb], bias=nbb[:, mid:mid + 1], scale=1.0)
                    kb.act(e3, e3[:, sl], bb, bb[:, sl], AF.Exp, extra_reads=[], bias=bb[:, mid:mid + 1], scale=-1.0)
                    kb.act(e4, e4[:, sl], bb, bb[:, sl], AF.Exp, extra_reads=[], bias=bb[:, last:last + 1], scale=-1.0)
                kb.copy("pool", c.dec, c.dec[:, cc, :], e1, e1[:, 127::128])
                kb.tt("dve", c.qin, c.qin[:, cc, :], c.qs, c.qs[:, cc, :], e1, e1[:, :], ALU.mult)
                kb.tt("dve", c.qmid, c.qmid[:, cc, :], c.qs, c.qs[:, cc, :], e2, e2[:, :], ALU.mult)
                kb.tt("pool", c.kmid, c.kmid[:, cc, :], kk, kk[:, :], e3, e3[:, :], ALU.mult)
                kb.tt("pool", c.kend, c.kend[:, cc, :], kk, kk[:, :], e4, e4[:, :], ALU.mult)
            wb = load_w_block(kb, c, "hgrn_w_in", 2 * NGC + g)
            for j in range(4):
                p = proj_token_major(kb, c, wb, j)
                kb.copy("act", c.Vg, c.Vg[:, j, :], p, p[:, :])
            wb = load_w_block(kb, c, "hgrn_w_in", 3 * NGC + g)
            for j in range(4):
                p = proj_token_major(kb, c, wb, j)
                kb.act(c.stmp, c.stmp[:, :], p, p[:, :], AF.Silu)
                kb.tt("dve", c.gz, c.gz[:, j, :], c.stmp, c.stmp[:, :], c.G4, c.G4[:, :], ALU.mult)
            for half in range(2):
                pt = next_ptr(c)
                for i in range(8):
                    idx = half * 8 + i
                    j, cc = idx // 4, idx % 4
                    kb.tr(pt, pt[:, i * 128:(i + 1) * 128], c.kend, c.kend[:, cc, j * 128:(j + 1) * 128], c.ident)
                kb.copy("act", c.kendT, c.kendT[:, half * 2:(half + 1) * 2, :], pt,
                        pt[:, :].rearrange("p (a b) -> p a b", b=512))
            ps_s, ps_d = c.pm[0], c.pm[2]
            for j in range(4):
                PT = c.PT4[j]
                t0 = j * 128
                for cc in range(4):
                    kb.mm(ps_s, ps_s[:, cc * 128 + 64:cc * 128 + 128], c.kmid, c.kmid[:, cc, t0:t0 + 128],
                          c.qmid, c.qmid[:, cc, t0 + 64:t0 + 128], True, True)
                    kb.mm(ps_s, ps_s[0:64, cc * 128:cc * 128 + 64], c.kmid, c.kmid[:, cc, t0:t0 + 64],
                          c.qmid, c.qmid[:, cc, t0:t0 + 64], True, True)
                ps3 = ps_s[:, :].rearrange("p (a b) -> p a b", b=128)
                kb.tt("dve", PT, PT[:, :, 64:128], ps_s, ps3[:, :, 64:128], c.M4, c.M4[:, :, 64:128], ALU.mult)
                ps3b = ps_s[0:64, :].rearrange("p (a b) -> p a b", b=128)
                kb.tt("dve", PT, PT[0:64, :, 0:64], ps_s, ps3b[:, :, 0:64], c.M4, c.M4[0:64, :, 0:64], ALU.mult)
                for cc in range(4):
                    kb.mm(ps_d, ps_d[:, cc * 128:(cc + 1) * 128], c.kendT, c.kendT[:, j, cc * 128:(cc + 1) * 128],
                          c.Vg, c.Vg[:, j, cc * 128:(cc + 1) * 128], True, True)
                kb.copy("act", c.dS[j], c.dS[j][:, :], ps_d, ps_d[:, :])
            S4 = c.Sst[:, 4 * g:4 * g + 4, :]
            kb.copy("act", c.SbfA[0], c.SbfA[0][:, :, :], c.Sst, S4)
            for j in range(4):
                for cc in range(4):
                    hd = 4 * g + cc
                    kb.stt(c.Sst, c.Sst[:, hd, :], c.Sst, c.Sst[:, hd, :], c.dec[:, cc, j:j + 1], c.dS[j], c.dS[j][:, cc * 128:(cc + 1) * 128],
                           ALU.mult, ALU.add, extra_reads=[c.dec])
                if j < 3:
                    kb.copy("act", c.SbfA[j + 1], c.SbfA[j + 1][:, :, :], c.Sst, S4)
            for j in range(4):
                PT = c.PT4[j]
                t0 = j * 128
                ps_o = c.pm[1] if j % 2 == 0 else c.pm[3]
                for cc in range(4):
                    kb.mm(ps_o, ps_o[:, cc * 128:(cc + 1) * 128], PT, PT[:, cc, :], c.Vg, c.Vg[:, j, cc * 128:(cc + 1) * 128], True, False)
                    kb.mm(ps_o, ps_o[:, cc * 128:(cc + 1) * 128], c.qin, c.qin[:, cc, t0:t0 + 128], c.SbfA[j], c.SbfA[j][:, cc, :], False, True)
                sq = c.sq2[j % 2]
                st2 = c.st22[j % 2]
                kb.act(sq, sq[:, :], ps_o, ps_o[:, :], AF.Square)
                kb.op("dve", lambda v, sq=sq, st2=st2: v.tensor_reduce(out=st2[:, 0:4], in_=sq[:, :].rearrange("p (a b) -> p a b", b=128),
                                                                     axis=AX.X, op=ALU.add), [sq], [st2])
                kb.op("act", lambda a, st2=st2: a.activation(out=st2[:, 4:8], in_=st2[:, 0:4], func=AF.Sqrt, bias=c.epsb[:, 0:1], scale=1.0 / 128),
                      [st2, c.epsb], [st2])
                kb.op("dve", lambda v, st2=st2: v.reciprocal(out=st2[:, 8:12], in_=st2[:, 4:8]), [st2], [st2])
                for cc in range(4):
                    hd = 4 * g + cc
                    kb.stt(c.og, c.og[:, j, hd * 128:(hd + 1) * 128], ps_o, ps_o[:, cc * 128:(cc + 1) * 128], st2[:, 8 + cc:9 + cc],
                           c.gz, c.gz[:, j, cc * 128:(cc + 1) * 128], ALU.mult, ALU.mult, extra_reads=[st2])
        for j in range(4):
            kb.dma("pool", scr["OG0src"][st].ap()[j * 128:(j + 1) * 128, :], c.og[:, j, :], [c.og], [scr["OG0src_b"][st]])
        kb.coll(scr["OG0src"][st], scr["OG0g"][st], [scr["OG0src_b"][st]], [scr["OG0g_b"][st]])
        if st >= 1:
            hgrn_out(kb, c, ins, scr, st - 1)
    hgrn_out(kb, c, ins, scr, nst - 1)


SCALE = 128.0 ** -0.5
BIG = 30000.0


def rot(c, name, n):
    lst = getattr(c, name)
    i = getattr(c, name + "_i", 0)
    setattr(c, name + "_i", (i + 1) % n)
    return lst[i]


def nsa_proj_phase(kb, c, ins, scr, x_tiles, x_fn):
    c.obf = [kb.tile("obf%d" % i, [128, 512], BF16) for i in range(4)]
    c.of32 = [kb.tile("of32_%d" % i, [128, 512], F32) for i in range(3)]
    for st in range(NST):
        load_norm_transpose(kb, c, x_tiles, x_fn, st, c.gb)
        t0 = st * 512
        for nb in range(NGC):
            wb = load_w_block(kb, c, "nsa_w_in", nb)
            for cc in range(4):
                p = proj_feature_major(kb, c, wb, cc)
                ob = rot(c, "obf", 4)
                kb.op("act", lambda a, p=p, ob=ob: a.mul(out=ob[:, :], in_=p[:, :], mul=SCALE), [p], [ob])
                kb.dma("pool", scr["QT"][nb * 4 + cc, :, t0:t0 + 512], ob[:, :], [ob], [])
        for nb, dsts in [(2, ("KCT", "VCT")), (3, ("KST", "KWT"))]:
            wb = load_w_block(kb, c, "nsa_w_in", nb)
            for cc in range(4):
                p = proj_feature_major(kb, c, wb, cc)
                ob = rot(c, "obf", 4)
                kb.copy("act", ob, ob[:, :], p, p[:, :])
                kb.dma("pool", scr[dsts[cc // 2]][cc % 2, :, t0:t0 + 512], ob[:, :], [ob], [])
        wb = load_w_block(kb, c, "nsa_w_in", 4)
        for j in range(4):
            p = proj_token_major(kb, c, wb, j)
            ob = rot(c, "obf", 4)
            kb.copy("act", ob, ob[:, :], p, p[:, :])
            kb.dma("pool", scr["VS"][t0 + j * 128:t0 + (j + 1) * 128, :], ob[:, 0:256], [ob], [])
            kb.dma("pool", scr["VW"][t0 + j * 128:t0 + (j + 1) * 128, :], ob[:, 256:512], [ob], [])
        wb = load_w_block(kb, c, "nsa_w_in", 5)
        for j in range(4):
            p = proj_token_major(kb, c, wb, j, ncols=24)
            of = rot(c, "of32", 3)
            kb.act(of, of[:, 0:24], p, p[:, 0:24], AF.Sigmoid)
            kb.dma("pool", scr["GATE"][t0 + j * 128:t0 + (j + 1) * 128, :], of[:, 0:24], [of], [])
        for i in range(NGC):
            wb = load_w_block(kb, c, "nsa_w_in", 6 + i)
            for j in range(4):
                p = proj_token_major(kb, c, wb, j)
                of = rot(c, "of32", 3)
                kb.act(of, of[:, :], p, p[:, :], AF.Silu)
                kb.dma("pool", scr["SZ"][t0 + j * 128:t0 + (j + 1) * 128, i * 512:(i + 1) * 512], of[:, :], [of], [])


def setup_attn(kb, c, ins, scr):
    c.KST = kb.tile("KST", [128, S], BF16)
    c.KWT = kb.tile("KWT", [128, S], BF16)
    c.VSe = kb.tile("VSe", [128, NT, 129], BF16)
    c.VWe = kb.tile("VWe", [128, NT, 129], BF16)
    c.KcT = kb.tile("KcT", [128, 256], BF16)
    c.Vce = kb.tile("Vce", [128, 2, 193], BF16)
    c.BT = [kb.tile("BT%d" % i, [128, 512], F32) for i in range(3)]
    c.q4s = [kb.tile("q4_%d" % i, [128, 4, 128], BF16) for i in range(3)]
    c.PTs = [kb.tile("PTa%d" % i, [128, 512], BF16) for i in range(11)]
    c.sbs = [kb.tile("sbs%d" % i, [128, 512], F32) for i in range(2)]
    c.BCs = [kb.tile("BC%d" % i, [128, 512], F32) for i in range(2)]
    c.accs = [kb.tile("acc%d" % i, [128, 4, 128], F32) for i in range(2)]
    c.obr = kb.tile("obr", [128, 4, 193], F32)
    c.szs = [kb.tile("sz%d" % i, [128, 512], F32) for i in range(3)]
    c.ogts = [kb.tile("ogt%d" % i, [128, 512], BF16) for i in range(2)]
    c.gates = kb.tile("gates", [128, NT, 24], F32)
    c.expand = kb.tile("expand", [64, NT, 128], BF16)
    c.seladd = kb.tile("seladd", [128, NT, 64], F32)
    c.ncb = kb.tile("ncb", [128, 8], F32)
    c.sm = kb.tile("sm", [128, 16], F32)
    c.imp = kb.tile("imp", [128, 64], F32)
    c.score = kb.tile("score", [128, 64], F32)
    c.work = kb.tile("work", [128, 64], F32)
    c.m8 = kb.tile("m8", [128, 16], F32)
    c.sel = kb.tile("sel", [128, 64], BF16)
    c.nselTs = [kb.tile("nselT%d" % i, [64, 512], BF16) for i in range(2)]
    c.w1b = [kb.tile("w1b%d" % i, [128, 32, 128], BF16) for i in range(2)]
    c.w2b = [kb.tile("w2b%d" % i, [128, 128], BF16) for i in range(2)]
    c.peTb = [kb.tile("peTb%d" % i, [128, 32], BF16) for i in range(2)]
    c.cbias = [kb.tile("cbias%d" % i, [128, 1], F32) for i in range(2)]
    c.kct = kb.tile("kct", [128, S], BF16)
    c.cu = [kb.tile("cu%d" % i, [128, 256], F32) for i in range(4)]
    c.GT = kb.tile("GT", [128, 256], BF16)
    c.ovb = kb.tile("ovb", [128, 2, 65], F32)

    kb.dma("sp", c.gates[:, :, :], scr["GATE"][:, :].rearrange("(j p) c -> p j c", p=128), [], [c.gates])
    kb.dma("sp", c.seladd[:, :, :], ins["seladd"][:, :, :], [], [c.seladd])
    kb.dma("sp", c.ncb[:, :], ins["cbb"][:, :], [], [c.ncb])
    kb.ts("dve", c.ncb, c.ncb[:, :], c.ncb, c.ncb[:, :], -1.0, None, ALU.mult)
    kb.dma("sp", c.ovb[:, :, :], ins["ov"][:, :, :], [], [c.ovb])
    kb.copy("dve", c.Vce, c.Vce[:, :, 128:193], c.ovb, c.ovb[:, :, :])
    kb.op("dve", lambda v: v.memset(c.VSe[:, :, 128:129], 1.0), [], [c.VSe])
    kb.op("dve", lambda v: v.memset(c.VWe[:, :, 128:129], 1.0), [], [c.VWe])
    kb.op("dve", lambda v: v.memset(c.GT[:, :], 0.0), [], [c.GT])
    with contextlib.ExitStack() as tes:
        old = kb.es
        kb.es = tes
        stg = kb.tile("stg_big", [128, 4096], F32)
        kb.dma("sp", stg[0:64, :], ins["expand"][:, :], [], [stg])
        kb.copy("dve", c.expand, c.expand[:, :, :].rearrange("p a b -> p (a b)"), stg, stg[0:64, :])
        for i, nm in enumerate(["k", "v"]):
            kb.dma("sp", stg[:, :].rearrange("p (j h) -> p j h", h=128),
                   ins["nsa_phi_%s_w1" % nm][:, :].rearrange("(j d) h -> d j h", d=128), [], [stg])
            kb.copy("dve", c.w1b[i], c.w1b[i][:, :, :].rearrange("p a b -> p (a b)"), stg, stg[:, :])
            kb.dma("sp", stg[:, 0:128], ins["nsa_phi_%s_w2" % nm][:, :], [], [stg])
            kb.copy("dve", c.w2b[i], c.w2b[i][:, :], stg, stg[:, 0:128])
            kb.dma("sp", stg[:, 0:32], ins["peT_%s" % nm][:, :], [], [stg])
            kb.copy("dve", c.peTb[i], c.peTb[i][:, :], stg, stg[:, 0:32])
            kb.dma("sp", stg[:, 0:1], ins["b1_%s" % nm][:, :], [], [stg])
            p = c.pm[0]
            for j in range(32):
                kb.mm(p, p[:, 0:1], c.w1b[i], c.w1b[i][:, j, :], c.peTb[i], c.peTb[i][:, j:j + 1], j == 0, j == 31)
            kb.tt("dve", c.cbias[i], c.cbias[i][:, :], p, p[:, 0:1], stg, stg[:, 0:1], ALU.add)
        kb.barrier()
        kb.es = old


def nsa_compress(kb, c, scr, g):
    for i, nm in enumerate(["KCT", "VCT"]):
        kb.dma("sp", c.kct[:, :], scr[nm][g, :, :], [], [c.kct])
        p = next_pacc(c)
        for j in range(32):
            kb.mm(p, p[:, 0:255], c.w1b[i], c.w1b[i][:, j, :], c.kct, c.kct[:, j:j + 4065:16], j == 0, j == 31)
        u, u2, inner, th = c.cu
        kb.act(u, u[:, 0:255], p, p[:, 0:255], AF.Identity, extra_reads=[c.cbias[i]], bias=c.cbias[i][:, 0:1], scale=1.0)
        kb.tt("dve", u2, u2[:, 0:255], u, u[:, 0:255], u, u[:, 0:255], ALU.mult)
        kb.ts("dve", u2, u2[:, 0:255], u2, u2[:, 0:255], 0.044715, 1.0, ALU.mult, ALU.add)
        kb.tt("dve", inner, inner[:, 0:255], u2, u2[:, 0:255], u, u[:, 0:255], ALU.mult)
        kb.act(th, th[:, 0:255], inner, inner[:, 0:255], AF.Tanh, scale=0.7978845608028654)
        kb.stt(inner, inner[:, 0:255], th, th[:, 0:255], 1.0, u, u[:, 0:255], ALU.add, ALU.mult)
        kb.op("act", lambda a: a.mul(out=c.GT[:, 0:255], in_=inner[:, 0:255], mul=0.5), [inner], [c.GT])
        if i == 0:
            p2 = next_pacc(c)
            kb.mm(p2, p2[:, 0:256], c.w2b[0], c.w2b[0][:, :], c.GT, c.GT[:, :], True, True)
            kb.copy("act", c.KcT, c.KcT[:, :], p2, p2[:, 0:256])
        else:
            for jm in range(2):
                p2 = next_pacc(c)
                kb.mm(p2, p2[:, 0:128], c.GT, c.GT[:, jm * 128:(jm + 1) * 128], c.w2b[1], c.w2b[1][:, :], True, True)
                kb.copy("act", c.Vce, c.Vce[:, jm, 0:128], p2, p2[:, 0:128])


def bias_exp(kb, c, ps, table, g, PT, shifted=False):
    sb = rot(c, "sbs", 2)
    if shifted:
        kb.tt("dve", sb, sb[:, :], ps, ps[:, :], table, table[:, :], ALU.add)
    else:
        for cc in range(4):
            h = 4 * g + cc
            kb.stt(sb, sb[:, cc * 128:(cc + 1) * 128], ps, ps[:, cc * 128:(cc + 1) * 128], c.ncb[:, h:h + 1],
                   table, table[:, cc * 128:(cc + 1) * 128], ALU.add, ALU.add, extra_reads=[c.ncb])
    kb.act(PT, PT[:, :], sb, sb[:, :], AF.Exp)


def branch_tail(kb, c, g, ti, br, width, first):
    ob = c.obr
    acc = c.accs[ti % 2]
    for cc in range(4):
        kb.copy("act", ob, ob[:, cc, 0:width], c.pm[cc], c.pm[cc][:, 0:width])
    zc = width - 1
    kb.ts("dve", c.sm, c.sm[:, 0:4], ob, ob[:, :, zc], 1e-30, None, ALU.max)
    kb.op("dve", lambda v: v.reciprocal(out=c.sm[:, 4:8], in_=c.sm[:, 0:4]), [c.sm], [c.sm])
    kb.tt("dve", c.sm, c.sm[:, 8:12], c.sm, c.sm[:, 4:8], c.gates, c.gates[:, ti, 12 * g + br:12 * g + 12:3], ALU.mult)
    if br == 0:
        kb.ts("dve", c.imp, c.imp[:, :], ob, ob[:, 0, 128:192], c.sm[:, 4:5], None, ALU.mult, extra_reads=[c.sm])
        for cc in range(1, 4):
            kb.stt(c.imp, c.imp[:, :], ob, ob[:, cc, 128:192], c.sm[:, 4 + cc:5 + cc], c.imp, c.imp[:, :],
                   ALU.mult, ALU.add, extra_reads=[c.sm])
    for cc in range(4):
        if first:
            kb.ts("dve", acc, acc[:, cc, :], ob, ob[:, cc, 0:128], c.sm[:, 8 + cc:9 + cc], None, ALU.mult, extra_reads=[c.sm])
        else:
            kb.stt(acc, acc[:, cc, :], ob, ob[:, cc, 0:128], c.sm[:, 8 + cc:9 + cc], acc, acc[:, cc, :],
                   ALU.mult, ALU.add, extra_reads=[c.sm])


def topk_select(kb, c, ti):
    kb.tt("dve", c.score, c.score[:, :], c.imp, c.imp[:, :], c.seladd, c.seladd[:, ti, :], ALU.add)
    kb.op("dve", lambda v: v.max(out=c.m8[:, 0:8], in_=c.score[:, :]), [c.score], [c.m8])
    kb.op("dve", lambda v: v.match_replace(out=c.work[:, :], in_to_replace=c.m8[:, 0:8], in_values=c.score[:, :], imm_value=-3.0e38),
          [c.m8, c.score], [c.work])
    kb.op("dve", lambda v: v.max(out=c.m8[:, 8:16], in_=c.work[:, :]), [c.work], [c.m8])
    kb.ts("dve", c.sel, c.sel[:, :], c.score, c.score[:, :], c.m8[:, 15:16], None, ALU.is_ge, extra_reads=[c.m8])
    if ti > 0:
        pt = c.ptr[0]
        for r in range(4):
            kb.tr(pt, pt[0:64, r * 128:(r + 1) * 128], c.sel, c.sel[:, :], c.ident)
        nselT = c.nselTs[ti % 2]
        kb.ts("dve", nselT, nselT[:, :], pt, pt[0:64, 0:512], -1.0, None, ALU.add)


def nsa_attn_group(kb, c, ins, scr, g, nt=NT):
    kb.dma("sp", c.KST[:, :], scr["KST"][g, :, :], [], [c.KST])
    kb.dma("sp", c.KWT[:, :], scr["KWT"][g, :, :], [], [c.KWT])
    kb.dma("sp", c.VSe[:, :, 0:128], scr["VS"][:, g * 128:(g + 1) * 128].rearrange("(j p) d -> p j d", p=128), [], [c.VSe])
    kb.dma("sp", c.VWe[:, :, 0:128], scr["VW"][:, g * 128:(g + 1) * 128].rearrange("(j p) d -> p j d", p=128), [], [c.VWe])
    for d in range(3):
        kb.dma("sp", c.BT[d][:, :], ins["BT"][d, g, :, :], [], [c.BT[d]])
        for cc in range(4):
            h = 4 * g + cc
            kb.ts("dve", c.BT[d], c.BT[d][:, cc * 128:(cc + 1) * 128], c.BT[d], c.BT[d][:, cc * 128:(cc + 1) * 128],
                  c.ncb[:, h:h + 1], None, ALU.add, extra_reads=[c.ncb])
    cmp_t, win_t, sel_t = {}, {}, {}

    def mk_task(lst, **kw):
        t = dict(pre=None, post=None, table=None, st={}, nselT=None)
        t.update(kw)
        lst.append(t)

    for ti in range(nt):
        st = {}
        cmp_t[ti], win_t[ti], sel_t[ti] = [], [], []

        def pre_tile(ti=ti, st=st):
            q4 = rot(c, "q4s", 3)
            kb.dma("sp", q4[:, :, :], scr["QT"][4 * g:4 * g + 4, :, ti * 128:(ti + 1) * 128].rearrange("h p t -> p h t"), [], [q4])
            sz = rot(c, "szs", 3)
            kb.dma("sp", sz[:, :], scr["SZ"][ti * 128:(ti + 1) * 128, g * 512:(g + 1) * 512], [], [sz])
            st["q4"] = q4
            st["sz"] = sz

        jms = [0] + ([1] if ti >= 16 else [])

        def post_cmp(ti=ti):
            branch_tail(kb, c, g, ti, 0, 193, True)
            topk_select(kb, c, ti)

        for idx, jm in enumerate(jms):
            near = ti < 17 + 16 * jm
            mk_task(cmp_t[ti], st=st, kT=(c.KcT, c.KcT[:, jm * 128:(jm + 1) * 128]), maskj=None,
                    table=("FULL", 128 * jm - 8 * ti + 8 + 240) if near else None,
                    v=(c.Vce, c.Vce[:, jm, :]), width=193, start=(idx == 0), stop=(idx == len(jms) - 1),
                    pre=pre_tile if idx == 0 else None,
                    post=post_cmp if idx == len(jms) - 1 else None)
        j0 = max(0, ti - 4)
        for j in range(j0, ti + 1):
            dl = ti - j
            tb = c.BT[dl] if dl <= 1 else (c.BT[2] if dl == 4 else None)
            mk_task(win_t[ti], st=st, kT=(c.KWT, c.KWT[:, j * 128:(j + 1) * 128]), maskj=None, table=tb,
                    v=(c.VWe, c.VWe[:, j, :]), width=129, start=(j == j0), stop=(j == ti),
                    post=(lambda ti=ti: branch_tail(kb, c, g, ti, 2, 129, False)) if j == ti else None)
        for j in range(ti + 1):
            dl = ti - j
            tb = c.BT[dl] if dl <= 1 else None

            def post_sel(ti=ti, st=st):
                branch_tail(kb, c, g, ti, 1, 129, False)
                acc = c.accs[ti % 2]
                ogt = rot(c, "ogts", 2)
                kb.tt("dve", ogt, ogt[:, :], acc, acc[:, :, :].rearrange("p a b -> p (a b)"), st["sz"], st["sz"][:, :], ALU.mult)
                sti = ti // 4
                kb.dma("pool", scr["OG1src"][sti].ap()[(ti % 4) * 128:(ti % 4 + 1) * 128, g * 512:(g + 1) * 512], ogt[:, :],
                       [ogt], [scr["OG1src_b"][sti]])
                if g == NGC - 1 and ti % 4 == 3:
                    kb.coll(scr["OG1src"][sti], scr["OG1g"][sti], [scr["OG1src_b"][sti]], [scr["OG1g_b"][sti]])

            mk_task(sel_t[ti], st=st, kT=(c.KST, c.KST[:, j * 128:(j + 1) * 128]), maskj=(j if j < ti else None), table=tb,
                    v=(c.VSe, c.VSe[:, j, :]), width=129, start=(j == 0), stop=(j == ti), nselT=c.nselTs[ti % 2],
                    post=post_sel if j == ti else None)

    tasks = list(cmp_t[0])
    for ti in range(nt):
        tasks += win_t[ti]
        if ti + 1 < nt:
            tasks += cmp_t[ti + 1]
        tasks += sel_t[ti]
    pos = {id(t): i for i, t in enumerate(tasks)}
    for ti in range(nt):
        sel_t[ti][0]["need_back"] = pos[id(cmp_t[ti][-1])]

    def emit_front(t):
        if t["pre"]:
            t["pre"]()
        q4 = t["st"]["q4"]
        q4f = q4[:, :, :].rearrange("p a b -> p (a b)")
        ps = next_pacc(c)
        kb.mm(ps, ps[:, :], t["kT"][0], t["kT"][1], q4, q4f, True, t["maskj"] is None)
        if t["maskj"] is not None:
            kb.mm(ps, ps[:, :], c.expand, c.expand[:, t["maskj"], :], t["nselT"], t["nselT"][:, :], False, True)
        PT = rot(c, "PTs", 11)
        tb = t["table"]
        if tb is None:
            kb.act(PT, PT[:, :], ps, ps[:, :], AF.Exp)
        elif isinstance(tb, tuple):
            BC = rot(c, "BCs", 2)
            kb.dma("sp", BC[:, :], ins["FULL"][g, tb[1]:tb[1] + 128, :], [], [BC])
            bias_exp(kb, c, ps, BC, g, PT)
        else:
            bias_exp(kb, c, ps, tb, g, PT, shifted=True)
        t["PT"] = PT

    def emit_back(t):
        PT = t["PT"]
        w = t["width"]
        for cc in range(4):
            kb.mm(c.pm[cc], c.pm[cc][:, 0:w], PT, PT[:, cc * 128:(cc + 1) * 128], t["v"][0], t["v"][1], t["start"], t["stop"])
        if t["post"]:
            t["post"]()

    n = len(tasks)
    DEPTH = 8
    nb = 0
    for i in range(n):
        need = max(i - DEPTH, tasks[i].get("need_back", -1))
        while nb <= need:
            emit_back(tasks[nb])
            nb += 1
        emit_front(tasks[i])
    while nb < n:
        emit_back(tasks[nb])
        nb += 1


def nsa_out_phase(kb, c, ins, scr, out, y_tiles, nst=NST):
    c.x2 = [kb.tile("x2_%d" % i, [128, 4, HW], F32) for i in range(2)]
    c.ssb = [kb.tile("ssb%d" % i, [128, 4], F32) for i in range(2)]
    c.ssg = kb.tile("ssg", [128, 2, 4], F32)
    c.rs = kb.tile("rs", [128, 12], F32)
    c.sqh = kb.tile("sqh", [128, HW], BF16)

    def finalize(st):
        x2 = c.x2[st % 2]
        kb.dma("sp", c.ssg[:, :, :], scr["SSg"][st].ap().rearrange("(r p) j -> p r j", p=128), [scr["SSg_b"][st]], [c.ssg])
        kb.tt("dve", c.rs, c.rs[:, 0:4], c.ssg, c.ssg[:, 0, :], c.ssg, c.ssg[:, 1, :], ALU.add)
        kb.op("act", lambda a: a.activation(out=c.rs[:, 4:8], in_=c.rs[:, 0:4], func=AF.Sqrt, bias=c.epsb[:, 0:1], scale=1.0 / D),
              [c.rs, c.epsb], [c.rs])
        kb.op("dve", lambda v: v.reciprocal(out=c.rs[:, 8:12], in_=c.rs[:, 4:8]), [c.rs], [c.rs])
        for j in range(4):
            ti = st * 4 + j
            kb.stt(c.xt, c.xt[:, 0:HW], x2, x2[:, j, :], c.rs[:, 8 + j:9 + j], c.gb, c.gb[:, 0:HW], ALU.mult, ALU.mult, extra_reads=[c.rs])
            kb.dma("pool", out[ti * 128:(ti + 1) * 128, :], c.xt[:, 0:HW], [c.xt], [y_tiles[ti]])

    for st in range(nst):
        x2 = c.x2[st % 2]
        ssb = c.ssb[st % 2]
        load_gathered_og(kb, c, scr["OG1g"][st], scr["OG1g_b"][st], c.hT)
        for nb in range(2):
            wb = load_w_block(kb, c, "nsa_w_out", nb)
            for j in range(4):
                ti = st * 4 + j
                p = proj_token_major(kb, c, wb, j)
                xs = c.xs[c.xs_i]
                c.xs_i ^= 1
                kb.dma("sp", xs[:, :], scr["X1own"][ti * 128:(ti + 1) * 128, nb * 512:(nb + 1) * 512], [], [xs])
                kb.tt("dve", x2, x2[:, j, nb * 512:(nb + 1) * 512], p, p[:, :], xs, xs[:, :], ALU.add)
        for j in range(4):
            kb.op("act", lambda a, j=j: a.activation(out=c.sqh[:, :], in_=x2[:, j, :], func=AF.Square, accum_out=ssb[:, j:j + 1]),
                  [x2], [c.sqh, ssb])
        kb.dma("pool", scr["SSsrc"][st].ap()[:, :], ssb[:, :], [ssb, c.sqh], [scr["SSsrc_b"][st]])
        kb.coll(scr["SSsrc"][st], scr["SSg"][st], [scr["SSsrc_b"][st]], [scr["SSg_b"][st]])
        if st >= 1:
            finalize(st - 1)
    finalize(nst - 1)


def t5_bucket_np(dist):
    import math
    n = np.maximum(dist, 0)
    me = 16
    large = me + (np.log(np.maximum(n, 1).astype(np.float32) / np.float32(me)) / np.float32(math.log(128 / me))
                  * np.float32(32 - me)).astype(np.int32)
    large = np.minimum(large, 31)
    return np.where(n < me, n, large)


def host_consts():
    k = {}
    ss = np.arange(128)[:, None]
    tt = np.arange(128)[None, :]
    tri = (ss <= tt).astype(np.float32)
    k["M4"] = np.ascontiguousarray(np.broadcast_to(tri[:, None, :], (128, 4, 128))).astype(np.float32)
    rm = np.ones((128, 512), np.float32)
    rm[:, 0::128] = 0.0
    k["rmask"] = rm
    k["identf"] = np.eye(128, dtype=np.float32)
    k["epsb"] = np.full((128, 1), EPS, np.float32)
    t = np.arange(S)[:, None]
    n = np.arange(64)[None, :]
    cur = t // 64
    forced = (n == 0) | (n == cur) | (n == cur - 1)
    visible = n * 64 <= t
    sa = np.where(forced, 1e9, np.where(visible, 0.0, -1e9)).astype(np.float32)
    k["seladd"] = np.ascontiguousarray(sa.reshape(NT, 128, 64).transpose(1, 0, 2))
    ex = np.zeros((64, NT, 128), np.float32)
    for j in range(NT):
        ex[2 * j, j, 0:64] = BIG
        ex[2 * j + 1, j, 64:128] = BIG
    k["expand"] = ex.reshape(64, NT * 128)
    m = np.arange(256)[:, None]
    ov = ((16 * m < 64 * (n + 1)) & (16 * m + 32 > 64 * n) & (m < 255)).astype(np.float32)
    ovx = np.concatenate([ov, np.ones((256, 1), np.float32)], axis=1)
    k["ov"] = np.ascontiguousarray(ovx.reshape(2, 128, 65).transpose(1, 0, 2))
    dist0 = tt - ss
    k["bt_idx"] = [t5_bucket_np(dist0), t5_bucket_np(dist0 + 128), t5_bucket_np(dist0 + 512)]
    k["bt_valid"] = [dist0 >= 0, np.ones_like(dist0, bool), (dist0 + 512) <= 511]
    r = np.arange(504)[:, None] - 240
    distf = tt - 16 * (r - 8) - 31
    k["full_idx"] = t5_bucket_np(distf)
    k["full_valid"] = distf >= 0
    return k


_HC = None


def make_inputs(b, r, x, norm_gains, final_gain, rel_bias, hgrn_lb, hgrn_w_in, hgrn_head_gain, hgrn_w_out,
                nsa_w_in, nsa_pe_k, nsa_pe_v, nsa_phi_k_w1, nsa_phi_k_b1, nsa_phi_k_w2,
                nsa_phi_v_w1, nsa_phi_v_b1, nsa_phi_v_w2, nsa_w_out):
    global _HC
    if _HC is None:
        _HC = host_consts()
    k = _HC
    f = np.float32
    cs = slice(r * HW, (r + 1) * HW)
    m = {}
    m["x"] = np.ascontiguousarray(x[b])
    m["xh"] = np.ascontiguousarray(x[b][:, cs])
    m["g0b"] = np.ascontiguousarray(np.broadcast_to(norm_gains[0][None, :], (128, D))).astype(f)
    m["g1b"] = np.ascontiguousarray(np.broadcast_to(norm_gains[1][None, :], (128, D))).astype(f)
    m["gfb"] = np.ascontiguousarray(np.broadcast_to(np.tile(final_gain[cs], 2)[None, :], (128, D))).astype(f)
    m["lbT"] = np.ascontiguousarray(hgrn_lb.reshape(3, 16, 128)[:, 8 * r:8 * r + 8, :].transpose(2, 0, 1)).astype(f)
    m["G4"] = np.ascontiguousarray(np.broadcast_to(np.tile(hgrn_head_gain[0], 4)[None, :], (128, 512))).astype(f)
    for nm in ["M4", "rmask", "identf", "epsb", "seladd", "expand", "ov"]:
        m[nm] = k[nm]
    m["hgrn_w_in"] = np.ascontiguousarray(hgrn_w_in[0].reshape(D, 4, 16, 128)[:, :, 8 * r:8 * r + 8, :].reshape(D, 4 * HW))
    m["hgrn_w_out"] = np.ascontiguousarray(hgrn_w_out[0][:, cs])
    W = nsa_w_in[0]
    h256 = lambda base: W[:, base + 256 * r:base + 256 * r + 256]
    m["nsa_w_in"] = np.ascontiguousarray(np.concatenate(
        [W[:, cs], h256(2048), h256(2560), h256(3072), h256(4096), h256(3584), h256(4608),
         W[:, 5120 + 24 * r:5120 + 24 * r + 24], W[:, 5168 + HW * r:5168 + HW * r + HW]], axis=1))
    m["nsa_w_out"] = np.ascontiguousarray(nsa_w_out[0][:, cs])
    m["nsa_phi_k_w1"] = np.ascontiguousarray(nsa_phi_k_w1[0])
    m["nsa_phi_v_w1"] = np.ascontiguousarray(nsa_phi_v_w1[0])
    m["nsa_phi_k_w2"] = np.ascontiguousarray(nsa_phi_k_w2[0])
    m["nsa_phi_v_w2"] = np.ascontiguousarray(nsa_phi_v_w2[0])
    m["peT_k"] = np.ascontiguousarray(nsa_pe_k[0].T)
    m["peT_v"] = np.ascontiguousarray(nsa_pe_v[0].T)
    m["b1_k"] = np.ascontiguousarray(nsa_phi_k_b1[0].reshape(128, 1))
    m["b1_v"] = np.ascontiguousarray(nsa_phi_v_b1[0].reshape(128, 1))
    tab = rel_bias.astype(f).reshape(32, 4, 4)[:, 2 * r:2 * r + 2, :]
    BT = np.empty((3, NGC, 128, 4, 128), f)
    for d in range(3):
        gth = tab[k["bt_idx"][d]]
        gth = np.where(k["bt_valid"][d][:, :, None, None], gth, f(NEG))
        BT[d] = gth.transpose(2, 0, 3, 1)
    m["BT"] = np.ascontiguousarray(BT.reshape(3, NGC, 128, 512))
    gth = tab[k["full_idx"]]
    gth = np.where(k["full_valid"][:, :, None, None], gth, f(NEG))
    m["FULL"] = np.ascontiguousarray(gth.transpose(2, 0, 3, 1).reshape(NGC, 504, 512)).astype(f)
    m["cbb"] = np.ascontiguousarray(np.broadcast_to(rel_bias[31][None, 8 * r:8 * r + 8], (128, 8))).astype(f)
    return m


INPUT_SHAPES = {
    "x": [S, D], "xh": [S, HW], "g0b": [128, D], "g1b": [128, D], "gfb": [128, D], "lbT": [128, 3, 4 * NGC], "G4": [128, 512],
    "M4": [128, 4, 128], "rmask": [128, 512], "identf": [128, 128], "epsb": [128, 1],
    "seladd": [128, NT, 64], "expand": [64, NT * 128], "ov": [128, 2, 65],
    "hgrn_w_in": [D, 4 * HW], "hgrn_w_out": [D, HW], "nsa_w_in": [D, 3608], "nsa_w_out": [D, HW],
    "nsa_phi_k_w1": [4096, 128], "nsa_phi_v_w1": [4096, 128], "nsa_phi_k_w2": [128, 128], "nsa_phi_v_w2": [128, 128],
    "peT_k": [128, 32], "peT_v": [128, 32], "b1_k": [128, 1], "b1_v": [128, 1],
    "BT": [3, NGC, 128, 512], "FULL": [NGC, 504, 512], "cbb": [128, 8],
}


def build():
    nc = bass.Bass("TRN2", target_bir_lowering=False)
    es = contextlib.ExitStack()
    kb = KB(nc, es)
    c = Ctx()
    ins = {}
    for name, shape in INPUT_SHAPES.items():
        ins[name] = kb.dram(name, shape, F32, kind="ExternalInput")
    out = kb.dram("out", [S, HW], F32, kind="ExternalOutput")
    scr = {}

    def chunks(name, rows, cols, dtype):
        scr[name + "src"] = [nc.dram_tensor("%ssrc%d" % (name, i), [rows, cols], dtype) for i in range(NST)]
        scr[name + "g"] = [nc.dram_tensor("%sg%d" % (name, i), [2 * rows, cols], dtype) for i in range(NST)]
        scr[name + "src_b"] = [Buf(None, "%ssrcb%d" % (name, i)) for i in range(NST)]
        scr[name + "g_b"] = [Buf(None, "%sgb%d" % (name, i)) for i in range(NST)]

    chunks("OG0", 512, HW, BF16)
    chunks("X1", 512, HW, F32)
    chunks("OG1", 512, HW, BF16)
    chunks("SS", 128, 4, F32)
    scr["X1own"] = kb.dram("X1own", [S, HW], F32)
    scr["QT"] = kb.dram("QT", [4 * NGC, 128, S], BF16)
    for nm in ["KCT", "VCT", "KST", "KWT"]:
        scr[nm] = kb.dram(nm, [NGC, 128, S], BF16)
    scr["VS"] = kb.dram("VS", [S, 128 * NGC], BF16)
    scr["VW"] = kb.dram("VW", [S, 128 * NGC], BF16)
    scr["GATE"] = kb.dram("GATE", [S, 12 * NGC], F32)
    scr["SZ"] = kb.dram("SZ", [S, HW], F32)

    c.ident = kb.tile("ident", [128, 128], BF16)
    c.epsb = kb.tile("epsb", [128, 1], F32)
    identf = kb.tile("identf", [128, 128], F32)
    kb.dma("sp", identf[:, :], ins["identf"][:, :], [], [identf])
    kb.copy("dve", c.ident, c.ident[:, :], identf, identf[:, :])
    kb.dma("sp", c.epsb[:, :], ins["epsb"][:, :], [], [c.epsb])

    def psum_banks(nacc, ntr):
        c.pacc = [kb.psum("pacc%d" % i, [128, 512], F32) for i in range(nacc)]
        c.pacc_i = 0
        c.ptr = [kb.psum("ptr%d" % i, [128, 1024], BF16) for i in range(ntr)]
        c.ptr_i = 0
        c.pm = [kb.psum("pm%d" % i, [128, 512], F32) for i in range(4)]

    dummy = [Buf(None, "d%d" % i) for i in range(NT)]
    y_tiles = [Buf(None, "y%d" % i) for i in range(NT)]
    x_fn = lambda ti: [(ins["x"][ti * 128:(ti + 1) * 128, :], 0, D, dummy[ti])]

    def x1_fn(ti):
        st, j = ti // 4, ti % 4
        return [(scr["X1g"][st].ap()[r * 512 + j * 128:r * 512 + (j + 1) * 128, :], r * HW, HW, scr["X1g_b"][st]) for r in range(2)]

    def proj_tiles(gain_name):
        psum_banks(2, 2)
        setup_common(kb, c)
        kb.dma("sp", c.gb[:, :], ins[gain_name][:, :], [], [c.gb])

    setup_wconv(kb, c, ins)
    issue_conv(kb, c, 10)
    with contextlib.ExitStack() as pes:
        kb.es = pes
        proj_tiles("g0b")
        setup_hgrn(kb, c, ins)
        hgrn_layer(kb, c, ins, scr, dummy, x_fn)
        issue_conv(kb, c, 1000)
        kb.barrier()
    kb.es = es
    with contextlib.ExitStack() as pes:
        kb.es = pes
        proj_tiles("g1b")
        nsa_proj_phase(kb, c, ins, scr, dummy, x1_fn)
        kb.barrier()
    kb.es = es
    with contextlib.ExitStack() as pes:
        kb.es = pes
        psum_banks(3, 1)
        setup_attn(kb, c, ins, scr)
        for g in range(NGC):
            nsa_compress(kb, c, scr, g)
            nsa_attn_group(kb, c, ins, scr, g)
        kb.barrier()
    kb.es = es
    with contextlib.ExitStack() as pes:
        kb.es = pes
        proj_tiles("gfb")
        nsa_out_phase(kb, c, ins, scr, out, y_tiles)
        kb.barrier()
    kb.es = es
    kb.barrier()
    es.close()
    print("n_inst", kb.n_inst)
    return nc


_NC = None


def kernel(**inputs):
    global _NC
    inputs = {k: np.asarray(v) for k, v in inputs.items()}
    if _NC is None:
        _NC = build()
    in_maps = [make_inputs(core // 2, core % 2, **inputs) for core in range(8)]
    res = run_bass_kernel_spmd(_NC, in_maps, core_ids=list(range(8)))
    out = np.empty((4, S, D), np.float32)
    for core in range(8):
        b, r = core // 2, core % 2
        out[b, :, r * HW:(r + 1) * HW] = np.asarray(res.results[core]["out"])
    return out
```

```python
import contextlib
import numpy as np
import concourse.bass as bass
import concourse.mybir as mybir
from concourse.bass_utils import run_bass_kernel_spmd

F32 = mybir.dt.float32
BF16 = mybir.dt.bfloat16
AF = mybir.ActivationFunctionType
ALU = mybir.AluOpType
AX = mybir.AxisListType

S = 4096
D = 2048
NT = S // 128
NST = S // 512
EPS = 1e-6
NEG = -30000.0
HW = D // 2
NGC = 2
PAIRS = [[0, 1], [2, 3], [4, 5], [6, 7]]


class Buf:
    __slots__ = ("t", "w", "r", "name")

    def __init__(self, t, name=""):
        self.t = t
        self.w = None
        self.r = {}
        self.name = name

    def __getitem__(self, idx):
        return self.t[idx]


class KB:
    def __init__(self, nc, es, nd=8):
        self.nc = nc
        self.es = es
        self.eng = {"pe": nc.tensor, "act": nc.scalar, "dve": nc.vector, "pool": nc.gpsimd, "sp": nc.sync}
        self.sems = {}
        self.cnt = {}
        for e in ["pe", "act", "dve", "pool"]:
            self.sems[e] = es.enter_context(nc.semaphore("s_" + e))
            self.cnt[e] = 0
        self.waited = {e: {} for e in self.eng}
        self.nd = nd
        self.dq = {}
        for q in ["sp", "pool"]:
            lst = []
            for i in range(nd):
                key = "d_%s%d" % (q, i)
                self.sems[key] = es.enter_context(nc.semaphore(key))
                lst.append([key, 0])
            self.dq[q] = [lst, 0]
        self.n_inst = 0
        self.cq = []
        for i in range(4):
            key = "cc_%d" % i
            self.sems[key] = es.enter_context(nc.semaphore(key))
            self.cq.append([key, 0])
        self.cq_i = 0

    def coll(self, src_t, dst_t, reads, writes):
        slot = self.cq[self.cq_i]
        self.cq_i = (self.cq_i + 1) % len(self.cq)
        deps = self._deps(reads, writes)
        if slot[1] > 0 and deps.get(slot[0], 0) < slot[1]:
            deps[slot[0]] = slot[1]
        self._wait("pool", deps)
        inst = self.nc.gpsimd.collective_compute("AllGather", ALU.bypass, replica_groups=PAIRS,
                                                 ins=[src_t.ap().opt()], outs=[dst_t.ap().opt()])
        slot[1] += 1
        inst.then_inc(self.sems[slot[0]], 1)
        self._mark((slot[0], slot[1]), reads, writes)
        self.n_inst += 1

    def tile(self, name, shape, dtype):
        self.n_tiles = getattr(self, "n_tiles", 0) + 1
        t = self.es.enter_context(self.nc.sbuf_tensor("sb%d_%s" % (self.n_tiles, name), list(shape), dtype))
        return Buf(t, name)

    def psum(self, name, shape, dtype):
        self.n_tiles = getattr(self, "n_tiles", 0) + 1
        t = self.es.enter_context(self.nc.psum_tensor("ps%d_%s" % (self.n_tiles, name), list(shape), dtype))
        return Buf(t, name)

    def dram(self, name, shape, dtype, kind="Internal"):
        t = self.nc.dram_tensor(name, list(shape), dtype, kind=kind)
        return Buf(t.ap(), name)

    def _deps(self, reads, writes):
        deps = {}

        def add(ev):
            if ev is None:
                return
            k, v = ev
            if deps.get(k, 0) < v:
                deps[k] = v

        for b in reads:
            add(b.w)
        for b in writes:
            add(b.w)
            for k, v in b.r.items():
                add((k, v))
        return deps

    def _wait(self, e, deps):
        w = self.waited[e]
        eo = self.eng[e]
        for k, v in deps.items():
            if e == "pe" and k == "pe":
                continue
            if w.get(k, 0) < v:
                eo.wait_ge(self.sems[k], v)
                w[k] = v

    def _mark(self, ev, reads, writes):
        k, v = ev
        for b in reads:
            if b.r.get(k, 0) < v:
                b.r[k] = v
        for b in writes:
            b.w = ev
            b.r = {}

    def op(self, e, fn, reads, writes):
        self._wait(e, self._deps(reads, writes))
        inst = fn(self.eng[e])
        self.cnt[e] += 1
        inst.then_inc(self.sems[e], 1)
        self._mark((e, self.cnt[e]), reads, writes)
        self.n_inst += 1

    def dma(self, q, out, in_, reads, writes, **kw):
        lst, nxt = self.dq[q]
        slot = lst[nxt]
        self.dq[q][1] = (nxt + 1) % self.nd
        deps = self._deps(reads, writes)
        if slot[1] > 0:
            if deps.get(slot[0], 0) < slot[1]:
                deps[slot[0]] = slot[1]
        self._wait(q, deps)
        inst = self.eng[q].dma_start(out=out, in_=in_, **kw)
        slot[1] += 16
        inst.then_inc(self.sems[slot[0]], 16)
        self._mark((slot[0], slot[1]), reads, writes)
        self.n_inst += 1

    def barrier(self):
        deps = {}
        for e in ["pe", "act", "dve", "pool"]:
            if self.cnt[e] > 0:
                deps[e] = self.cnt[e]
        for q in self.dq:
            for key, val in self.dq[q][0]:
                if val > 0:
                    deps[key] = val
        for key, val in self.cq:
            if val > 0:
                deps[key] = val
        for e in self.eng:
            d = dict(deps)
            if e in d:
                del d[e]
            w = self.waited[e]
            for k, v in d.items():
                if w.get(k, 0) < v:
                    self.eng[e].wait_ge(self.sems[k], v)
                    w[k] = v

    def wait_all(self, e, bufs):
        deps = {}
        for b in bufs:
            if b.w is not None:
                k, v = b.w
                if deps.get(k, 0) < v:
                    deps[k] = v
        self._wait(e, deps)

    def mm(self, out_b, out_ap, l_b, l_ap, r_b, r_ap, start, stop, **kw):
        self.op("pe", lambda pe: pe.matmul(out_ap, l_ap, r_ap, start=start, stop=stop, **kw), [l_b, r_b], [out_b])

    def tr(self, out_b, out_ap, in_b, in_ap, ident):
        self.op("pe", lambda pe: pe.transpose(out_ap, in_ap, ident[:]), [in_b, ident], [out_b])

    def act(self, out_b, out_ap, in_b, in_ap, func, extra_reads=(), **kw):
        self.op("act", lambda a: a.activation(out=out_ap, in_=in_ap, func=func, **kw), [in_b] + list(extra_reads), [out_b])

    def tt(self, e, out_b, out_ap, a_b, a_ap, b_b, b_ap, op):
        self.op(e, lambda v: v.tensor_tensor(out=out_ap, in0=a_ap, in1=b_ap, op=op), [a_b, b_b], [out_b])

    def ts(self, e, out_b, out_ap, a_b, a_ap, s1, s2, op0, op1=None, extra_reads=()):
        if op1 is None:
            fn = lambda v: v.tensor_scalar(out=out_ap, in0=a_ap, scalar1=s1, scalar2=None, op0=op0)
        else:
            fn = lambda v: v.tensor_scalar(out=out_ap, in0=a_ap, scalar1=s1, scalar2=s2, op0=op0, op1=op1)
        self.op(e, fn, [a_b] + list(extra_reads), [out_b])

    def stt(self, out_b, out_ap, a_b, a_ap, scalar, b_b, b_ap, op0, op1, extra_reads=()):
        self.op("dve", lambda v: v.scalar_tensor_tensor(out=out_ap, in0=a_ap, scalar=scalar, in1=b_ap, op0=op0, op1=op1),
                [a_b, b_b] + list(extra_reads), [out_b])

    def copy(self, e, out_b, out_ap, in_b, in_ap):
        if e == "act":
            self.op("act", lambda a: a.copy(out=out_ap, in_=in_ap), [in_b], [out_b])
        else:
            self.op(e, lambda v: v.tensor_copy(out=out_ap, in_=in_ap), [in_b], [out_b])


class Ctx:
    pass


def setup_common(kb, c):
    c.xt = kb.tile("xt", [128, D], F32)
    c.hb = kb.tile("hb", [128, D], BF16)
    c.st1 = kb.tile("st1", [128, 8], F32)
    c.hT = kb.tile("hT", [128, 16, 512], BF16)
    c.ogT = kb.tile("ogT", [128, 16, 512], BF16)
    c.wb = [kb.tile("wb%d" % i, [128, 16, 512], BF16) for i in range(2)]
    c.wb_i = 0
    c.gb = kb.tile("gb", [128, D], F32)
    c.og = kb.tile("og", [128, 4, HW], BF16)
    c.ogf = kb.tile("ogf", [128, 4, D], BF16)
    c.xs = [kb.tile("xs%d" % i, [128, 512], F32) for i in range(2)]
    c.xo = [kb.tile("xo%d" % i, [128, 512], F32) for i in range(2)]
    c.xs_i = 0


def next_pacc(c):
    p = c.pacc[c.pacc_i]
    c.pacc_i = (c.pacc_i + 1) % len(c.pacc)
    return p


def next_ptr(c):
    p = c.ptr[c.ptr_i]
    c.ptr_i = (c.ptr_i + 1) % len(c.ptr)
    return p


def load_norm_transpose(kb, c, x_tiles, x_ap_fn, st, gain_b):
    for j in range(4):
        ti = st * 4 + j
        for (ap, col0, ncols, rb) in x_ap_fn(ti):
            kb.dma("sp", c.xt[:, col0:col0 + ncols], ap, [rb], [c.xt])
        kb.act(c.hb, c.hb[:, :], c.xt, c.xt[:, :], AF.Square, extra_reads=[], accum_out=c.st1[:, 0:1])
        kb.op("act", lambda a: a.activation(out=c.st1[:, 1:2], in_=c.st1[:, 0:1], func=AF.Sqrt, bias=c.epsb[:, 0:1], scale=1.0 / D),
              [c.hb, c.st1, c.epsb], [c.st1])
        kb.op("dve", lambda v: v.reciprocal(out=c.st1[:, 2:3], in_=c.st1[:, 1:2]), [c.st1], [c.st1])
        kb.stt(c.hb, c.hb[:, :], c.xt, c.xt[:, :], c.st1[:, 2:3], gain_b, gain_b[:, :], ALU.mult, ALU.mult, extra_reads=[c.st1])
        transpose_16(kb, c, c.hb, lambda kc: c.hb[:, kc * 128:(kc + 1) * 128], c.hT, j)


def transpose_16(kb, c, src_b, src_fn, dst_b, j):
    for half in range(2):
        p = next_ptr(c)
        for i in range(8):
            kc = half * 8 + i
            kb.tr(p, p[:, i * 128:(i + 1) * 128], src_b, src_fn(kc), c.ident)
        eng = "act" if half == 0 else "dve"
        kb.copy(eng, dst_b, dst_b[:, half * 8:(half + 1) * 8, j * 128:(j + 1) * 128],
                p, p[:, :].rearrange("p (a b) -> p a b", b=128))


WBLOCKS = {
    "hgrn_w_in": [(i * 512, 512) for i in range(8)],
    "hgrn_w_out": [(i * 512, 512) for i in range(2)],
    "nsa_w_in": [(0, 512), (512, 512), (1024, 512), (1536, 512), (2048, 512), (2560, 24), (2584, 512), (3096, 512)],
    "nsa_w_out": [(i * 512, 512) for i in range(2)],
}


def setup_wconv(kb, c, ins):
    c.WB = {}
    c.WBbuf = {}
    c.conv_q = []
    for wn, blks in WBLOCKS.items():
        c.WB[wn] = kb.dram("WB_" + wn, [len(blks), 128, 16, 512], BF16)
        c.WBbuf[wn] = [Buf(None, "%s_%d" % (wn, i)) for i in range(len(blks))]
        w3 = ins[wn][:, :].rearrange("(kc p) n -> p kc n", p=128)
        order = list(range(len(blks)))
        if wn == "hgrn_w_in":
            order = [sec * NGC + g for g in range(NGC) for sec in range(4)]
        for i in order:
            col0, ncols = blks[i]
            c.conv_q.append((wn, i, w3[:, :, col0:col0 + ncols], ncols))


def issue_conv(kb, c, n):
    for _ in range(n):
        if not c.conv_q:
            return
        wn, i, src, ncols = c.conv_q.pop(0)
        kb.dma("pool", c.WB[wn][i, :, :, 0:ncols], src, [], [c.WBbuf[wn][i]])


def load_w_block(kb, c, wn, i):
    ncols = WBLOCKS[wn][i][1]
    wb = c.wb[c.wb_i]
    c.wb_i = (c.wb_i + 1) % len(c.wb)
    kb.dma("sp", wb[:, :, 0:ncols], c.WB[wn][i, :, :, 0:ncols], [c.WBbuf[wn][i]], [wb])
    return wb


def proj_feature_major(kb, c, wb, cc, ncols_tok=512):
    p = next_pacc(c)
    for kc in range(16):
        kb.mm(p, p[:, 0:ncols_tok], wb, wb[:, kc, cc * 128:(cc + 1) * 128], c.hT, c.hT[:, kc, 0:ncols_tok], kc == 0, kc == 15)
    return p


def proj_token_major(kb, c, wb, j, ncols=512, aT=None):
    aT = aT or c.hT
    p = next_pacc(c)
    for kc in range(16):
        kb.mm(p, p[:, 0:ncols], aT, aT[:, kc, j * 128:(j + 1) * 128], wb, wb[:, kc, 0:ncols], kc == 0, kc == 15)
    return p


def load_gathered_og(kb, c, gath, st_buf, dstT):
    for j in range(4):
        for r in range(2):
            kb.dma("sp", c.ogf[:, j, r * HW:(r + 1) * HW], gath.ap()[r * 512 + j * 128:r * 512 + (j + 1) * 128, :], [st_buf], [c.ogf])
    for j in range(4):
        transpose_16(kb, c, c.ogf, lambda kc, j=j: c.ogf[:, j, kc * 128:(kc + 1) * 128], dstT, j)


def hgrn_out(kb, c, ins, scr, st):
    load_gathered_og(kb, c, scr["OG0g"][st], scr["OG0g_b"][st], c.ogT)
    for nb in range(2):
        wb = load_w_block(kb, c, "hgrn_w_out", nb)
        for j in range(4):
            ti = st * 4 + j
            p = proj_token_major(kb, c, wb, j, aT=c.ogT)
            xs = c.xs[c.xs_i]
            xo = c.xo[c.xs_i]
            c.xs_i ^= 1
            kb.dma("sp", xs[:, :], ins["xh"][ti * 128:(ti + 1) * 128, nb * 512:(nb + 1) * 512], [], [xs])
            kb.tt("dve", xo, xo[:, :], p, p[:, :], xs, xs[:, :], ALU.add)
            kb.dma("pool", scr["X1own"][ti * 128:(ti + 1) * 128, nb * 512:(nb + 1) * 512], xo[:, :], [xo], [])
            kb.dma("pool", scr["X1src"][st].ap()[j * 128:(j + 1) * 128, nb * 512:(nb + 1) * 512], xo[:, :], [xo], [scr["X1src_b"][st]])
    kb.coll(scr["X1src"][st], scr["X1g"][st], [scr["X1src_b"][st]], [scr["X1g_b"][st]])


def setup_hgrn(kb, c, ins):
    c.lbT = kb.tile("lbT", [128, 3, 4 * NGC], F32)
    c.lbw = kb.tile("lbw", [128, 6, 4 * NGC], F32)
    c.G4 = kb.tile("G4", [128, 512], F32)
    c.M4 = kb.tile("M4", [128, 4, 128], F32)
    c.rmask = kb.tile("rmask", [128, 512], F32)
    c.qs = kb.tile("qs", [128, 4, 512], F32)
    c.tmp = [kb.tile("htmp%d" % i, [128, 512], F32) for i in range(9)]
    c.qin = kb.tile("qin", [128, 4, 512], BF16)
    c.qmid = kb.tile("qmid", [128, 4, 512], BF16)
    c.kmid = kb.tile("kmid", [128, 4, 512], BF16)
    c.kend = kb.tile("kend", [128, 4, 512], BF16)
    c.kendT = kb.tile("kendT", [128, 4, 512], BF16)
    c.Vg = kb.tile("Vg", [128, 4, 512], BF16)
    c.gz = kb.tile("gz", [128, 4, 512], F32)
    c.dec = kb.tile("dec", [128, 4, 4], F32)
    c.Sst = kb.tile("Sst", [128, 4 * NGC, 128], F32)
    c.SbfA = [kb.tile("SbfA%d" % i, [128, 4, 128], BF16) for i in range(4)]
    c.PT4 = [kb.tile("PT%d" % i, [128, 4, 128], BF16) for i in range(4)]
    c.dS = [kb.tile("dS%d" % i, [128, 512], F32) for i in range(4)]
    c.sq2 = [kb.tile("sq2_%d" % i, [128, 512], F32) for i in range(2)]
    c.st22 = [kb.tile("st22_%d" % i, [128, 12], F32) for i in range(2)]
    c.stmp = kb.tile("stmp", [128, 512], F32)

    kb.dma("sp", c.lbT[:, :, :], ins["lbT"][:, :, :], [ins["lbT"]], [c.lbT])
    kb.dma("sp", c.G4[:, :], ins["G4"][:, :], [ins["G4"]], [c.G4])
    kb.dma("sp", c.M4[:, :, :], ins["M4"][:, :, :], [ins["M4"]], [c.M4])
    kb.dma("sp", c.rmask[:, :], ins["rmask"][:, :], [ins["rmask"]], [c.rmask])
    w = c.lbw
    kb.tt("dve", w, w[:, 3, :], c.lbT, c.lbT[:, 0, :], c.lbT, c.lbT[:, 1, :], ALU.max)
    kb.tt("dve", w, w[:, 3, :], w, w[:, 3, :], c.lbT, c.lbT[:, 2, :], ALU.max)
    for r in range(3):
        kb.tt("dve", c.lbT, c.lbT[:, r, :], c.lbT, c.lbT[:, r, :], w, w[:, 3, :], ALU.subtract)
    kb.act(c.lbT, c.lbT[:, :, :], c.lbT, c.lbT[:, :, :], AF.Exp)
    kb.tt("dve", w, w[:, 4, :], c.lbT, c.lbT[:, 0, :], c.lbT, c.lbT[:, 1, :], ALU.add)
    kb.tt("dve", w, w[:, 4, :], w, w[:, 4, :], c.lbT, c.lbT[:, 2, :], ALU.add)
    kb.op("dve", lambda v: v.reciprocal(out=w[:, 5, :], in_=w[:, 4, :]), [w], [w])
    kb.tt("dve", w, w[:, 0, :], c.lbT, c.lbT[:, 0, :], w, w[:, 5, :], ALU.mult)
    kb.ts("dve", w, w[:, 1, :], w, w[:, 0, :], -1.0, 1.0, ALU.mult, ALU.add)
    kb.ts("dve", w, w[:, 2, :], w, w[:, 1, :], -1.0, None, ALU.mult)
    kb.op("dve", lambda v: v.memset(c.Sst[:, :, :], 0.0), [], [c.Sst])
    for i in range(4):
        kb.op("dve", lambda v, i=i: v.memset(c.PT4[i][:, :, :], 0.0), [], [c.PT4[i]])


def hgrn_layer(kb, c, ins, scr, x_tiles, x_ap_fn, nst=NST):
    lbw = c.lbw
    T = c.tmp
    for st in range(nst):
        issue_conv(kb, c, 4)
        load_norm_transpose(kb, c, x_tiles, x_ap_fn, st, c.gb)
        for g in range(NGC):
            wb = load_w_block(kb, c, "hgrn_w_in", 0 * NGC + g)
            for cc in range(4):
                p = proj_feature_major(kb, c, wb, cc)
                kb.op("act", lambda a, p=p, cc=cc: a.mul(out=c.qs[:, cc, :], in_=p[:, :], mul=128.0 ** -0.5), [p], [c.qs])
            wb = load_w_block(kb, c, "hgrn_w_in", 1 * NGC + g)
            for cc in range(4):
                hd = 4 * g + cc
                p = proj_feature_major(kb, c, wb, cc)
                sig, lf, bb, nbb, kk, e1, e2, e3, e4 = T
                kb.act(sig, sig[:, :], p, p[:, :], AF.Sigmoid)
                kb.act(lf, lf[:, :], sig, sig[:, :], AF.Ln, extra_reads=[lbw], bias=lbw[:, 0, hd:hd + 1], scale=lbw[:, 1, hd:hd + 1])
                kb.op("dve", lambda v: v.tensor_tensor_scan(out=bb[:, :], data0=c.rmask[:, :], data1=lf[:, :], initial=0.0,
                                                            op0=ALU.mult, op1=ALU.add), [c.rmask, lf], [bb])
                kb.ts("dve", nbb, nbb[:, :], bb, bb[:, :], -1.0, None, ALU.mult)
                kb.ts("pool", kk, kk[:, :], sig, sig[:, :], lbw[:, 2, hd:hd + 1], lbw[:, 1, hd:hd + 1], ALU.mult, ALU.add, extra_reads=[lbw])
                kb.act(e1, e1[:, :], bb, bb[:, :], AF.Exp)
                for j in range(4):
                    sl = slice(j * 128, (j + 1) * 128)
                    mid = j * 128 + 63
                    last = j * 128 + 127
                    kb.act(e2, e2[:, sl], bb, bb[:, sl], AF.Exp, extra_reads=[nbb], bias=nbb[:, mid:mid + 1], scale=1.0)
                    kb.act(e3, e3[:, sl], bb, bb[:, sl], AF.Exp, extra_reads=[], bias=bb[:, mid:mid + 1], scale=-1.0)
                    kb.act(e4, e4[:, sl], bb, bb[:, sl], AF.Exp, extra_reads=[], bias=bb[:, last:last + 1], scale=-1.0)
                kb.copy("pool", c.dec, c.dec[:, cc, :], e1, e1[:, 127::128])
                kb.tt("dve", c.qin, c.qin[:, cc, :], c.qs, c.qs[:, cc, :], e1, e1[:, :], ALU.mult)
                kb.tt("dve", c.qmid, c.qmid[:, cc, :], c.qs, c.qs[:, cc, :], e2, e2[:, :], ALU.mult)
                kb.tt("pool", c.kmid, c.kmid[:, cc, :], kk, kk[:, :], e3, e3[:, :], ALU.mult)
                kb.tt("pool", c.kend, c.kend[:, cc, :], kk, kk[:, :], e4, e4[:, :], ALU.mult)
            wb = load_w_block(kb, c, "hgrn_w_in", 2 * NGC + g)
            for j in range(4):
                p = proj_token_major(kb, c, wb, j)
                kb.copy("act", c.Vg, c.Vg[:, j, :], p, p[:, :])
            wb = load_w_block(kb, c, "hgrn_w_in", 3 * NGC + g)
            for j in range(4):
                p = proj_token_major(kb, c, wb, j)
                kb.act(c.stmp, c.stmp[:, :], p, p[:, :], AF.Silu)
                kb.tt("dve", c.gz, c.gz[:, j, :], c.stmp, c.stmp[:, :], c.G4, c.G4[:, :], ALU.mult)
            for half in range(2):
                pt = next_ptr(c)
                for i in range(8):
                    idx = half * 8 + i
                    j, cc = idx // 4, idx % 4
                    kb.tr(pt, pt[:, i * 128:(i + 1) * 128], c.kend, c.kend[:, cc, j * 128:(j + 1) * 128], c.ident)
                kb.copy("act", c.kendT, c.kendT[:, half * 2:(half + 1) * 2, :], pt,
                        pt[:, :].rearrange("p (a b) -> p a b", b=512))
            ps_s, ps_d = c.pm[0], c.pm[2]
            for j in range(4):
                PT = c.PT4[j]
                t0 = j * 128
                for cc in range(4):
                    kb.mm(ps_s, ps_s[:, cc * 128 + 64:cc * 128 + 128], c.kmid, c.kmid[:, cc, t0:t0 + 128],
                          c.qmid, c.qmid[:, cc, t0 + 64:t0 + 128], True, True)
                    kb.mm(ps_s, ps_s[0:64, cc * 128:cc * 128 + 64], c.kmid, c.kmid[:, cc, t0:t0 + 64],
                          c.qmid, c.qmid[:, cc, t0:t0 + 64], True, True)
                ps3 = ps_s[:, :].rearrange("p (a b) -> p a b", b=128)
                kb.tt("dve", PT, PT[:, :, 64:128], ps_s, ps3[:, :, 64:128], c.M4, c.M4[:, :, 64:128], ALU.mult)
                ps3b = ps_s[0:64, :].rearrange("p (a b) -> p a b", b=128)
                kb.tt("dve", PT, PT[0:64, :, 0:64], ps_s, ps3b[:, :, 0:64], c.M4, c.M4[0:64, :, 0:64], ALU.mult)
                for cc in range(4):
                    kb.mm(ps_d, ps_d[:, cc * 128:(cc + 1) * 128], c.kendT, c.kendT[:, j, cc * 128:(cc + 1) * 128],
                          c.Vg, c.Vg[:, j, cc * 128:(cc + 1) * 128], True, True)
                kb.copy("act", c.dS[j], c.dS[j][:, :], ps_d, ps_d[:, :])
            S4 = c.Sst[:, 4 * g:4 * g + 4, :]
            kb.copy("act", c.SbfA[0], c.SbfA[0][:, :, :], c.Sst, S4)
            for j in range(4):
                for cc in range(4):
                    hd = 4 * g + cc
                    kb.stt(c.Sst, c.Sst[:, hd, :], c.Sst, c.Sst[:, hd, :], c.dec[:, cc, j:j + 1], c.dS[j], c.dS[j][:, cc * 128:(cc + 1) * 128],
                           ALU.mult, ALU.add, extra_reads=[c.dec])
                if j < 3:
                    kb.copy("act", c.SbfA[j + 1], c.SbfA[j + 1][:, :, :], c.Sst, S4)
            for j in range(4):
                PT = c.PT4[j]
                t0 = j * 128
                ps_o = c.pm[1] if j % 2 == 0 else c.pm[3]
                for cc in range(4):
                    kb.mm(ps_o, ps_o[:, cc * 128:(cc + 1) * 128], PT, PT[:, cc, :], c.Vg, c.Vg[:, j, cc * 128:(cc + 1) * 128], True, False)
                    kb.mm(ps_o, ps_o[:, cc * 128:(cc + 1) * 128], c.qin, c.qin[:, cc, t0:t0 + 128], c.SbfA[j], c.SbfA[j][:, cc, :], False, True)
                sq = c.sq2[j % 2]
                st2 = c.st22[j % 2]
                kb.act(sq, sq[:, :], ps_o, ps_o[:, :], AF.Square)
                kb.op("dve", lambda v, sq=sq, st2=st2: v.tensor_reduce(out=st2[:, 0:4], in_=sq[:, :].rearrange("p (a b) -> p a b", b=128),
                                                                     axis=AX.X, op=ALU.add), [sq], [st2])
                kb.op("act", lambda a, st2=st2: a.activation(out=st2[:, 4:8], in_=st2[:, 0:4], func=AF.Sqrt, bias=c.epsb[:, 0:1], scale=1.0 / 128),
                      [st2, c.epsb], [st2])
                kb.op("dve", lambda v, st2=st2: v.reciprocal(out=st2[:, 8:12], in_=st2[:, 4:8]), [st2], [st2])
                for cc in range(4):
                    hd = 4 * g + cc
                    kb.stt(c.og, c.og[:, j, hd * 128:(hd + 1) * 128], ps_o, ps_o[:, cc * 128:(cc + 1) * 128], st2[:, 8 + cc:9 + cc],
                           c.gz, c.gz[:, j, cc * 128:(cc + 1) * 128], ALU.mult, ALU.mult, extra_reads=[st2])
        for j in range(4):
            kb.dma("pool", scr["OG0src"][st].ap()[j * 128:(j + 1) * 128, :], c.og[:, j, :], [c.og], [scr["OG0src_b"][st]])
        kb.coll(scr["OG0src"][st], scr["OG0g"][st], [scr["OG0src_b"][st]], [scr["OG0g_b"][st]])
        if st >= 1:
            hgrn_out(kb, c, ins, scr, st - 1)
    hgrn_out(kb, c, ins, scr, nst - 1)


SCALE = 128.0 ** -0.5
BIG = 30000.0


def rot(c, name, n):
    lst = getattr(c, name)
    i = getattr(c, name + "_i", 0)
    setattr(c, name + "_i", (i + 1) % n)
    return lst[i]


def nsa_proj_phase(kb, c, ins, scr, x_tiles, x_fn):
    c.obf = [kb.tile("obf%d" % i, [128, 512], BF16) for i in range(4)]
    c.of32 = [kb.tile("of32_%d" % i, [128, 512], F32) for i in range(3)]
    for st in range(NST):
        load_norm_transpose(kb, c, x_tiles, x_fn, st, c.gb)
        t0 = st * 512
        for nb in range(NGC):
            wb = load_w_block(kb, c, "nsa_w_in", nb)
            for cc in range(4):
                p = proj_feature_major(kb, c, wb, cc)
                ob = rot(c, "obf", 4)
                kb.op("act", lambda a, p=p, ob=ob: a.mul(out=ob[:, :], in_=p[:, :], mul=SCALE), [p], [ob])
                kb.dma("pool", scr["QT"][nb * 4 + cc, :, t0:t0 + 512], ob[:, :], [ob], [])
        for nb, dsts in [(2, ("KCT", "VCT")), (3, ("KST", "KWT"))]:
            wb = load_w_block(kb, c, "nsa_w_in", nb)
            for cc in range(4):
                p = proj_feature_major(kb, c, wb, cc)
                ob = rot(c, "obf", 4)
                kb.copy("act", ob, ob[:, :], p, p[:, :])
                kb.dma("pool", scr[dsts[cc // 2]][cc % 2, :, t0:t0 + 512], ob[:, :], [ob], [])
        wb = load_w_block(kb, c, "nsa_w_in", 4)
        for j in range(4):
            p = proj_token_major(kb, c, wb, j)
            ob = rot(c, "obf", 4)
            kb.copy("act", ob, ob[:, :], p, p[:, :])
            kb.dma("pool", scr["VS"][t0 + j * 128:t0 + (j + 1) * 128, :], ob[:, 0:256], [ob], [])
            kb.dma("pool", scr["VW"][t0 + j * 128:t0 + (j + 1) * 128, :], ob[:, 256:512], [ob], [])
        wb = load_w_block(kb, c, "nsa_w_in", 5)
        for j in range(4):
            p = proj_token_major(kb, c, wb, j, ncols=24)
            of = rot(c, "of32", 3)
            kb.act(of, of[:, 0:24], p, p[:, 0:24], AF.Sigmoid)
            kb.dma("pool", scr["GATE"][t0 + j * 128:t0 + (j + 1) * 128, :], of[:, 0:24], [of], [])
        for i in range(NGC):
            wb = load_w_block(kb, c, "nsa_w_in", 6 + i)
            for j in range(4):
                p = proj_token_major(kb, c, wb, j)
                of = rot(c, "of32", 3)
                kb.act(of, of[:, :], p, p[:, :], AF.Silu)
                kb.dma("pool", scr["SZ"][t0 + j * 128:t0 + (j + 1) * 128, i * 512:(i + 1) * 512], of[:, :], [of], [])


def setup_attn(kb, c, ins, scr):
    c.KST = kb.tile("KST", [128, S], BF16)
    c.KWT = kb.tile("KWT", [128, S], BF16)
    c.VSe = kb.tile("VSe", [128, NT, 129], BF16)
    c.VWe = kb.tile("VWe", [128, NT, 129], BF16)
    c.KcT = kb.tile("KcT", [128, 256], BF16)
    c.Vce = kb.tile("Vce", [128, 2, 193], BF16)
    c.BT = [kb.tile("BT%d" % i, [128, 512], F32) for i in range(3)]
    c.q4s = [kb.tile("q4_%d" % i, [128, 4, 128], BF16) for i in range(4)]
    c.PTs = [kb.tile("PTa%d" % i, [128, 512], BF16) for i in range(15)]
    c.sbs = [kb.tile("sbs%d" % i, [128, 512], F32) for i in range(2)]
    c.BCs = [kb.tile("BC%d" % i, [128, 512], F32) for i in range(2)]
    c.accs = [kb.tile("acc%d" % i, [128, 4, 128], F32) for i in range(2)]
    c.obr = kb.tile("obr", [128, 4, 193], F32)
    c.szs = [kb.tile("sz%d" % i, [128, 512], F32) for i in range(4)]
    c.ogts = [kb.tile("ogt%d" % i, [128, 512], BF16) for i in range(2)]
    c.gates = kb.tile("gates", [128, NT, 24], F32)
    c.expand = kb.tile("expand", [64, NT, 128], BF16)
    c.seladd = kb.tile("seladd", [128, NT, 64], F32)
    c.ncb = kb.tile("ncb", [128, 8], F32)
    c.sm = kb.tile("sm", [128, 16], F32)
    c.imp = kb.tile("imp", [128, 64], F32)
    c.score = kb.tile("score", [128, 64], F32)
    c.work = kb.tile("work", [128, 64], F32)
    c.m8 = kb.tile("m8", [128, 16], F32)
    c.sel = kb.tile("sel", [128, 64], BF16)
    c.nselTs = [kb.tile("nselT%d" % i, [64, 512], BF16) for i in range(2)]
    c.w1b = [kb.tile("w1b%d" % i, [128, 32, 128], BF16) for i in range(2)]
    c.w2b = [kb.tile("w2b%d" % i, [128, 128], BF16) for i in range(2)]
    c.peTb = [kb.tile("peTb%d" % i, [128, 32], BF16) for i in range(2)]
    c.cbias = [kb.tile("cbias%d" % i, [128, 1], F32) for i in range(2)]
    c.kct = kb.tile("kct", [128, S], BF16)
    c.cu = [kb.tile("cu%d" % i, [128, 256], F32) for i in range(4)]
    c.GT = kb.tile("GT", [128, 256], BF16)
    c.ovb = kb.tile("ovb", [128, 2, 65], F32)

    kb.dma("sp", c.gates[:, :, :], scr["GATE"][:, :].rearrange("(j p) c -> p j c", p=128), [], [c.gates])
    kb.dma("sp", c.seladd[:, :, :], ins["seladd"][:, :, :], [], [c.seladd])
    kb.dma("sp", c.ncb[:, :], ins["cbb"][:, :], [], [c.ncb])
    kb.ts("dve", c.ncb, c.ncb[:, :], c.ncb, c.ncb[:, :], -1.0, None, ALU.mult)
    kb.dma("sp", c.ovb[:, :, :], ins["ov"][:, :, :], [], [c.ovb])
    kb.copy("dve", c.Vce, c.Vce[:, :, 128:193], c.ovb, c.ovb[:, :, :])
    kb.op("dve", lambda v: v.memset(c.VSe[:, :, 128:129], 1.0), [], [c.VSe])
    kb.op("dve", lambda v: v.memset(c.VWe[:, :, 128:129], 1.0), [], [c.VWe])
    kb.op("dve", lambda v: v.memset(c.GT[:, :], 0.0), [], [c.GT])
    with contextlib.ExitStack() as tes:
        old = kb.es
        kb.es = tes
        stg = kb.tile("stg_big", [128, 4096], F32)
        kb.dma("sp", stg[0:64, :], ins["expand"][:, :], [], [stg])
        kb.copy("dve", c.expand, c.expand[:, :, :].rearrange("p a b -> p (a b)"), stg, stg[0:64, :])
        for i, nm in enumerate(["k", "v"]):
            kb.dma("sp", stg[:, :].rearrange("p (j h) -> p j h", h=128),
                   ins["nsa_phi_%s_w1" % nm][:, :].rearrange("(j d) h -> d j h", d=128), [], [stg])
            kb.copy("dve", c.w1b[i], c.w1b[i][:, :, :].rearrange("p a b -> p (a b)"), stg, stg[:, :])
            kb.dma("sp", stg[:, 0:128], ins["nsa_phi_%s_w2" % nm][:, :], [], [stg])
            kb.copy("dve", c.w2b[i], c.w2b[i][:, :], stg, stg[:, 0:128])
            kb.dma("sp", stg[:, 0:32], ins["peT_%s" % nm][:, :], [], [stg])
            kb.copy("dve", c.peTb[i], c.peTb[i][:, :], stg, stg[:, 0:32])
            kb.dma("sp", stg[:, 0:1], ins["b1_%s" % nm][:, :], [], [stg])
            p = c.pm[0]
            for j in range(32):
                kb.mm(p, p[:, 0:1], c.w1b[i], c.w1b[i][:, j, :], c.peTb[i], c.peTb[i][:, j:j + 1], j == 0, j == 31)
            kb.tt("dve", c.cbias[i], c.cbias[i][:, :], p, p[:, 0:1], stg, stg[:, 0:1], ALU.add)
        kb.barrier()
        kb.es = old


def nsa_compress(kb, c, scr, g):
    for i, nm in enumerate(["KCT", "VCT"]):
        kb.dma("sp", c.kct[:, :], scr[nm][g, :, :], [], [c.kct])
        p = next_pacc(c)
        for j in range(32):
            kb.mm(p, p[:, 0:255], c.w1b[i], c.w1b[i][:, j, :], c.kct, c.kct[:, j:j + 4065:16], j == 0, j == 31)
        u, u2, inner, th = c.cu
        kb.act(u, u[:, 0:255], p, p[:, 0:255], AF.Identity, extra_reads=[c.cbias[i]], bias=c.cbias[i][:, 0:1], scale=1.0)
        kb.tt("dve", u2, u2[:, 0:255], u, u[:, 0:255], u, u[:, 0:255], ALU.mult)
        kb.ts("dve", u2, u2[:, 0:255], u2, u2[:, 0:255], 0.044715, 1.0, ALU.mult, ALU.add)
        kb.tt("dve", inner, inner[:, 0:255], u2, u2[:, 0:255], u, u[:, 0:255], ALU.mult)
        kb.act(th, th[:, 0:255], inner, inner[:, 0:255], AF.Tanh, scale=0.7978845608028654)
        kb.stt(inner, inner[:, 0:255], th, th[:, 0:255], 1.0, u, u[:, 0:255], ALU.add, ALU.mult)
        kb.op("act", lambda a: a.mul(out=c.GT[:, 0:255], in_=inner[:, 0:255], mul=0.5), [inner], [c.GT])
        if i == 0:
            p2 = next_pacc(c)
            kb.mm(p2, p2[:, 0:256], c.w2b[0], c.w2b[0][:, :], c.GT, c.GT[:, :], True, True)
            kb.copy("act", c.KcT, c.KcT[:, :], p2, p2[:, 0:256])
        else:
            for jm in range(2):
                p2 = next_pacc(c)
                kb.mm(p2, p2[:, 0:128], c.GT, c.GT[:, jm * 128:(jm + 1) * 128], c.w2b[1], c.w2b[1][:, :], True, True)
                kb.copy("act", c.Vce, c.Vce[:, jm, 0:128], p2, p2[:, 0:128])


def bias_exp(kb, c, ps, table, g, PT, shifted=False):
    sb = rot(c, "sbs", 2)
    if shifted:
        kb.tt("dve", sb, sb[:, :], ps, ps[:, :], table, table[:, :], ALU.add)
    else:
        for cc in range(4):
            h = 4 * g + cc
            kb.stt(sb, sb[:, cc * 128:(cc + 1) * 128], ps, ps[:, cc * 128:(cc + 1) * 128], c.ncb[:, h:h + 1],
                   table, table[:, cc * 128:(cc + 1) * 128], ALU.add, ALU.add, extra_reads=[c.ncb])
    kb.act(PT, PT[:, :], sb, sb[:, :], AF.Exp)


def branch_tail(kb, c, g, ti, br, width, first):
    ob = c.obr
    acc = c.accs[ti % 2]
    for cc in range(4):
        kb.copy("act", ob, ob[:, cc, 0:width], c.pm[cc], c.pm[cc][:, 0:width])
    zc = width - 1
    kb.ts("dve", c.sm, c.sm[:, 0:4], ob, ob[:, :, zc], 1e-30, None, ALU.max)
    kb.op("dve", lambda v: v.reciprocal(out=c.sm[:, 4:8], in_=c.sm[:, 0:4]), [c.sm], [c.sm])
    kb.tt("dve", c.sm, c.sm[:, 8:12], c.sm, c.sm[:, 4:8], c.gates, c.gates[:, ti, 12 * g + br:12 * g + 12:3], ALU.mult)
    if br == 0:
        kb.ts("dve", c.imp, c.imp[:, :], ob, ob[:, 0, 128:192], c.sm[:, 4:5], None, ALU.mult, extra_reads=[c.sm])
        for cc in range(1, 4):
            kb.stt(c.imp, c.imp[:, :], ob, ob[:, cc, 128:192], c.sm[:, 4 + cc:5 + cc], c.imp, c.imp[:, :],
                   ALU.mult, ALU.add, extra_reads=[c.sm])
    for cc in range(4):
        if first:
            kb.ts("dve", acc, acc[:, cc, :], ob, ob[:, cc, 0:128], c.sm[:, 8 + cc:9 + cc], None, ALU.mult, extra_reads=[c.sm])
        else:
            kb.stt(acc, acc[:, cc, :], ob, ob[:, cc, 0:128], c.sm[:, 8 + cc:9 + cc], acc, acc[:, cc, :],
                   ALU.mult, ALU.add, extra_reads=[c.sm])


def topk_select(kb, c, ti):
    kb.tt("dve", c.score, c.score[:, :], c.imp, c.imp[:, :], c.seladd, c.seladd[:, ti, :], ALU.add)
    kb.op("dve", lambda v: v.max(out=c.m8[:, 0:8], in_=c.score[:, :]), [c.score], [c.m8])
    kb.op("dve", lambda v: v.match_replace(out=c.work[:, :], in_to_replace=c.m8[:, 0:8], in_values=c.score[:, :], imm_value=-3.0e38),
          [c.m8, c.score], [c.work])
    kb.op("dve", lambda v: v.max(out=c.m8[:, 8:16], in_=c.work[:, :]), [c.work], [c.m8])
    kb.ts("dve", c.sel, c.sel[:, :], c.score, c.score[:, :], c.m8[:, 15:16], None, ALU.is_ge, extra_reads=[c.m8])
    if ti > 0:
        pt = c.ptr[0]
        for r in range(4):
            kb.tr(pt, pt[0:64, r * 128:(r + 1) * 128], c.sel, c.sel[:, :], c.ident)
        nselT = c.nselTs[ti % 2]
        kb.ts("dve", nselT, nselT[:, :], pt, pt[0:64, 0:512], -1.0, None, ALU.add)


def nsa_attn_group(kb, c, ins, scr, g, nt=NT):
    kb.dma("sp", c.KST[:, :], scr["KST"][g, :, :], [], [c.KST])
    kb.dma("sp", c.KWT[:, :], scr["KWT"][g, :, :], [], [c.KWT])
    kb.dma("sp", c.VSe[:, :, 0:128], scr["VS"][:, g * 128:(g + 1) * 128].rearrange("(j p) d -> p j d", p=128), [], [c.VSe])
    kb.dma("sp", c.VWe[:, :, 0:128], scr["VW"][:, g * 128:(g + 1) * 128].rearrange("(j p) d -> p j d", p=128), [], [c.VWe])
    for d in range(3):
        kb.dma("sp", c.BT[d][:, :], ins["BT"][d, g, :, :], [], [c.BT[d]])
        for cc in range(4):
            h = 4 * g + cc
            kb.ts("dve", c.BT[d], c.BT[d][:, cc * 128:(cc + 1) * 128], c.BT[d], c.BT[d][:, cc * 128:(cc + 1) * 128],
                  c.ncb[:, h:h + 1], None, ALU.add, extra_reads=[c.ncb])
    cmp_t, win_t, sel_t = {}, {}, {}

    def mk_task(lst, **kw):
        t = dict(pre=None, post=None, table=None, st={}, nselT=None)
        t.update(kw)
        lst.append(t)

    for ti in range(nt):
        st = {}
        cmp_t[ti], win_t[ti], sel_t[ti] = [], [], []

        def pre_tile(ti=ti, st=st):
            q4 = rot(c, "q4s", 4)
            kb.dma("sp", q4[:, :, :], scr["QT"][4 * g:4 * g + 4, :, ti * 128:(ti + 1) * 128].rearrange("h p t -> p h t"), [], [q4])
            sz = rot(c, "szs", 4)
            kb.dma("sp", sz[:, :], scr["SZ"][ti * 128:(ti + 1) * 128, g * 512:(g + 1) * 512], [], [sz])
            st["q4"] = q4
            st["sz"] = sz

        jms = [0] + ([1] if ti >= 16 else [])

        def post_cmp(ti=ti):
            branch_tail(kb, c, g, ti, 0, 193, True)
            topk_select(kb, c, ti)

        for idx, jm in enumerate(jms):
            near = ti < 17 + 16 * jm
            mk_task(cmp_t[ti], st=st, kT=(c.KcT, c.KcT[:, jm * 128:(jm + 1) * 128]), maskj=None,
                    table=("FULL", 128 * jm - 8 * ti + 8 + 240) if near else None,
                    v=(c.Vce, c.Vce[:, jm, :]), width=193, start=(idx == 0), stop=(idx == len(jms) - 1),
                    pre=pre_tile if idx == 0 else None,
                    post=post_cmp if idx == len(jms) - 1 else None)
        j0 = max(0, ti - 4)
        for j in range(j0, ti + 1):
            dl = ti - j
            tb = c.BT[dl] if dl <= 1 else (c.BT[2] if dl == 4 else None)
            mk_task(win_t[ti], st=st, kT=(c.KWT, c.KWT[:, j * 128:(j + 1) * 128]), maskj=None, table=tb,
                    v=(c.VWe, c.VWe[:, j, :]), width=129, start=(j == j0), stop=(j == ti),
                    post=(lambda ti=ti: branch_tail(kb, c, g, ti, 2, 129, False)) if j == ti else None)
        for j in range(ti + 1):
            dl = ti - j
            tb = c.BT[dl] if dl <= 1 else None

            def post_sel(ti=ti, st=st):
                branch_tail(kb, c, g, ti, 1, 129, False)
                acc = c.accs[ti % 2]
                ogt = rot(c, "ogts", 2)
                kb.tt("dve", ogt, ogt[:, :], acc, acc[:, :, :].rearrange("p a b -> p (a b)"), st["sz"], st["sz"][:, :], ALU.mult)
                sti = ti // 4
                kb.dma("pool", scr["OG1src"][sti].ap()[(ti % 4) * 128:(ti % 4 + 1) * 128, g * 512:(g + 1) * 512], ogt[:, :],
                       [ogt], [scr["OG1src_b"][sti]])
                if g == NGC - 1 and ti % 4 == 3:
                    kb.coll(scr["OG1src"][sti], scr["OG1g"][sti], [scr["OG1src_b"][sti]], [scr["OG1g_b"][sti]])

            mk_task(sel_t[ti], st=st, kT=(c.KST, c.KST[:, j * 128:(j + 1) * 128]), maskj=(j if j < ti else None), table=tb,
                    v=(c.VSe, c.VSe[:, j, :]), width=129, start=(j == 0), stop=(j == ti), nselT=c.nselTs[ti % 2],
                    post=post_sel if j == ti else None)

    tasks = list(cmp_t[0])
    for ti in range(nt):
        tasks += win_t[ti]
        if ti + 1 < nt:
            tasks += cmp_t[ti + 1]
        tasks += sel_t[ti]
    pos = {id(t): i for i, t in enumerate(tasks)}
    for ti in range(nt):
        sel_t[ti][0]["need_back"] = pos[id(cmp_t[ti][-1])]

    def emit_front(t):
        if t["pre"]:
            t["pre"]()
        q4 = t["st"]["q4"]
        q4f = q4[:, :, :].rearrange("p a b -> p (a b)")
        ps = next_pacc(c)
        kb.mm(ps, ps[:, :], t["kT"][0], t["kT"][1], q4, q4f, True, t["maskj"] is None)
        if t["maskj"] is not None:
            kb.mm(ps, ps[:, :], c.expand, c.expand[:, t["maskj"], :], t["nselT"], t["nselT"][:, :], False, True)
        PT = rot(c, "PTs", 15)
        tb = t["table"]
        if tb is None:
            kb.act(PT, PT[:, :], ps, ps[:, :], AF.Exp)
        elif isinstance(tb, tuple):
            BC = rot(c, "BCs", 2)
            kb.dma("sp", BC[:, :], ins["FULL"][g, tb[1]:tb[1] + 128, :], [], [BC])
            bias_exp(kb, c, ps, BC, g, PT)
        else:
            bias_exp(kb, c, ps, tb, g, PT, shifted=True)
        t["PT"] = PT

    def emit_back(t):
        PT = t["PT"]
        w = t["width"]
        for cc in range(4):
            kb.mm(c.pm[cc], c.pm[cc][:, 0:w], PT, PT[:, cc * 128:(cc + 1) * 128], t["v"][0], t["v"][1], t["start"], t["stop"])
        if t["post"]:
            t["post"]()

    n = len(tasks)
    DEPTH = 12
    nb = 0
    for i in range(n):
        need = max(i - DEPTH, tasks[i].get("need_back", -1))
        while nb <= need:
            emit_back(tasks[nb])
            nb += 1
        emit_front(tasks[i])
    while nb < n:
        emit_back(tasks[nb])
        nb += 1


def nsa_out_phase(kb, c, ins, scr, out, y_tiles, nst=NST):
    c.x2 = [kb.tile("x2_%d" % i, [128, 4, HW], F32) for i in range(2)]
    c.ssb = [kb.tile("ssb%d" % i, [128, 4], F32) for i in range(2)]
    c.ssg = kb.tile("ssg", [128, 2, 4], F32)
    c.rs = kb.tile("rs", [128, 12], F32)
    c.sqh = kb.tile("sqh", [128, HW], BF16)

    def finalize(st):
        x2 = c.x2[st % 2]
        kb.dma("sp", c.ssg[:, :, :], scr["SSg"][st].ap().rearrange("(r p) j -> p r j", p=128), [scr["SSg_b"][st]], [c.ssg])
        kb.tt("dve", c.rs, c.rs[:, 0:4], c.ssg, c.ssg[:, 0, :], c.ssg, c.ssg[:, 1, :], ALU.add)
        kb.op("act", lambda a: a.activation(out=c.rs[:, 4:8], in_=c.rs[:, 0:4], func=AF.Sqrt, bias=c.epsb[:, 0:1], scale=1.0 / D),
              [c.rs, c.epsb], [c.rs])
        kb.op("dve", lambda v: v.reciprocal(out=c.rs[:, 8:12], in_=c.rs[:, 4:8]), [c.rs], [c.rs])
        for j in range(4):
            ti = st * 4 + j
            kb.stt(c.xt, c.xt[:, 0:HW], x2, x2[:, j, :], c.rs[:, 8 + j:9 + j], c.gb, c.gb[:, 0:HW], ALU.mult, ALU.mult, extra_reads=[c.rs])
            kb.dma("pool", out[ti * 128:(ti + 1) * 128, :], c.xt[:, 0:HW], [c.xt], [y_tiles[ti]])

    for st in range(nst):
        x2 = c.x2[st % 2]
        ssb = c.ssb[st % 2]
        load_gathered_og(kb, c, scr["OG1g"][st], scr["OG1g_b"][st], c.hT)
        for nb in range(2):
            wb = load_w_block(kb, c, "nsa_w_out", nb)
            for j in range(4):
                ti = st * 4 + j
                p = proj_token_major(kb, c, wb, j)
                xs = c.xs[c.xs_i]
                c.xs_i ^= 1
                kb.dma("sp", xs[:, :], scr["X1own"][ti * 128:(ti + 1) * 128, nb * 512:(nb + 1) * 512], [], [xs])
                kb.tt("dve", x2, x2[:, j, nb * 512:(nb + 1) * 512], p, p[:, :], xs, xs[:, :], ALU.add)
        for j in range(4):
            kb.op("act", lambda a, j=j: a.activation(out=c.sqh[:, :], in_=x2[:, j, :], func=AF.Square, accum_out=ssb[:, j:j + 1]),
                  [x2], [c.sqh, ssb])
        kb.dma("pool", scr["SSsrc"][st].ap()[:, :], ssb[:, :], [ssb, c.sqh], [scr["SSsrc_b"][st]])
        kb.coll(scr["SSsrc"][st], scr["SSg"][st], [scr["SSsrc_b"][st]], [scr["SSg_b"][st]])
        if st >= 1:
            finalize(st - 1)
    finalize(nst - 1)


def t5_bucket_np(dist):
    import math
    n = np.maximum(dist, 0)
    me = 16
    large = me + (np.log(np.maximum(n, 1).astype(np.float32) / np.float32(me)) / np.float32(math.log(128 / me))
                  * np.float32(32 - me)).astype(np.int32)
    large = np.minimum(large, 31)
    return np.where(n < me, n, large)


def host_consts():
    k = {}
    ss = np.arange(128)[:, None]
    tt = np.arange(128)[None, :]
    tri = (ss <= tt).astype(np.float32)
    k["M4"] = np.ascontiguousarray(np.broadcast_to(tri[:, None, :], (128, 4, 128))).astype(np.float32)
    rm = np.ones((128, 512), np.float32)
    rm[:, 0::128] = 0.0
    k["rmask"] = rm
    k["identf"] = np.eye(128, dtype=np.float32)
    k["epsb"] = np.full((128, 1), EPS, np.float32)
    t = np.arange(S)[:, None]
    n = np.arange(64)[None, :]
    cur = t // 64
    forced = (n == 0) | (n == cur) | (n == cur - 1)
    visible = n * 64 <= t
    sa = np.where(forced, 1e9, np.where(visible, 0.0, -1e9)).astype(np.float32)
    k["seladd"] = np.ascontiguousarray(sa.reshape(NT, 128, 64).transpose(1, 0, 2))
    ex = np.zeros((64, NT, 128), np.float32)
    for j in range(NT):
        ex[2 * j, j, 0:64] = BIG
        ex[2 * j + 1, j, 64:128] = BIG
    k["expand"] = ex.reshape(64, NT * 128)
    m = np.arange(256)[:, None]
    ov = ((16 * m < 64 * (n + 1)) & (16 * m + 32 > 64 * n) & (m < 255)).astype(np.float32)
    ovx = np.concatenate([ov, np.ones((256, 1), np.float32)], axis=1)
    k["ov"] = np.ascontiguousarray(ovx.reshape(2, 128, 65).transpose(1, 0, 2))
    dist0 = tt - ss
    k["bt_idx"] = [t5_bucket_np(dist0), t5_bucket_np(dist0 + 128), t5_bucket_np(dist0 + 512)]
    k["bt_valid"] = [dist0 >= 0, np.ones_like(dist0, bool), (dist0 + 512) <= 511]
    r = np.arange(504)[:, None] - 240
    distf = tt - 16 * (r - 8) - 31
    k["full_idx"] = t5_bucket_np(distf)
    k["full_valid"] = distf >= 0
    return k


_HC = None


def make_inputs(b, r, x, norm_gains, final_gain, rel_bias, hgrn_lb, hgrn_w_in, hgrn_head_gain, hgrn_w_out,
                nsa_w_in, nsa_pe_k, nsa_pe_v, nsa_phi_k_w1, nsa_phi_k_b1, nsa_phi_k_w2,
                nsa_phi_v_w1, nsa_phi_v_b1, nsa_phi_v_w2, nsa_w_out):
    global _HC
    if _HC is None:
        _HC = host_consts()
    k = _HC
    f = np.float32
    cs = slice(r * HW, (r + 1) * HW)
    m = {}
    m["x"] = np.ascontiguousarray(x[b])
    m["xh"] = np.ascontiguousarray(x[b][:, cs])
    m["g0b"] = np.ascontiguousarray(np.broadcast_to(norm_gains[0][None, :], (128, D))).astype(f)
    m["g1b"] = np.ascontiguousarray(np.broadcast_to(norm_gains[1][None, :], (128, D))).astype(f)
    m["gfb"] = np.ascontiguousarray(np.broadcast_to(np.tile(final_gain[cs], 2)[None, :], (128, D))).astype(f)
    m["lbT"] = np.ascontiguousarray(hgrn_lb.reshape(3, 16, 128)[:, 8 * r:8 * r + 8, :].transpose(2, 0, 1)).astype(f)
    m["G4"] = np.ascontiguousarray(np.broadcast_to(np.tile(hgrn_head_gain[0], 4)[None, :], (128, 512))).astype(f)
    for nm in ["M4", "rmask", "identf", "epsb", "seladd", "expand", "ov"]:
        m[nm] = k[nm]
    m["hgrn_w_in"] = np.ascontiguousarray(hgrn_w_in[0].reshape(D, 4, 16, 128)[:, :, 8 * r:8 * r + 8, :].reshape(D, 4 * HW))
    m["hgrn_w_out"] = np.ascontiguousarray(hgrn_w_out[0][:, cs])
    W = nsa_w_in[0]
    h256 = lambda base: W[:, base + 256 * r:base + 256 * r + 256]
    m["nsa_w_in"] = np.ascontiguousarray(np.concatenate(
        [W[:, cs], h256(2048), h256(2560), h256(3072), h256(4096), h256(3584), h256(4608),
         W[:, 5120 + 24 * r:5120 + 24 * r + 24], W[:, 5168 + HW * r:5168 + HW * r + HW]], axis=1))
    m["nsa_w_out"] = np.ascontiguousarray(nsa_w_out[0][:, cs])
    m["nsa_phi_k_w1"] = np.ascontiguousarray(nsa_phi_k_w1[0])
    m["nsa_phi_v_w1"] = np.ascontiguousarray(nsa_phi_v_w1[0])
    m["nsa_phi_k_w2"] = np.ascontiguousarray(nsa_phi_k_w2[0])
    m["nsa_phi_v_w2"] = np.ascontiguousarray(nsa_phi_v_w2[0])
    m["peT_k"] = np.ascontiguousarray(nsa_pe_k[0].T)
    m["peT_v"] = np.ascontiguousarray(nsa_pe_v[0].T)
    m["b1_k"] = np.ascontiguousarray(nsa_phi_k_b1[0].reshape(128, 1))
    m["b1_v"] = np.ascontiguousarray(nsa_phi_v_b1[0].reshape(128, 1))
    tab = rel_bias.astype(f).reshape(32, 4, 4)[:, 2 * r:2 * r + 2, :]
    BT = np.empty((3, NGC, 128, 4, 128), f)
    for d in range(3):
        gth = tab[k["bt_idx"][d]]
        gth = np.where(k["bt_valid"][d][:, :, None, None], gth, f(NEG))
        BT[d] = gth.transpose(2, 0, 3, 1)
    m["BT"] = np.ascontiguousarray(BT.reshape(3, NGC, 128, 512))
    gth = tab[k["full_idx"]]
    gth = np.where(k["full_valid"][:, :, None, None], gth, f(NEG))
    m["FULL"] = np.ascontiguousarray(gth.transpose(2, 0, 3, 1).reshape(NGC, 504, 512)).astype(f)
    m["cbb"] = np.ascontiguousarray(np.broadcast_to(rel_bias[31][None, 8 * r:8 * r + 8], (128, 8))).astype(f)
    return m


INPUT_SHAPES = {
    "x": [S, D], "xh": [S, HW], "g0b": [128, D], "g1b": [128, D], "gfb": [128, D], "lbT": [128, 3, 4 * NGC], "G4": [128, 512],
    "M4": [128, 4, 128], "rmask": [128, 512], "identf": [128, 128], "epsb": [128, 1],
    "seladd": [128, NT, 64], "expand": [64, NT * 128], "ov": [128, 2, 65],
    "hgrn_w_in": [D, 4 * HW], "hgrn_w_out": [D, HW], "nsa_w_in": [D, 3608], "nsa_w_out": [D, HW],
    "nsa_phi_k_w1": [4096, 128], "nsa_phi_v_w1": [4096, 128], "nsa_phi_k_w2": [128, 128], "nsa_phi_v_w2": [128, 128],
    "peT_k": [128, 32], "peT_v": [128, 32], "b1_k": [128, 1], "b1_v": [128, 1],
    "BT": [3, NGC, 128, 512], "FULL": [NGC, 504, 512], "cbb": [128, 8],
}


def build():
    nc = bass.Bass("TRN2", target_bir_lowering=False)
    es = contextlib.ExitStack()
    kb = KB(nc, es)
    c = Ctx()
    ins = {}
    for name, shape in INPUT_SHAPES.items():
        ins[name] = kb.dram(name, shape, F32, kind="ExternalInput")
    out = kb.dram("out", [S, HW], F32, kind="ExternalOutput")
    scr = {}

    def chunks(name, rows, cols, dtype):
        scr[name + "src"] = [nc.dram_tensor("%ssrc%d" % (name, i), [rows, cols], dtype) for i in range(NST)]
        scr[name + "g"] = [nc.dram_tensor("%sg%d" % (name, i), [2 * rows, cols], dtype) for i in range(NST)]
        scr[name + "src_b"] = [Buf(None, "%ssrcb%d" % (name, i)) for i in range(NST)]
        scr[name + "g_b"] = [Buf(None, "%sgb%d" % (name, i)) for i in range(NST)]

    chunks("OG0", 512, HW, BF16)
    chunks("X1", 512, HW, F32)
    chunks("OG1", 512, HW, BF16)
    chunks("SS", 128, 4, F32)
    scr["X1own"] = kb.dram("X1own", [S, HW], F32)
    scr["QT"] = kb.dram("QT", [4 * NGC, 128, S], BF16)
    for nm in ["KCT", "VCT", "KST", "KWT"]:
        scr[nm] = kb.dram(nm, [NGC, 128, S], BF16)
    scr["VS"] = kb.dram("VS", [S, 128 * NGC], BF16)
    scr["VW"] = kb.dram("VW", [S, 128 * NGC], BF16)
    scr["GATE"] = kb.dram("GATE", [S, 12 * NGC], F32)
    scr["SZ"] = kb.dram("SZ", [S, HW], F32)

    c.ident = kb.tile("ident", [128, 128], BF16)
    c.epsb = kb.tile("epsb", [128, 1], F32)
    identf = kb.tile("identf", [128, 128], F32)
    kb.dma("sp", identf[:, :], ins["identf"][:, :], [], [identf])
    kb.copy("dve", c.ident, c.ident[:, :], identf, identf[:, :])
    kb.dma("sp", c.epsb[:, :], ins["epsb"][:, :], [], [c.epsb])

    def psum_banks(nacc, ntr):
        c.pacc = [kb.psum("pacc%d" % i, [128, 512], F32) for i in range(nacc)]
        c.pacc_i = 0
        c.ptr = [kb.psum("ptr%d" % i, [128, 1024], BF16) for i in range(ntr)]
        c.ptr_i = 0
        c.pm = [kb.psum("pm%d" % i, [128, 512], F32) for i in range(4)]

    dummy = [Buf(None, "d%d" % i) for i in range(NT)]
    y_tiles = [Buf(None, "y%d" % i) for i in range(NT)]
    x_fn = lambda ti: [(ins["x"][ti * 128:(ti + 1) * 128, :], 0, D, dummy[ti])]

    def x1_fn(ti):
        st, j = ti // 4, ti % 4
        return [(scr["X1g"][st].ap()[r * 512 + j * 128:r * 512 + (j + 1) * 128, :], r * HW, HW, scr["X1g_b"][st]) for r in range(2)]

    def proj_tiles(gain_name):
        psum_banks(2, 2)
        setup_common(kb, c)
        kb.dma("sp", c.gb[:, :], ins[gain_name][:, :], [], [c.gb])

    setup_wconv(kb, c, ins)
    issue_conv(kb, c, 10)
    with contextlib.ExitStack() as pes:
        kb.es = pes
        proj_tiles("g0b")
        setup_hgrn(kb, c, ins)
        hgrn_layer(kb, c, ins, scr, dummy, x_fn)
        issue_conv(kb, c, 1000)
        kb.barrier()
    kb.es = es
    with contextlib.ExitStack() as pes:
        kb.es = pes
        proj_tiles("g1b")
        nsa_proj_phase(kb, c, ins, scr, dummy, x1_fn)
        kb.barrier()
    kb.es = es
    with contextlib.ExitStack() as pes:
        kb.es = pes
        psum_banks(3, 1)
        setup_attn(kb, c, ins, scr)
        for g in range(NGC):
            nsa_compress(kb, c, scr, g)
            nsa_attn_group(kb, c, ins, scr, g)
        kb.barrier()
    kb.es = es
    with contextlib.ExitStack() as pes:
        kb.es = pes
        proj_tiles("gfb")
        nsa_out_phase(kb, c, ins, scr, out, y_tiles)
        kb.barrier()
    kb.es = es
    kb.barrier()
    es.close()
    print("n_inst", kb.n_inst)
    return nc


_NC = None


def kernel(**inputs):
    global _NC
    inputs = {k: np.asarray(v) for k, v in inputs.items()}
    if _NC is None:
        _NC = build()
    in_maps = [make_inputs(core // 2, core % 2, **inputs) for core in range(8)]
    res = run_bass_kernel_spmd(_NC, in_maps, core_ids=list(range(8)))
    out = np.empty((4, S, D), np.float32)
    for core in range(8):
        b, r = core // 2, core % 2
        out[b, :, r * HW:(r + 1) * HW] = np.asarray(res.results[core]["out"])
    return out
```

```python
import contextlib
import numpy as np
import concourse.bass as bass
import concourse.mybir as mybir
from concourse.bass_utils import run_bass_kernel_spmd

F32 = mybir.dt.float32
BF16 = mybir.dt.bfloat16
AF = mybir.ActivationFunctionType
ALU = mybir.AluOpType
AX = mybir.AxisListType

S = 4096
D = 2048
NT = S // 128
NST = S // 512
EPS = 1e-6
NEG = -30000.0
HW = D // 2
NGC = 2
PAIRS = [[0, 1], [2, 3], [4, 5], [6, 7]]


class Buf:
    __slots__ = ("t", "w", "r", "name")

    def __init__(self, t, name=""):
        self.t = t
        self.w = None
        self.r = {}
        self.name = name

    def __getitem__(self, idx):
        return self.t[idx]


class KB:
    def __init__(self, nc, es, nd=8):
        self.nc = nc
        self.es = es
        self.eng = {"pe": nc.tensor, "act": nc.scalar, "dve": nc.vector, "pool": nc.gpsimd, "sp": nc.sync}
        self.sems = {}
        self.cnt = {}
        for e in ["pe", "act", "dve", "pool"]:
            self.sems[e] = es.enter_context(nc.semaphore("s_" + e))
            self.cnt[e] = 0
        self.waited = {e: {} for e in self.eng}
        self.nd = nd
        self.dq = {}
        for q in ["sp", "pool"]:
            lst = []
            for i in range(nd):
                key = "d_%s%d" % (q, i)
                self.sems[key] = es.enter_context(nc.semaphore(key))
                lst.append([key, 0])
            self.dq[q] = [lst, 0]
        self.n_inst = 0
        self.cq = []
        for i in range(4):
            key = "cc_%d" % i
            self.sems[key] = es.enter_context(nc.semaphore(key))
            self.cq.append([key, 0])
        self.cq_i = 0

    def coll(self, src_t, dst_t, reads, writes):
        slot = self.cq[self.cq_i]
        self.cq_i = (self.cq_i + 1) % len(self.cq)
        deps = self._deps(reads, writes)
        if slot[1] > 0 and deps.get(slot[0], 0) < slot[1]:
            deps[slot[0]] = slot[1]
        self._wait("pool", deps)
        inst = self.nc.gpsimd.collective_compute("AllGather", ALU.bypass, replica_groups=PAIRS,
                                                 ins=[src_t.ap().opt()], outs=[dst_t.ap().opt()])
        slot[1] += 1
        inst.then_inc(self.sems[slot[0]], 1)
        self._mark((slot[0], slot[1]), reads, writes)
        self.n_inst += 1

    def tile(self, name, shape, dtype):
        self.n_tiles = getattr(self, "n_tiles", 0) + 1
        t = self.es.enter_context(self.nc.sbuf_tensor("sb%d_%s" % (self.n_tiles, name), list(shape), dtype))
        return Buf(t, name)

    def psum(self, name, shape, dtype):
        self.n_tiles = getattr(self, "n_tiles", 0) + 1
        t = self.es.enter_context(self.nc.psum_tensor("ps%d_%s" % (self.n_tiles, name), list(shape), dtype))
        return Buf(t, name)

    def dram(self, name, shape, dtype, kind="Internal"):
        t = self.nc.dram_tensor(name, list(shape), dtype, kind=kind)
        return Buf(t.ap(), name)

    def _deps(self, reads, writes):
        deps = {}

        def add(ev):
            if ev is None:
                return
            k, v = ev
            if deps.get(k, 0) < v:
                deps[k] = v

        for b in reads:
            add(b.w)
        for b in writes:
            add(b.w)
            for k, v in b.r.items():
                add((k, v))
        return deps

    def _wait(self, e, deps):
        w = self.waited[e]
        eo = self.eng[e]
        for k, v in deps.items():
            if e == "pe" and k == "pe":
                continue
            if w.get(k, 0) < v:
                eo.wait_ge(self.sems[k], v)
                w[k] = v

    def _mark(self, ev, reads, writes):
        k, v = ev
        for b in reads:
            if b.r.get(k, 0) < v:
                b.r[k] = v
        for b in writes:
            b.w = ev
            b.r = {}

    def op(self, e, fn, reads, writes):
        self._wait(e, self._deps(reads, writes))
        inst = fn(self.eng[e])
        self.cnt[e] += 1
        inst.then_inc(self.sems[e], 1)
        self._mark((e, self.cnt[e]), reads, writes)
        self.n_inst += 1

    def dma(self, q, out, in_, reads, writes, **kw):
        lst, nxt = self.dq[q]
        slot = lst[nxt]
        self.dq[q][1] = (nxt + 1) % self.nd
        deps = self._deps(reads, writes)
        if slot[1] > 0:
            if deps.get(slot[0], 0) < slot[1]:
                deps[slot[0]] = slot[1]
        self._wait(q, deps)
        inst = self.eng[q].dma_start(out=out, in_=in_, **kw)
        slot[1] += 16
        inst.then_inc(self.sems[slot[0]], 16)
        self._mark((slot[0], slot[1]), reads, writes)
        self.n_inst += 1

    def barrier(self):
        deps = {}
        for e in ["pe", "act", "dve", "pool"]:
            if self.cnt[e] > 0:
                deps[e] = self.cnt[e]
        for q in self.dq:
            for key, val in self.dq[q][0]:
                if val > 0:
                    deps[key] = val
        for key, val in self.cq:
            if val > 0:
                deps[key] = val
        for e in self.eng:
            d = dict(deps)
            if e in d:
                del d[e]
            w = self.waited[e]
            for k, v in d.items():
                if w.get(k, 0) < v:
                    self.eng[e].wait_ge(self.sems[k], v)
                    w[k] = v

    def wait_all(self, e, bufs):
        deps = {}
        for b in bufs:
            if b.w is not None:
                k, v = b.w
                if deps.get(k, 0) < v:
                    deps[k] = v
        self._wait(e, deps)

    def mm(self, out_b, out_ap, l_b, l_ap, r_b, r_ap, start, stop, **kw):
        self.op("pe", lambda pe: pe.matmul(out_ap, l_ap, r_ap, start=start, stop=stop, **kw), [l_b, r_b], [out_b])

    def tr(self, out_b, out_ap, in_b, in_ap, ident):
        self.op("pe", lambda pe: pe.transpose(out_ap, in_ap, ident[:]), [in_b, ident], [out_b])

    def act(self, out_b, out_ap, in_b, in_ap, func, extra_reads=(), **kw):
        self.op("act", lambda a: a.activation(out=out_ap, in_=in_ap, func=func, **kw), [in_b] + list(extra_reads), [out_b])

    def tt(self, e, out_b, out_ap, a_b, a_ap, b_b, b_ap, op):
        self.op(e, lambda v: v.tensor_tensor(out=out_ap, in0=a_ap, in1=b_ap, op=op), [a_b, b_b], [out_b])

    def ts(self, e, out_b, out_ap, a_b, a_ap, s1, s2, op0, op1=None, extra_reads=()):
        if op1 is None:
            fn = lambda v: v.tensor_scalar(out=out_ap, in0=a_ap, scalar1=s1, scalar2=None, op0=op0)
        else:
            fn = lambda v: v.tensor_scalar(out=out_ap, in0=a_ap, scalar1=s1, scalar2=s2, op0=op0, op1=op1)
        self.op(e, fn, [a_b] + list(extra_reads), [out_b])

    def stt(self, out_b, out_ap, a_b, a_ap, scalar, b_b, b_ap, op0, op1, extra_reads=()):
        self.op("dve", lambda v: v.scalar_tensor_tensor(out=out_ap, in0=a_ap, scalar=scalar, in1=b_ap, op0=op0, op1=op1),
                [a_b, b_b] + list(extra_reads), [out_b])

    def copy(self, e, out_b, out_ap, in_b, in_ap):
        if e == "act":
            self.op("act", lambda a: a.copy(out=out_ap, in_=in_ap), [in_b], [out_b])
        else:
            self.op(e, lambda v: v.tensor_copy(out=out_ap, in_=in_ap), [in_b], [out_b])


class Ctx:
    pass


def setup_common(kb, c):
    c.xt = kb.tile("xt", [128, D], F32)
    c.hb = kb.tile("hb", [128, D], BF16)
    c.st1 = kb.tile("st1", [128, 8], F32)
    c.hT = kb.tile("hT", [128, 16, 512], BF16)
    c.ogT = kb.tile("ogT", [128, 16, 512], BF16)
    c.wb = [kb.tile("wb%d" % i, [128, 16, 512], BF16) for i in range(2)]
    c.wb_i = 0
    c.gb = kb.tile("gb", [128, D], F32)
    c.og = kb.tile("og", [128, 4, HW], BF16)
    c.ogf = kb.tile("ogf", [128, 4, D], BF16)
    c.xs = [kb.tile("xs%d" % i, [128, 512], F32) for i in range(2)]
    c.xo = [kb.tile("xo%d" % i, [128, 512], F32) for i in range(2)]
    c.xs_i = 0


def next_pacc(c):
    p = c.pacc[c.pacc_i]
    c.pacc_i = (c.pacc_i + 1) % len(c.pacc)
    return p


def next_ptr(c):
    p = c.ptr[c.ptr_i]
    c.ptr_i = (c.ptr_i + 1) % len(c.ptr)
    return p


def load_norm_transpose(kb, c, x_tiles, x_ap_fn, st, gain_b):
    for j in range(4):
        ti = st * 4 + j
        for (ap, col0, ncols, rb) in x_ap_fn(ti):
            kb.dma("sp", c.xt[:, col0:col0 + ncols], ap, [rb], [c.xt])
        kb.act(c.hb, c.hb[:, :], c.xt, c.xt[:, :], AF.Square, extra_reads=[], accum_out=c.st1[:, 0:1])
        kb.op("act", lambda a: a.activation(out=c.st1[:, 1:2], in_=c.st1[:, 0:1], func=AF.Sqrt, bias=c.epsb[:, 0:1], scale=1.0 / D),
              [c.hb, c.st1, c.epsb], [c.st1])
        kb.op("dve", lambda v: v.reciprocal(out=c.st1[:, 2:3], in_=c.st1[:, 1:2]), [c.st1], [c.st1])
        kb.stt(c.hb, c.hb[:, :], c.xt, c.xt[:, :], c.st1[:, 2:3], gain_b, gain_b[:, :], ALU.mult, ALU.mult, extra_reads=[c.st1])
        transpose_16(kb, c, c.hb, lambda kc: c.hb[:, kc * 128:(kc + 1) * 128], c.hT, j)


def transpose_16(kb, c, src_b, src_fn, dst_b, j):
    for half in range(2):
        p = next_ptr(c)
        for i in range(8):
            kc = half * 8 + i
            kb.tr(p, p[:, i * 128:(i + 1) * 128], src_b, src_fn(kc), c.ident)
        eng = "act" if half == 0 else "dve"
        kb.copy(eng, dst_b, dst_b[:, half * 8:(half + 1) * 8, j * 128:(j + 1) * 128],
                p, p[:, :].rearrange("p (a b) -> p a b", b=128))


WBLOCKS = {
    "hgrn_w_in": [(i * 512, 512) for i in range(8)],
    "hgrn_w_out": [(i * 512, 512) for i in range(2)],
    "nsa_w_in": [(0, 512), (512, 512), (1024, 512), (1536, 512), (2048, 512), (2560, 24), (2584, 512), (3096, 512)],
    "nsa_w_out": [(i * 512, 512) for i in range(2)],
}


def setup_wconv(kb, c, ins):
    c.WB = {}
    c.WBbuf = {}
    c.conv_q = []
    for wn, blks in WBLOCKS.items():
        c.WB[wn] = kb.dram("WB_" + wn, [len(blks), 128, 16, 512], BF16)
        c.WBbuf[wn] = [Buf(None, "%s_%d" % (wn, i)) for i in range(len(blks))]
        w3 = ins[wn][:, :].rearrange("(kc p) n -> p kc n", p=128)
        order = list(range(len(blks)))
        if wn == "hgrn_w_in":
            order = [sec * NGC + g for g in range(NGC) for sec in range(4)]
        for i in order:
            col0, ncols = blks[i]
            c.conv_q.append((wn, i, w3[:, :, col0:col0 + ncols], ncols))


def issue_conv(kb, c, n):
    for _ in range(n):
        if not c.conv_q:
            return
        wn, i, src, ncols = c.conv_q.pop(0)
        kb.dma("pool", c.WB[wn][i, :, :, 0:ncols], src, [], [c.WBbuf[wn][i]])


def load_w_block(kb, c, wn, i):
    ncols = WBLOCKS[wn][i][1]
    wb = c.wb[c.wb_i]
    c.wb_i = (c.wb_i + 1) % len(c.wb)
    kb.dma("sp", wb[:, :, 0:ncols], c.WB[wn][i, :, :, 0:ncols], [c.WBbuf[wn][i]], [wb])
    return wb


def proj_feature_major(kb, c, wb, cc, ncols_tok=512):
    p = next_pacc(c)
    for kc in range(16):
        kb.mm(p, p[:, 0:ncols_tok], wb, wb[:, kc, cc * 128:(cc + 1) * 128], c.hT, c.hT[:, kc, 0:ncols_tok], kc == 0, kc == 15)
    return p


def proj_token_major(kb, c, wb, j, ncols=512, aT=None):
    aT = aT or c.hT
    p = next_pacc(c)
    for kc in range(16):
        kb.mm(p, p[:, 0:ncols], aT, aT[:, kc, j * 128:(j + 1) * 128], wb, wb[:, kc, 0:ncols], kc == 0, kc == 15)
    return p


def load_gathered_og(kb, c, gath, st_buf, dstT):
    for j in range(4):
        for r in range(2):
            kb.dma("sp", c.ogf[:, j, r * HW:(r + 1) * HW], gath.ap()[r * 512 + j * 128:r * 512 + (j + 1) * 128, :], [st_buf], [c.ogf])
    for j in range(4):
        transpose_16(kb, c, c.ogf, lambda kc, j=j: c.ogf[:, j, kc * 128:(kc + 1) * 128], dstT, j)


def hgrn_out(kb, c, ins, scr, st):
    load_gathered_og(kb, c, scr["OG0g"][st], scr["OG0g_b"][st], c.ogT)
    for nb in range(2):
        wb = load_w_block(kb, c, "hgrn_w_out", nb)
        for j in range(4):
            ti = st * 4 + j
            p = proj_token_major(kb, c, wb, j, aT=c.ogT)
            xs = c.xs[c.xs_i]
            xo = c.xo[c.xs_i]
            c.xs_i ^= 1
            kb.dma("sp", xs[:, :], ins["xh"][ti * 128:(ti + 1) * 128, nb * 512:(nb + 1) * 512], [], [xs])
            kb.tt("dve", xo, xo[:, :], p, p[:, :], xs, xs[:, :], ALU.add)
            kb.dma("pool", scr["X1own"][ti * 128:(ti + 1) * 128, nb * 512:(nb + 1) * 512], xo[:, :], [xo], [])
            kb.dma("pool", scr["X1src"][st].ap()[j * 128:(j + 1) * 128, nb * 512:(nb + 1) * 512], xo[:, :], [xo], [scr["X1src_b"][st]])
    kb.coll(scr["X1src"][st], scr["X1g"][st], [scr["X1src_b"][st]], [scr["X1g_b"][st]])


def setup_hgrn(kb, c, ins):
    c.lbT = kb.tile("lbT", [128, 3, 4 * NGC], F32)
    c.lbw = kb.tile("lbw", [128, 6, 4 * NGC], F32)
    c.G4 = kb.tile("G4", [128, 512], F32)
    c.M4 = kb.tile("M4", [128, 4, 128], F32)
    c.rmask = kb.tile("rmask", [128, 512], F32)
    c.qs = kb.tile("qs", [128, 4, 512], F32)
    c.tmp = [kb.tile("htmp%d" % i, [128, 512], F32) for i in range(9)]
    c.qin = kb.tile("qin", [128, 4, 512], BF16)
    c.qmid = kb.tile("qmid", [128, 4, 512], BF16)
    c.kmid = kb.tile("kmid", [128, 4, 512], BF16)
    c.kend = kb.tile("kend", [128, 4, 512], BF16)
    c.kendT = kb.tile("kendT", [128, 4, 512], BF16)
    c.Vg = kb.tile("Vg", [128, 4, 512], BF16)
    c.gz = kb.tile("gz", [128, 4, 512], F32)
    c.dec = kb.tile("dec", [128, 4, 4], F32)
    c.Sst = kb.tile("Sst", [128, 4 * NGC, 128], F32)
    c.SbfA = [kb.tile("SbfA%d" % i, [128, 4, 128], BF16) for i in range(4)]
    c.PT4 = [kb.tile("PT%d" % i, [128, 4, 128], BF16) for i in range(4)]
    c.dS = [kb.tile("dS%d" % i, [128, 512], F32) for i in range(4)]
    c.sq2 = [kb.tile("sq2_%d" % i, [128, 512], F32) for i in range(2)]
    c.st22 = [kb.tile("st22_%d" % i, [128, 12], F32) for i in range(2)]
    c.stmp = kb.tile("stmp", [128, 512], F32)

    kb.dma("sp", c.lbT[:, :, :], ins["lbT"][:, :, :], [ins["lbT"]], [c.lbT])
    kb.dma("sp", c.G4[:, :], ins["G4"][:, :], [ins["G4"]], [c.G4])
    kb.dma("sp", c.M4[:, :, :], ins["M4"][:, :, :], [ins["M4"]], [c.M4])
    kb.dma("sp", c.rmask[:, :], ins["rmask"][:, :], [ins["rmask"]], [c.rmask])
    w = c.lbw
    kb.tt("dve", w, w[:, 3, :], c.lbT, c.lbT[:, 0, :], c.lbT, c.lbT[:, 1, :], ALU.max)
    kb.tt("dve", w, w[:, 3, :], w, w[:, 3, :], c.lbT, c.lbT[:, 2, :], ALU.max)
    for r in range(3):
        kb.tt("dve", c.lbT, c.lbT[:, r, :], c.lbT, c.lbT[:, r, :], w, w[:, 3, :], ALU.subtract)
    kb.act(c.lbT, c.lbT[:, :, :], c.lbT, c.lbT[:, :, :], AF.Exp)
    kb.tt("dve", w, w[:, 4, :], c.lbT, c.lbT[:, 0, :], c.lbT, c.lbT[:, 1, :], ALU.add)
    kb.tt("dve", w, w[:, 4, :], w, w[:, 4, :], c.lbT, c.lbT[:, 2, :], ALU.add)
    kb.op("dve", lambda v: v.reciprocal(out=w[:, 5, :], in_=w[:, 4, :]), [w], [w])
    kb.tt("dve", w, w[:, 0, :], c.lbT, c.lbT[:, 0, :], w, w[:, 5, :], ALU.mult)
    kb.ts("dve", w, w[:, 1, :], w, w[:, 0, :], -1.0, 1.0, ALU.mult, ALU.add)
    kb.ts("dve", w, w[:, 2, :], w, w[:, 1, :], -1.0, None, ALU.mult)
    kb.op("dve", lambda v: v.memset(c.Sst[:, :, :], 0.0), [], [c.Sst])
    for i in range(4):
        kb.op("dve", lambda v, i=i: v.memset(c.PT4[i][:, :, :], 0.0), [], [c.PT4[i]])


def hgrn_layer(kb, c, ins, scr, x_tiles, x_ap_fn, nst=NST):
    lbw = c.lbw
    T = c.tmp
    for st in range(nst):
        issue_conv(kb, c, 4)
        load_norm_transpose(kb, c, x_tiles, x_ap_fn, st, c.gb)
        for g in range(NGC):
            wb = load_w_block(kb, c, "hgrn_w_in", 0 * NGC + g)
            for cc in range(4):
                p = proj_feature_major(kb, c, wb, cc)
                kb.op("act", lambda a, p=p, cc=cc: a.mul(out=c.qs[:, cc, :], in_=p[:, :], mul=128.0 ** -0.5), [p], [c.qs])
            wb = load_w_block(kb, c, "hgrn_w_in", 1 * NGC + g)
            for cc in range(4):
                hd = 4 * g + cc
                p = proj_feature_major(kb, c, wb, cc)
                sig, lf, bb, nbb, kk, e1, e2, e3, e4 = T
                kb.act(sig, sig[:, :], p, p[:, :], AF.Sigmoid)
                kb.act(lf, lf[:, :], sig, sig[:, :], AF.Ln, extra_reads=[lbw], bias=lbw[:, 0, hd:hd + 1], scale=lbw[:, 1, hd:hd + 1])
                kb.op("dve", lambda v: v.tensor_tensor_scan(out=bb[:, :], data0=c.rmask[:, :], data1=lf[:, :], initial=0.0,
                                                            op0=ALU.mult, op1=ALU.add), [c.rmask, lf], [bb])
                kb.ts("dve", nbb, nbb[:, :], bb, bb[:, :], -1.0, None, ALU.mult)
                kb.ts("pool", kk, kk[:, :], sig, sig[:, :], lbw[:, 2, hd:hd + 1], lbw[:, 1, hd:hd + 1], ALU.mult, ALU.add, extra_reads=[lbw])
                kb.act(e1, e1[:, :], bb, bb[:, :], AF.Exp)
                for j in range(4):
                    sl = slice(j * 128, (j + 1) * 128)
                    mid = j * 128 + 63
                    last = j * 128 + 127
                    kb.act(e2, e2[:, sl], bb, bb[:, sl], AF.Exp, extra_reads=[nbb], bias=nbb[:, mid:mid + 1], scale=1.0)
                    kb.act(e3, e3[:, sl], bb, bb[:, sl], AF.Exp, extra_reads=[], bias=bb[:, mid:mid + 1], scale=-1.0)
                    kb.act(e4, e4[:, sl], bb, bb[:, sl], AF.Exp, extra_reads=[], bias=bb[:, last:last + 1], scale=-1.0)
                kb.copy("pool", c.dec, c.dec[:, cc, :], e1, e1[:, 127::128])
                kb.tt("dve", c.qin, c.qin[:, cc, :], c.qs, c.qs[:, cc, :], e1, e1[:, :], ALU.mult)
                kb.tt("dve", c.qmid, c.qmid[:, cc, :], c.qs, c.qs[:, cc, :], e2, e2[:, :], ALU.mult)
                kb.tt("pool", c.kmid, c.kmid[:, cc, :], kk, kk[:, :], e3, e3[:, :], ALU.mult)
                kb.tt("pool", c.kend, c.kend[:, cc, :], kk, kk[:, :], e4, e4[:, :], ALU.mult)
            wb = load_w_block(kb, c, "hgrn_w_in", 2 * NGC + g)
            for j in range(4):
                p = proj_token_major(kb, c, wb, j)
                kb.copy("act", c.Vg, c.Vg[:, j, :], p, p[:, :])
            wb = load_w_block(kb, c, "hgrn_w_in", 3 * NGC + g)
            for j in range(4):
                p = proj_token_major(kb, c, wb, j)
                kb.act(c.stmp, c.stmp[:, :], p, p[:, :], AF.Silu)
                kb.tt("dve", c.gz, c.gz[:, j, :], c.stmp, c.stmp[:, :], c.G4, c.G4[:, :], ALU.mult)
            for half in range(2):
                pt = next_ptr(c)
                for i in range(8):
                    idx = half * 8 + i
                    j, cc = idx // 4, idx % 4
                    kb.tr(pt, pt[:, i * 128:(i + 1) * 128], c.kend, c.kend[:, cc, j * 128:(j + 1) * 128], c.ident)
                kb.copy("act", c.kendT, c.kendT[:, half * 2:(half + 1) * 2, :], pt,
                        pt[:, :].rearrange("p (a b) -> p a b", b=512))
            ps_s, ps_d = c.pm[0], c.pm[2]
            for j in range(4):
                PT = c.PT4[j]
                t0 = j * 128
                for cc in range(4):
                    kb.mm(ps_s, ps_s[:, cc * 128 + 64:cc * 128 + 128], c.kmid, c.kmid[:, cc, t0:t0 + 128],
                          c.qmid, c.qmid[:, cc, t0 + 64:t0 + 128], True, True)
                    kb.mm(ps_s, ps_s[0:64, cc * 128:cc * 128 + 64], c.kmid, c.kmid[:, cc, t0:t0 + 64],
                          c.qmid, c.qmid[:, cc, t0:t0 + 64], True, True)
                ps3 = ps_s[:, :].rearrange("p (a b) -> p a b", b=128)
                kb.tt("dve", PT, PT[:, :, 64:128], ps_s, ps3[:, :, 64:128], c.M4, c.M4[:, :, 64:128], ALU.mult)
                ps3b = ps_s[0:64, :].rearrange("p (a b) -> p a b", b=128)
                kb.tt("dve", PT, PT[0:64, :, 0:64], ps_s, ps3b[:, :, 0:64], c.M4, c.M4[0:64, :, 0:64], ALU.mult)
                for cc in range(4):
                    kb.mm(ps_d, ps_d[:, cc * 128:(cc + 1) * 128], c.kendT, c.kendT[:, j, cc * 128:(cc + 1) * 128],
                          c.Vg, c.Vg[:, j, cc * 128:(cc + 1) * 128], True, True)
                kb.copy("act", c.dS[j], c.dS[j][:, :], ps_d, ps_d[:, :])
            S4 = c.Sst[:, 4 * g:4 * g + 4, :]
            kb.copy("act", c.SbfA[0], c.SbfA[0][:, :, :], c.Sst, S4)
            for j in range(4):
                for cc in range(4):
                    hd = 4 * g + cc
                    kb.stt(c.Sst, c.Sst[:, hd, :], c.Sst, c.Sst[:, hd, :], c.dec[:, cc, j:j + 1], c.dS[j], c.dS[j][:, cc * 128:(cc + 1) * 128],
                           ALU.mult, ALU.add, extra_reads=[c.dec])
                if j < 3:
                    kb.copy("act", c.SbfA[j + 1], c.SbfA[j + 1][:, :, :], c.Sst, S4)
            for j in range(4):
                PT = c.PT4[j]
                t0 = j * 128
                ps_o = c.pm[1] if j % 2 == 0 else c.pm[3]
                for cc in range(4):
                    kb.mm(ps_o, ps_o[:, cc * 128:(cc + 1) * 128], PT, PT[:, cc, :], c.Vg, c.Vg[:, j, cc * 128:(cc + 1) * 128], True, False)
                    kb.mm(ps_o, ps_o[:, cc * 128:(cc + 1) * 128], c.qin, c.qin[:, cc, t0:t0 + 128], c.SbfA[j], c.SbfA[j][:, cc, :], False, True)
                sq = c.sq2[j % 2]
                st2 = c.st22[j % 2]
                kb.act(sq, sq[:, :], ps_o, ps_o[:, :], AF.Square)
                kb.op("dve", lambda v, sq=sq, st2=st2: v.tensor_reduce(out=st2[:, 0:4], in_=sq[:, :].rearrange("p (a b) -> p a b", b=128),
                                                                     axis=AX.X, op=ALU.add), [sq], [st2])
                kb.op("act", lambda a, st2=st2: a.activation(out=st2[:, 4:8], in_=st2[:, 0:4], func=AF.Sqrt, bias=c.epsb[:, 0:1], scale=1.0 / 128),
                      [st2, c.epsb], [st2])
                kb.op("dve", lambda v, st2=st2: v.reciprocal(out=st2[:, 8:12], in_=st2[:, 4:8]), [st2], [st2])
                for cc in range(4):
                    hd = 4 * g + cc
                    kb.stt(c.og, c.og[:, j, hd * 128:(hd + 1) * 128], ps_o, ps_o[:, cc * 128:(cc + 1) * 128], st2[:, 8 + cc:9 + cc],
                           c.gz, c.gz[:, j, cc * 128:(cc + 1) * 128], ALU.mult, ALU.mult, extra_reads=[st2])
        for j in range(4):
            kb.dma("pool", scr["OG0src"][st].ap()[j * 128:(j + 1) * 128, :], c.og[:, j, :], [c.og], [scr["OG0src_b"][st]])
        kb.coll(scr["OG0src"][st], scr["OG0g"][st], [scr["OG0src_b"][st]], [scr["OG0g_b"][st]])
        if st >= 1:
            hgrn_out(kb, c, ins, scr, st - 1)
    hgrn_out(kb, c, ins, scr, nst - 1)


SCALE = 128.0 ** -0.5
BIG = 30000.0


def rot(c, name, n):
    lst = getattr(c, name)
    i = getattr(c, name + "_i", 0)
    setattr(c, name + "_i", (i + 1) % n)
    return lst[i]


def nsa_proj_phase(kb, c, ins, scr, x_tiles, x_fn):
    c.obf = [kb.tile("obf%d" % i, [128, 512], BF16) for i in range(4)]
    c.of32 = [kb.tile("of32_%d" % i, [128, 512], F32) for i in range(3)]
    for st in range(NST):
        load_norm_transpose(kb, c, x_tiles, x_fn, st, c.gb)
        t0 = st * 512
        for nb in range(NGC):
            wb = load_w_block(kb, c, "nsa_w_in", nb)
            for cc in range(4):
                p = proj_feature_major(kb, c, wb, cc)
                ob = rot(c, "obf", 4)
                kb.op("act", lambda a, p=p, ob=ob: a.mul(out=ob[:, :], in_=p[:, :], mul=SCALE), [p], [ob])
                kb.dma("pool", scr["QT"][nb * 4 + cc, :, t0:t0 + 512], ob[:, :], [ob], [])
        for nb, dsts in [(2, ("KCT", "VCT")), (3, ("KST", "KWT"))]:
            wb = load_w_block(kb, c, "nsa_w_in", nb)
            for cc in range(4):
                p = proj_feature_major(kb, c, wb, cc)
                ob = rot(c, "obf", 4)
                kb.copy("act", ob, ob[:, :], p, p[:, :])
                kb.dma("pool", scr[dsts[cc // 2]][cc % 2, :, t0:t0 + 512], ob[:, :], [ob], [])
        wb = load_w_block(kb, c, "nsa_w_in", 4)
        for j in range(4):
            p = proj_token_major(kb, c, wb, j)
            ob = rot(c, "obf", 4)
            kb.copy("act", ob, ob[:, :], p, p[:, :])
            kb.dma("pool", scr["VS"][t0 + j * 128:t0 + (j + 1) * 128, :], ob[:, 0:256], [ob], [])
            kb.dma("pool", scr["VW"][t0 + j * 128:t0 + (j + 1) * 128, :], ob[:, 256:512], [ob], [])
        wb = load_w_block(kb, c, "nsa_w_in", 5)
        for j in range(4):
            p = proj_token_major(kb, c, wb, j, ncols=24)
            of = rot(c, "of32", 3)
            kb.act(of, of[:, 0:24], p, p[:, 0:24], AF.Sigmoid)
            kb.dma("pool", scr["GATE"][t0 + j * 128:t0 + (j + 1) * 128, :], of[:, 0:24], [of], [])
        for i in range(NGC):
            wb = load_w_block(kb, c, "nsa_w_in", 6 + i)
            for j in range(4):
                p = proj_token_major(kb, c, wb, j)
                of = rot(c, "of32", 3)
                kb.act(of, of[:, :], p, p[:, :], AF.Silu)
                kb.dma("pool", scr["SZ"][t0 + j * 128:t0 + (j + 1) * 128, i * 512:(i + 1) * 512], of[:, :], [of], [])


def setup_attn(kb, c, ins, scr):
    c.KST = kb.tile("KST", [128, S], BF16)
    c.KWT = kb.tile("KWT", [128, S], BF16)
    c.VSe = kb.tile("VSe", [128, NT, 129], BF16)
    c.VWe = kb.tile("VWe", [128, NT, 129], BF16)
    c.KcT = kb.tile("KcT", [128, 256], BF16)
    c.Vce = kb.tile("Vce", [128, 2, 193], BF16)
    c.BT = [kb.tile("BT%d" % i, [128, 512], F32) for i in range(3)]
    c.q4s = [kb.tile("q4_%d" % i, [128, 4, 128], BF16) for i in range(5)]
    c.PTs = [kb.tile("PTa%d" % i, [128, 512], BF16) for i in range(19)]
    c.sbs = [kb.tile("sbs%d" % i, [128, 512], F32) for i in range(2)]
    c.BCs = [kb.tile("BC%d" % i, [128, 512], F32) for i in range(2)]
    c.accs = [kb.tile("acc%d" % i, [128, 4, 128], F32) for i in range(2)]
    c.obr = kb.tile("obr", [128, 4, 193], F32)
    c.szs = [kb.tile("sz%d" % i, [128, 512], F32) for i in range(5)]
    c.ogts = [kb.tile("ogt%d" % i, [128, 512], BF16) for i in range(2)]
    c.gates = kb.tile("gates", [128, NT, 24], F32)
    c.expand = kb.tile("expand", [64, NT, 128], BF16)
    c.seladd = kb.tile("seladd", [128, NT, 64], F32)
    c.ncb = kb.tile("ncb", [128, 8], F32)
    c.sm = kb.tile("sm", [128, 16], F32)
    c.imp = kb.tile("imp", [128, 64], F32)
    c.score = kb.tile("score", [128, 64], F32)
    c.work = kb.tile("work", [128, 64], F32)
    c.m8 = kb.tile("m8", [128, 16], F32)
    c.sel = kb.tile("sel", [128, 64], BF16)
    c.nselTs = [kb.tile("nselT%d" % i, [64, 512], BF16) for i in range(2)]
    c.w1b = [kb.tile("w1b%d" % i, [128, 32, 128], BF16) for i in range(2)]
    c.w2b = [kb.tile("w2b%d" % i, [128, 128], BF16) for i in range(2)]
    c.peTb = [kb.tile("peTb%d" % i, [128, 32], BF16) for i in range(2)]
    c.cbias = [kb.tile("cbias%d" % i, [128, 1], F32) for i in range(2)]
    c.kct = kb.tile("kct", [128, S], BF16)
    c.cu = [kb.tile("cu%d" % i, [128, 256], F32) for i in range(4)]
    c.GT = kb.tile("GT", [128, 256], BF16)
    c.ovb = kb.tile("ovb", [128, 2, 65], F32)

    kb.dma("sp", c.gates[:, :, :], scr["GATE"][:, :].rearrange("(j p) c -> p j c", p=128), [], [c.gates])
    kb.dma("sp", c.seladd[:, :, :], ins["seladd"][:, :, :], [], [c.seladd])
    kb.dma("sp", c.ncb[:, :], ins["cbb"][:, :], [], [c.ncb])
    kb.ts("dve", c.ncb, c.ncb[:, :], c.ncb, c.ncb[:, :], -1.0, None, ALU.mult)
    kb.dma("sp", c.ovb[:, :, :], ins["ov"][:, :, :], [], [c.ovb])
    kb.copy("dve", c.Vce, c.Vce[:, :, 128:193], c.ovb, c.ovb[:, :, :])
    kb.op("dve", lambda v: v.memset(c.VSe[:, :, 128:129], 1.0), [], [c.VSe])
    kb.op("dve", lambda v: v.memset(c.VWe[:, :, 128:129], 1.0), [], [c.VWe])
    kb.op("dve", lambda v: v.memset(c.GT[:, :], 0.0), [], [c.GT])
    with contextlib.ExitStack() as tes:
        old = kb.es
        kb.es = tes
        stg = kb.tile("stg_big", [128, 4096], F32)
        kb.dma("sp", stg[0:64, :], ins["expand"][:, :], [], [stg])
        kb.copy("dve", c.expand, c.expand[:, :, :].rearrange("p a b -> p (a b)"), stg, stg[0:64, :])
        for i, nm in enumerate(["k", "v"]):
            kb.dma("sp", stg[:, :].rearrange("p (j h) -> p j h", h=128),
                   ins["nsa_phi_%s_w1" % nm][:, :].rearrange("(j d) h -> d j h", d=128), [], [stg])
            kb.copy("dve", c.w1b[i], c.w1b[i][:, :, :].rearrange("p a b -> p (a b)"), stg, stg[:, :])
            kb.dma("sp", stg[:, 0:128], ins["nsa_phi_%s_w2" % nm][:, :], [], [stg])
            kb.copy("dve", c.w2b[i], c.w2b[i][:, :], stg, stg[:, 0:128])
            kb.dma("sp", stg[:, 0:32], ins["peT_%s" % nm][:, :], [], [stg])
            kb.copy("dve", c.peTb[i], c.peTb[i][:, :], stg, stg[:, 0:32])
            kb.dma("sp", stg[:, 0:1], ins["b1_%s" % nm][:, :], [], [stg])
            p = c.pm[0]
            for j in range(32):
                kb.mm(p, p[:, 0:1], c.w1b[i], c.w1b[i][:, j, :], c.peTb[i], c.peTb[i][:, j:j + 1], j == 0, j == 31)
            kb.tt("dve", c.cbias[i], c.cbias[i][:, :], p, p[:, 0:1], stg, stg[:, 0:1], ALU.add)
        kb.barrier()
        kb.es = old


def nsa_compress(kb, c, scr, g):
    for i, nm in enumerate(["KCT", "VCT"]):
        kb.dma("sp", c.kct[:, :], scr[nm][g, :, :], [], [c.kct])
        p = next_pacc(c)
        for j in range(32):
            kb.mm(p, p[:, 0:255], c.w1b[i], c.w1b[i][:, j, :], c.kct, c.kct[:, j:j + 4065:16], j == 0, j == 31)
        u, u2, inner, th = c.cu
        kb.act(u, u[:, 0:255], p, p[:, 0:255], AF.Identity, extra_reads=[c.cbias[i]], bias=c.cbias[i][:, 0:1], scale=1.0)
        kb.tt("dve", u2, u2[:, 0:255], u, u[:, 0:255], u, u[:, 0:255], ALU.mult)
        kb.ts("dve", u2, u2[:, 0:255], u2, u2[:, 0:255], 0.044715, 1.0, ALU.mult, ALU.add)
        kb.tt("dve", inner, inner[:, 0:255], u2, u2[:, 0:255], u, u[:, 0:255], ALU.mult)
        kb.act(th, th[:, 0:255], inner, inner[:, 0:255], AF.Tanh, scale=0.7978845608028654)
        kb.stt(inner, inner[:, 0:255], th, th[:, 0:255], 1.0, u, u[:, 0:255], ALU.add, ALU.mult)
        kb.op("act", lambda a: a.mul(out=c.GT[:, 0:255], in_=inner[:, 0:255], mul=0.5), [inner], [c.GT])
        if i == 0:
            p2 = next_pacc(c)
            kb.mm(p2, p2[:, 0:256], c.w2b[0], c.w2b[0][:, :], c.GT, c.GT[:, :], True, True)
            kb.copy("act", c.KcT, c.KcT[:, :], p2, p2[:, 0:256])
        else:
            for jm in range(2):
                p2 = next_pacc(c)
                kb.mm(p2, p2[:, 0:128], c.GT, c.GT[:, jm * 128:(jm + 1) * 128], c.w2b[1], c.w2b[1][:, :], True, True)
                kb.copy("act", c.Vce, c.Vce[:, jm, 0:128], p2, p2[:, 0:128])


def bias_exp(kb, c, ps, table, g, PT, shifted=False):
    sb = rot(c, "sbs", 2)
    if shifted:
        kb.tt("dve", sb, sb[:, :], ps, ps[:, :], table, table[:, :], ALU.add)
    else:
        for cc in range(4):
            h = 4 * g + cc
            kb.stt(sb, sb[:, cc * 128:(cc + 1) * 128], ps, ps[:, cc * 128:(cc + 1) * 128], c.ncb[:, h:h + 1],
                   table, table[:, cc * 128:(cc + 1) * 128], ALU.add, ALU.add, extra_reads=[c.ncb])
    kb.act(PT, PT[:, :], sb, sb[:, :], AF.Exp)


def branch_tail(kb, c, g, ti, br, width, first):
    ob = c.obr
    acc = c.accs[ti % 2]
    for cc in range(4):
        kb.copy("act", ob, ob[:, cc, 0:width], c.pm[cc], c.pm[cc][:, 0:width])
    zc = width - 1
    kb.ts("dve", c.sm, c.sm[:, 0:4], ob, ob[:, :, zc], 1e-30, None, ALU.max)
    kb.op("dve", lambda v: v.reciprocal(out=c.sm[:, 4:8], in_=c.sm[:, 0:4]), [c.sm], [c.sm])
    kb.tt("dve", c.sm, c.sm[:, 8:12], c.sm, c.sm[:, 4:8], c.gates, c.gates[:, ti, 12 * g + br:12 * g + 12:3], ALU.mult)
    if br == 0:
        kb.ts("dve", c.imp, c.imp[:, :], ob, ob[:, 0, 128:192], c.sm[:, 4:5], None, ALU.mult, extra_reads=[c.sm])
        for cc in range(1, 4):
            kb.stt(c.imp, c.imp[:, :], ob, ob[:, cc, 128:192], c.sm[:, 4 + cc:5 + cc], c.imp, c.imp[:, :],
                   ALU.mult, ALU.add, extra_reads=[c.sm])
    for cc in range(4):
        if first:
            kb.ts("dve", acc, acc[:, cc, :], ob, ob[:, cc, 0:128], c.sm[:, 8 + cc:9 + cc], None, ALU.mult, extra_reads=[c.sm])
        else:
            kb.stt(acc, acc[:, cc, :], ob, ob[:, cc, 0:128], c.sm[:, 8 + cc:9 + cc], acc, acc[:, cc, :],
                   ALU.mult, ALU.add, extra_reads=[c.sm])


def topk_select(kb, c, ti):
    kb.tt("dve", c.score, c.score[:, :], c.imp, c.imp[:, :], c.seladd, c.seladd[:, ti, :], ALU.add)
    kb.op("dve", lambda v: v.max(out=c.m8[:, 0:8], in_=c.score[:, :]), [c.score], [c.m8])
    kb.op("dve", lambda v: v.match_replace(out=c.work[:, :], in_to_replace=c.m8[:, 0:8], in_values=c.score[:, :], imm_value=-3.0e38),
          [c.m8, c.score], [c.work])
    kb.op("dve", lambda v: v.max(out=c.m8[:, 8:16], in_=c.work[:, :]), [c.work], [c.m8])
    kb.ts("dve", c.sel, c.sel[:, :], c.score, c.score[:, :], c.m8[:, 15:16], None, ALU.is_ge, extra_reads=[c.m8])
    if ti > 0:
        pt = c.ptr[0]
        for r in range(4):
            kb.tr(pt, pt[0:64, r * 128:(r + 1) * 128], c.sel, c.sel[:, :], c.ident)
        nselT = c.nselTs[ti % 2]
        kb.ts("dve", nselT, nselT[:, :], pt, pt[0:64, 0:512], -1.0, None, ALU.add)


def nsa_attn_group(kb, c, ins, scr, g, nt=NT):
    kb.dma("sp", c.KST[:, :], scr["KST"][g, :, :], [], [c.KST])
    kb.dma("sp", c.KWT[:, :], scr["KWT"][g, :, :], [], [c.KWT])
    kb.dma("sp", c.VSe[:, :, 0:128], scr["VS"][:, g * 128:(g + 1) * 128].rearrange("(j p) d -> p j d", p=128), [], [c.VSe])
    kb.dma("sp", c.VWe[:, :, 0:128], scr["VW"][:, g * 128:(g + 1) * 128].rearrange("(j p) d -> p j d", p=128), [], [c.VWe])
    for d in range(3):
        kb.dma("sp", c.BT[d][:, :], ins["BT"][d, g, :, :], [], [c.BT[d]])
        for cc in range(4):
            h = 4 * g + cc
            kb.ts("dve", c.BT[d], c.BT[d][:, cc * 128:(cc + 1) * 128], c.BT[d], c.BT[d][:, cc * 128:(cc + 1) * 128],
                  c.ncb[:, h:h + 1], None, ALU.add, extra_reads=[c.ncb])
    cmp_t, win_t, sel_t = {}, {}, {}

    def mk_task(lst, **kw):
        t = dict(pre=None, post=None, table=None, st={}, nselT=None)
        t.update(kw)
        lst.append(t)

    for ti in range(nt):
        st = {}
        cmp_t[ti], win_t[ti], sel_t[ti] = [], [], []

        def pre_tile(ti=ti, st=st):
            q4 = rot(c, "q4s", 5)
            kb.dma("sp", q4[:, :, :], scr["QT"][4 * g:4 * g + 4, :, ti * 128:(ti + 1) * 128].rearrange("h p t -> p h t"), [], [q4])
            sz = rot(c, "szs", 5)
            kb.dma("sp", sz[:, :], scr["SZ"][ti * 128:(ti + 1) * 128, g * 512:(g + 1) * 512], [], [sz])
            st["q4"] = q4
            st["sz"] = sz

        jms = [0] + ([1] if ti >= 16 else [])

        def post_cmp(ti=ti):
            branch_tail(kb, c, g, ti, 0, 193, True)
            topk_select(kb, c, ti)

        for idx, jm in enumerate(jms):
            near = ti < 17 + 16 * jm
            mk_task(cmp_t[ti], st=st, kT=(c.KcT, c.KcT[:, jm * 128:(jm + 1) * 128]), maskj=None,
                    table=("FULL", 128 * jm - 8 * ti + 8 + 240) if near else None,
                    v=(c.Vce, c.Vce[:, jm, :]), width=193, start=(idx == 0), stop=(idx == len(jms) - 1),
                    pre=pre_tile if idx == 0 else None,
                    post=post_cmp if idx == len(jms) - 1 else None)
        j0 = max(0, ti - 4)
        for j in range(j0, ti + 1):
            dl = ti - j
            tb = c.BT[dl] if dl <= 1 else (c.BT[2] if dl == 4 else None)
            mk_task(win_t[ti], st=st, kT=(c.KWT, c.KWT[:, j * 128:(j + 1) * 128]), maskj=None, table=tb,
                    v=(c.VWe, c.VWe[:, j, :]), width=129, start=(j == j0), stop=(j == ti),
                    post=(lambda ti=ti: branch_tail(kb, c, g, ti, 2, 129, False)) if j == ti else None)
        for j in range(ti + 1):
            dl = ti - j
            tb = c.BT[dl] if dl <= 1 else None

            def post_sel(ti=ti, st=st):
                branch_tail(kb, c, g, ti, 1, 129, False)
                acc = c.accs[ti % 2]
                ogt = rot(c, "ogts", 2)
                kb.tt("dve", ogt, ogt[:, :], acc, acc[:, :, :].rearrange("p a b -> p (a b)"), st["sz"], st["sz"][:, :], ALU.mult)
                sti = ti // 4
                kb.dma("pool", scr["OG1src"][sti].ap()[(ti % 4) * 128:(ti % 4 + 1) * 128, g * 512:(g + 1) * 512], ogt[:, :],
                       [ogt], [scr["OG1src_b"][sti]])
                if g == NGC - 1 and ti % 4 == 3:
                    kb.coll(scr["OG1src"][sti], scr["OG1g"][sti], [scr["OG1src_b"][sti]], [scr["OG1g_b"][sti]])

            mk_task(sel_t[ti], st=st, kT=(c.KST, c.KST[:, j * 128:(j + 1) * 128]), maskj=(j if j < ti else None), table=tb,
                    v=(c.VSe, c.VSe[:, j, :]), width=129, start=(j == 0), stop=(j == ti), nselT=c.nselTs[ti % 2],
                    post=post_sel if j == ti else None)

    tasks = list(cmp_t[0])
    for ti in range(nt):
        tasks += win_t[ti]
        if ti + 1 < nt:
            tasks += cmp_t[ti + 1]
        tasks += sel_t[ti]
    pos = {id(t): i for i, t in enumerate(tasks)}
    for ti in range(nt):
        sel_t[ti][0]["need_back"] = pos[id(cmp_t[ti][-1])]

    def emit_front(t):
        if t["pre"]:
            t["pre"]()
        q4 = t["st"]["q4"]
        q4f = q4[:, :, :].rearrange("p a b -> p (a b)")
        ps = next_pacc(c)
        kb.mm(ps, ps[:, :], t["kT"][0], t["kT"][1], q4, q4f, True, t["maskj"] is None)
        if t["maskj"] is not None:
            kb.mm(ps, ps[:, :], c.expand, c.expand[:, t["maskj"], :], t["nselT"], t["nselT"][:, :], False, True)
        PT = rot(c, "PTs", 19)
        tb = t["table"]
        if tb is None:
            kb.act(PT, PT[:, :], ps, ps[:, :], AF.Exp)
        elif isinstance(tb, tuple):
            BC = rot(c, "BCs", 2)
            kb.dma("sp", BC[:, :], ins["FULL"][g, tb[1]:tb[1] + 128, :], [], [BC])
            bias_exp(kb, c, ps, BC, g, PT)
        else:
            bias_exp(kb, c, ps, tb, g, PT, shifted=True)
        t["PT"] = PT

    def emit_back(t):
        PT = t["PT"]
        w = t["width"]
        for cc in range(4):
            kb.mm(c.pm[cc], c.pm[cc][:, 0:w], PT, PT[:, cc * 128:(cc + 1) * 128], t["v"][0], t["v"][1], t["start"], t["stop"])
        if t["post"]:
            t["post"]()

    n = len(tasks)
    DEPTH = 16
    nb = 0
    for i in range(n):
        need = max(i - DEPTH, tasks[i].get("need_back", -1))
        while nb <= need:
            emit_back(tasks[nb])
            nb += 1
        emit_front(tasks[i])
    while nb < n:
        emit_back(tasks[nb])
        nb += 1


def nsa_out_phase(kb, c, ins, scr, out, y_tiles, nst=NST):
    c.x2 = [kb.tile("x2_%d" % i, [128, 4, HW], F32) for i in range(2)]
    c.ssb = [kb.tile("ssb%d" % i, [128, 4], F32) for i in range(2)]
    c.ssg = kb.tile("ssg", [128, 2, 4], F32)
    c.rs = kb.tile("rs", [128, 12], F32)
    c.sqh = kb.tile("sqh", [128, HW], BF16)

    def finalize(st):
        x2 = c.x2[st % 2]
        kb.dma("sp", c.ssg[:, :, :], scr["SSg"][st].ap().rearrange("(r p) j -> p r j", p=128), [scr["SSg_b"][st]], [c.ssg])
        kb.tt("dve", c.rs, c.rs[:, 0:4], c.ssg, c.ssg[:, 0, :], c.ssg, c.ssg[:, 1, :], ALU.add)
        kb.op("act", lambda a: a.activation(out=c.rs[:, 4:8], in_=c.rs[:, 0:4], func=AF.Sqrt, bias=c.epsb[:, 0:1], scale=1.0 / D),
              [c.rs, c.epsb], [c.rs])
        kb.op("dve", lambda v: v.reciprocal(out=c.rs[:, 8:12], in_=c.rs[:, 4:8]), [c.rs], [c.rs])
        for j in range(4):
            ti = st * 4 + j
            kb.stt(c.xt, c.xt[:, 0:HW], x2, x2[:, j, :], c.rs[:, 8 + j:9 + j], c.gb, c.gb[:, 0:HW], ALU.mult, ALU.mult, extra_reads=[c.rs])
            kb.dma("pool", out[ti * 128:(ti + 1) * 128, :], c.xt[:, 0:HW], [c.xt], [y_tiles[ti]])

    for st in range(nst):
        x2 = c.x2[st % 2]
        ssb = c.ssb[st % 2]
        load_gathered_og(kb, c, scr["OG1g"][st], scr["OG1g_b"][st], c.hT)
        for nb in range(2):
            wb = load_w_block(kb, c, "nsa_w_out", nb)
            for j in range(4):
                ti = st * 4 + j
                p = proj_token_major(kb, c, wb, j)
                xs = c.xs[c.xs_i]
                c.xs_i ^= 1
                kb.dma("sp", xs[:, :], scr["X1own"][ti * 128:(ti + 1) * 128, nb * 512:(nb + 1) * 512], [], [xs])
                kb.tt("dve", x2, x2[:, j, nb * 512:(nb + 1) * 512], p, p[:, :], xs, xs[:, :], ALU.add)
        for j in range(4):
            kb.op("act", lambda a, j=j: a.activation(out=c.sqh[:, :], in_=x2[:, j, :], func=AF.Square, accum_out=ssb[:, j:j + 1]),
                  [x2], [c.sqh, ssb])
        kb.dma("pool", scr["SSsrc"][st].ap()[:, :], ssb[:, :], [ssb, c.sqh], [scr["SSsrc_b"][st]])
        kb.coll(scr["SSsrc"][st], scr["SSg"][st], [scr["SSsrc_b"][st]], [scr["SSg_b"][st]])
        if st >= 1:
            finalize(st - 1)
    finalize(nst - 1)


def t5_bucket_np(dist):
    import math
    n = np.maximum(dist, 0)
    me = 16
    large = me + (np.log(np.maximum(n, 1).astype(np.float32) / np.float32(me)) / np.float32(math.log(128 / me))
                  * np.float32(32 - me)).astype(np.int32)
    large = np.minimum(large, 31)
    return np.where(n < me, n, large)


def host_consts():
    k = {}
    ss = np.arange(128)[:, None]
    tt = np.arange(128)[None, :]
    tri = (ss <= tt).astype(np.float32)
    k["M4"] = np.ascontiguousarray(np.broadcast_to(tri[:, None, :], (128, 4, 128))).astype(np.float32)
    rm = np.ones((128, 512), np.float32)
    rm[:, 0::128] = 0.0
    k["rmask"] = rm
    k["identf"] = np.eye(128, dtype=np.float32)
    k["epsb"] = np.full((128, 1), EPS, np.float32)
    t = np.arange(S)[:, None]
    n = np.arange(64)[None, :]
    cur = t // 64
    forced = (n == 0) | (n == cur) | (n == cur - 1)
    visible = n * 64 <= t
    sa = np.where(forced, 1e9, np.where(visible, 0.0, -1e9)).astype(np.float32)
    k["seladd"] = np.ascontiguousarray(sa.reshape(NT, 128, 64).transpose(1, 0, 2))
    ex = np.zeros((64, NT, 128), np.float32)
    for j in range(NT):
        ex[2 * j, j, 0:64] = BIG
        ex[2 * j + 1, j, 64:128] = BIG
    k["expand"] = ex.reshape(64, NT * 128)
    m = np.arange(256)[:, None]
    ov = ((16 * m < 64 * (n + 1)) & (16 * m + 32 > 64 * n) & (m < 255)).astype(np.float32)
    ovx = np.concatenate([ov, np.ones((256, 1), np.float32)], axis=1)
    k["ov"] = np.ascontiguousarray(ovx.reshape(2, 128, 65).transpose(1, 0, 2))
    dist0 = tt - ss
    k["bt_idx"] = [t5_bucket_np(dist0), t5_bucket_np(dist0 + 128), t5_bucket_np(dist0 + 512)]
    k["bt_valid"] = [dist0 >= 0, np.ones_like(dist0, bool), (dist0 + 512) <= 511]
    r = np.arange(504)[:, None] - 240
    distf = tt - 16 * (r - 8) - 31
    k["full_idx"] = t5_bucket_np(distf)
    k["full_valid"] = distf >= 0
    return k


_HC = None


def make_inputs(b, r, x, norm_gains, final_gain, rel_bias, hgrn_lb, hgrn_w_in, hgrn_head_gain, hgrn_w_out,
                nsa_w_in, nsa_pe_k, nsa_pe_v, nsa_phi_k_w1, nsa_phi_k_b1, nsa_phi_k_w2,
                nsa_phi_v_w1, nsa_phi_v_b1, nsa_phi_v_w2, nsa_w_out):
    global _HC
    if _HC is None:
        _HC = host_consts()
    k = _HC
    f = np.float32
    cs = slice(r * HW, (r + 1) * HW)
    m = {}
    m["x"] = np.ascontiguousarray(x[b])
    m["xh"] = np.ascontiguousarray(x[b][:, cs])
    m["g0b"] = np.ascontiguousarray(np.broadcast_to(norm_gains[0][None, :], (128, D))).astype(f)
    m["g1b"] = np.ascontiguousarray(np.broadcast_to(norm_gains[1][None, :], (128, D))).astype(f)
    m["gfb"] = np.ascontiguousarray(np.broadcast_to(np.tile(final_gain[cs], 2)[None, :], (128, D))).astype(f)
    m["lbT"] = np.ascontiguousarray(hgrn_lb.reshape(3, 16, 128)[:, 8 * r:8 * r + 8, :].transpose(2, 0, 1)).astype(f)
    m["G4"] = np.ascontiguousarray(np.broadcast_to(np.tile(hgrn_head_gain[0], 4)[None, :], (128, 512))).astype(f)
    for nm in ["M4", "rmask", "identf", "epsb", "seladd", "expand", "ov"]:
        m[nm] = k[nm]
    m["hgrn_w_in"] = np.ascontiguousarray(hgrn_w_in[0].reshape(D, 4, 16, 128)[:, :, 8 * r:8 * r + 8, :].reshape(D, 4 * HW))
    m["hgrn_w_out"] = np.ascontiguousarray(hgrn_w_out[0][:, cs])
    W = nsa_w_in[0]
    h256 = lambda base: W[:, base + 256 * r:base + 256 * r + 256]
    m["nsa_w_in"] = np.ascontiguousarray(np.concatenate(
        [W[:, cs], h256(2048), h256(2560), h256(3072), h256(4096), h256(3584), h256(4608),
         W[:, 5120 + 24 * r:5120 + 24 * r + 24], W[:, 5168 + HW * r:5168 + HW * r + HW]], axis=1))
    m["nsa_w_out"] = np.ascontiguousarray(nsa_w_out[0][:, cs])
    m["nsa_phi_k_w1"] = np.ascontiguousarray(nsa_phi_k_w1[0])
    m["nsa_phi_v_w1"] = np.ascontiguousarray(nsa_phi_v_w1[0])
    m["nsa_phi_k_w2"] = np.ascontiguousarray(nsa_phi_k_w2[0])
    m["nsa_phi_v_w2"] = np.ascontiguousarray(nsa_phi_v_w2[0])
    m["peT_k"] = np.ascontiguousarray(nsa_pe_k[0].T)
    m["peT_v"] = np.ascontiguousarray(nsa_pe_v[0].T)
    m["b1_k"] = np.ascontiguousarray(nsa_phi_k_b1[0].reshape(128, 1))
    m["b1_v"] = np.ascontiguousarray(nsa_phi_v_b1[0].reshape(128, 1))
    tab = rel_bias.astype(f).reshape(32, 4, 4)[:, 2 * r:2 * r + 2, :]
    BT = np.empty((3, NGC, 128, 4, 128), f)
    for d in range(3):
        gth = tab[k["bt_idx"][d]]
        gth = np.where(k["bt_valid"][d][:, :, None, None], gth, f(NEG))
        BT[d] = gth.transpose(2, 0, 3, 1)
    m["BT"] = np.ascontiguousarray(BT.reshape(3, NGC, 128, 512))
    gth = tab[k["full_idx"]]
    gth = np.where(k["full_valid"][:, :, None, None], gth, f(NEG))
    m["FULL"] = np.ascontiguousarray(gth.transpose(2, 0, 3, 1).reshape(NGC, 504, 512)).astype(f)
    m["cbb"] = np.ascontiguousarray(np.broadcast_to(rel_bias[31][None, 8 * r:8 * r + 8], (128, 8))).astype(f)
    return m


INPUT_SHAPES = {
    "x": [S, D], "xh": [S, HW], "g0b": [128, D], "g1b": [128, D], "gfb": [128, D], "lbT": [128, 3, 4 * NGC], "G4": [128, 512],
    "M4": [128, 4, 128], "rmask": [128, 512], "identf": [128, 128], "epsb": [128, 1],
    "seladd": [128, NT, 64], "expand": [64, NT * 128], "ov": [128, 2, 65],
    "hgrn_w_in": [D, 4 * HW], "hgrn_w_out": [D, HW], "nsa_w_in": [D, 3608], "nsa_w_out": [D, HW],
    "nsa_phi_k_w1": [4096, 128], "nsa_phi_v_w1": [4096, 128], "nsa_phi_k_w2": [128, 128], "nsa_phi_v_w2": [128, 128],
    "peT_k": [128, 32], "peT_v": [128, 32], "b1_k": [128, 1], "b1_v": [128, 1],
    "BT": [3, NGC, 128, 512], "FULL": [NGC, 504, 512], "cbb": [128, 8],
}


def build():
    nc = bass.Bass("TRN2", target_bir_lowering=False)
    es = contextlib.ExitStack()
    kb = KB(nc, es)
    c = Ctx()
    ins = {}
    for name, shape in INPUT_SHAPES.items():
        ins[name] = kb.dram(name, shape, F32, kind="ExternalInput")
    out = kb.dram("out", [S, HW], F32, kind="ExternalOutput")
    scr = {}

    def chunks(name, rows, cols, dtype):
        scr[name + "src"] = [nc.dram_tensor("%ssrc%d" % (name, i), [rows, cols], dtype) for i in range(NST)]
        scr[name + "g"] = [nc.dram_tensor("%sg%d" % (name, i), [2 * rows, cols], dtype) for i in range(NST)]
        scr[name + "src_b"] = [Buf(None, "%ssrcb%d" % (name, i)) for i in range(NST)]
        scr[name + "g_b"] = [Buf(None, "%sgb%d" % (name, i)) for i in range(NST)]

    chunks("OG0", 512, HW, BF16)
    chunks("X1", 512, HW, F32)
    chunks("OG1", 512, HW, BF16)
    chunks("SS", 128, 4, F32)
    scr["X1own"] = kb.dram("X1own", [S, HW], F32)
    scr["QT"] = kb.dram("QT", [4 * NGC, 128, S], BF16)
    for nm in ["KCT", "VCT", "KST", "KWT"]:
        scr[nm] = kb.dram(nm, [NGC, 128, S], BF16)
    scr["VS"] = kb.dram("VS", [S, 128 * NGC], BF16)
    scr["VW"] = kb.dram("VW", [S, 128 * NGC], BF16)
    scr["GATE"] = kb.dram("GATE", [S, 12 * NGC], F32)
    scr["SZ"] = kb.dram("SZ", [S, HW], F32)

    c.ident = kb.tile("ident", [128, 128], BF16)
    c.epsb = kb.tile("epsb", [128, 1], F32)
    identf = kb.tile("identf", [128, 128], F32)
    kb.dma("sp", identf[:, :], ins["identf"][:, :], [], [identf])
    kb.copy("dve", c.ident, c.ident[:, :], identf, identf[:, :])
    kb.dma("sp", c.epsb[:, :], ins["epsb"][:, :], [], [c.epsb])

    def psum_banks(nacc, ntr):
        c.pacc = [kb.psum("pacc%d" % i, [128, 512], F32) for i in range(nacc)]
        c.pacc_i = 0
        c.ptr = [kb.psum("ptr%d" % i, [128, 1024], BF16) for i in range(ntr)]
        c.ptr_i = 0
        c.pm = [kb.psum("pm%d" % i, [128, 512], F32) for i in range(4)]

    dummy = [Buf(None, "d%d" % i) for i in range(NT)]
    y_tiles = [Buf(None, "y%d" % i) for i in range(NT)]
    x_fn = lambda ti: [(ins["x"][ti * 128:(ti + 1) * 128, :], 0, D, dummy[ti])]

    def x1_fn(ti):
        st, j = ti // 4, ti % 4
        return [(scr["X1g"][st].ap()[r * 512 + j * 128:r * 512 + (j + 1) * 128, :], r * HW, HW, scr["X1g_b"][st]) for r in range(2)]

    def proj_tiles(gain_name):
        psum_banks(2, 2)
        setup_common(kb, c)
        kb.dma("sp", c.gb[:, :], ins[gain_name][:, :], [], [c.gb])

    setup_wconv(kb, c, ins)
    issue_conv(kb, c, 10)
    with contextlib.ExitStack() as pes:
        kb.es = pes
        proj_tiles("g0b")
        setup_hgrn(kb, c, ins)
        hgrn_layer(kb, c, ins, scr, dummy, x_fn)
        issue_conv(kb, c, 1000)
        kb.barrier()
    kb.es = es
    with contextlib.ExitStack() as pes:
        kb.es = pes
        proj_tiles("g1b")
        nsa_proj_phase(kb, c, ins, scr, dummy, x1_fn)
        kb.barrier()
    kb.es = es
    with contextlib.ExitStack() as pes:
        kb.es = pes
        psum_banks(3, 1)
        setup_attn(kb, c, ins, scr)
        for g in range(NGC):
            nsa_compress(kb, c, scr, g)
            nsa_attn_group(kb, c, ins, scr, g)
        kb.barrier()
    kb.es = es
    with contextlib.ExitStack() as pes:
        kb.es = pes
        proj_tiles("gfb")
        nsa_out_phase(kb, c, ins, scr, out, y_tiles)
        kb.barrier()
    kb.es = es
    kb.barrier()
    es.close()
    print("n_inst", kb.n_inst)
    return nc


_NC = None


def kernel(**inputs):
    global _NC
    inputs = {k: np.asarray(v) for k, v in inputs.items()}
    if _NC is None:
        _NC = build()
    in_maps = [make_inputs(core // 2, core % 2, **inputs) for core in range(8)]
    res = run_bass_kernel_spmd(_NC, in_maps, core_ids=list(range(8)))
    out = np.empty((4, S, D), np.float32)
    for core in range(8):
        b, r = core // 2, core % 2
        out[b, :, r * HW:(r + 1) * HW] = np.asarray(res.results[core]["out"])
    return out
```
